# Optimizing a Trainium2 kernel written in Bass

```python
import math
import jax, jax.numpy as jnp
from jax import lax
import numpy as np

D_MODEL = 2048
BATCH = 2
SEQ = 4096
DEPTH = 1
DEC_BATCH = 4
DEC_SEQ = 8192
PAST_LEN = 128

GRID_W = 64
D_ATTN = D_MODEL // 2
D_HYENA = D_MODEL - D_ATTN
HEAD_DIM = 128
N_Q_HEADS = D_ATTN // HEAD_DIM
N_KV_HEADS = 2
Q_PER_KV = N_Q_HEADS // N_KV_HEADS
ROPE_HALF = HEAD_DIM // 2
ROPE_THETA = 10000.0
Q_BLOCK = 128
HYENA_ORDER = 2
SHORT_CONV = 3
FILTER_EMB = 33
FILTER_HIDDEN = 64
DECAY_MIN = -math.log(1e-2) / 1.5
DECAY_MAX = -math.log(1e-2) / 0.3
D_FF = 5632
EPS = 1e-6
D_KV = N_KV_HEADS * HEAD_DIM
D_IN_PROJ = D_ATTN + 2 * D_KV + (HYENA_ORDER + 1) * D_HYENA

kernel_name = "hybrid_attn_hyena_macaron_encoder"


def rmsnorm(x, g):
    xf = x.astype(jnp.float32)
    y = xf * lax.rsqrt(jnp.mean(xf * xf, axis=-1, keepdims=True) + EPS)
    return (y * g.astype(jnp.float32)).astype(x.dtype)


def swiglu(h, w13, w2):
    gate, up = jnp.split(h @ w13, 2, axis=-1)
    return (jax.nn.silu(gate) * up) @ w2


def axial_rope_tables(L):
    rows = L // GRID_W
    row = jnp.repeat(jnp.arange(rows, dtype=jnp.float32), GRID_W)
    col = jnp.tile(jnp.arange(GRID_W, dtype=jnp.float32), rows)
    inv = ROPE_THETA ** (-jnp.arange(0, ROPE_HALF, 2, dtype=jnp.float32) / ROPE_HALF)
    ang_r = row[:, None] * inv[None]
    ang_c = col[:, None] * inv[None]
    return jnp.cos(ang_r), jnp.sin(ang_r), jnp.cos(ang_c), jnp.sin(ang_c)


def rotate(x, cos, sin):
    x1, x2 = jnp.split(x, 2, axis=-1)
    c = cos[None, :, None, :]
    s = sin[None, :, None, :]
    return jnp.concatenate([x1 * c - x2 * s, x2 * c + x1 * s], axis=-1)


def apply_axial_rope(x, tabs):
    cr, sr, cc, sc = tabs
    xf = x.astype(jnp.float32)
    out = jnp.concatenate([rotate(xf[..., :ROPE_HALF], cr, sr),
                           rotate(xf[..., ROPE_HALF:], cc, sc)], axis=-1)
    return out.astype(x.dtype)


def block_attention(q, k, v):
    B, L, _, _ = q.shape
    nb = L // Q_BLOCK
    qb = q.reshape(B, nb, Q_BLOCK, N_KV_HEADS, Q_PER_KV, HEAD_DIM).transpose(1, 0, 2, 3, 4, 5)

    def one_block(qi):
        s = jnp.einsum('bqkgd,bskd->bkgqs', qi, k).astype(jnp.float32)
        p = jax.nn.softmax(s, axis=-1).astype(v.dtype)
        return jnp.einsum('bkgqs,bskd->bqkgd', p, v)

    o = lax.map(one_block, qb)
    return o.transpose(1, 0, 2, 3, 4, 5).reshape(B, L, N_Q_HEADS * HEAD_DIM)


def short_conv(x, w, b):
    xp = jnp.pad(x, ((0, 0), (1, 1), (0, 0)))
    return xp[:, :-2] * w[0] + xp[:, 1:-1] * w[1] + xp[:, 2:] * w[2] + b


def hyena_filters(L, w1, b1, w2, b2, w3, freq, decay):
    f32 = jnp.float32
    t01 = jnp.linspace(0.0, 1.0, L, dtype=f32)[:, None]
    bands = (FILTER_EMB - 1) // 2
    fr = jnp.linspace(1e-4, bands - 1, bands, dtype=f32)[None]
    w = 2.0 * math.pi * jnp.arange(L, dtype=f32)[:, None] / L
    feats = jnp.concatenate([t01, jnp.cos(fr * w), -jnp.sin(fr * w)], axis=-1)
    fq = freq.astype(f32)
    h = jnp.sin(fq * (feats @ w1.astype(f32) + b1.astype(f32)))
    h = jnp.sin(fq * (h @ w2.astype(f32) + b2.astype(f32)))
    h = (h @ w3.astype(f32)).reshape(L, 2, HYENA_ORDER, D_HYENA)
    h = h * jnp.exp(-t01[:, :, None, None] * jnp.abs(decay.astype(f32))[None])
    fwd, bwd = h[:, 0], h[:, 1]
    kfull = jnp.concatenate([fwd, jnp.zeros((1, HYENA_ORDER, D_HYENA), f32), bwd[1:][::-1]], axis=0)
    return kfull / jnp.sum(jnp.abs(kfull), axis=0, keepdims=True)


def fftconv(z, kf, d):
    L = z.shape[1]
    zf = z.astype(jnp.float32)
    Z = jnp.fft.rfft(zf, n=2 * L, axis=1)
    K = jnp.fft.rfft(kf, axis=0)
    y = jnp.fft.irfft(Z * K[None], n=2 * L, axis=1)[:, :L]
    return (y + zf * d.astype(jnp.float32)).astype(z.dtype)


def encoder_layer(x, ffn1_norm, ffn1_w13, ffn1_w2, mix_norm, w_in, q_norm, k_norm,
                  conv_w, conv_b, filt_w1, filt_b1, filt_w2, filt_b2, filt_w3, filt_freq,
                  hyena_decay, hyena_bias, group_out_norm, w_out,
                  ffn2_norm, ffn2_w13, ffn2_w2):
    B, L, _ = x.shape
    x = x + 0.5 * swiglu(rmsnorm(x, ffn1_norm), ffn1_w13, ffn1_w2)
    h = rmsnorm(x, mix_norm)
    p = h @ w_in
    q = p[..., :D_ATTN].reshape(B, L, N_Q_HEADS, HEAD_DIM)
    k = p[..., D_ATTN:D_ATTN + D_KV].reshape(B, L, N_KV_HEADS, HEAD_DIM)
    v = p[..., D_ATTN + D_KV:D_ATTN + 2 * D_KV].reshape(B, L, N_KV_HEADS, HEAD_DIM)
    hy = p[..., D_ATTN + 2 * D_KV:]
    tabs = axial_rope_tables(L)
    q = apply_axial_rope(rmsnorm(q, q_norm), tabs) * (HEAD_DIM ** -0.5)
    k = apply_axial_rope(rmsnorm(k, k_norm), tabs)
    attn_out = block_attention(q, k, v)
    hy = short_conv(hy, conv_w, conv_b)
    hv, hx1, hx2 = jnp.split(hy, 3, axis=-1)
    kf = hyena_filters(L, filt_w1, filt_b1, filt_w2, filt_b2, filt_w3, filt_freq, hyena_decay)
    z = hx1 * fftconv(hv, kf[:, 0], hyena_bias[0])
    hyena_out = hx2 * fftconv(z, kf[:, 1], hyena_bias[1])
    merged = jnp.concatenate([rmsnorm(attn_out, group_out_norm[:D_ATTN]),
                              rmsnorm(hyena_out, group_out_norm[D_ATTN:])], axis=-1)
    x = x + merged @ w_out
    x = x + 0.5 * swiglu(rmsnorm(x, ffn2_norm), ffn2_w13, ffn2_w2)
    return x


def setup_inputs(seed: int = 0) -> dict:
    key = jax.random.key(seed)
    ks = jax.random.split(key, 32)
    f32 = jnp.float32

    def nrm(k, shape, scale):
        return jax.random.normal(k, shape, f32) * scale

    def gain(k, shape):
        return 1.0 + 0.02 * jax.random.normal(k, shape, f32)

    rates = jnp.linspace(DECAY_MIN, DECAY_MAX, D_HYENA, dtype=f32)
    return {
        "x_prompt": jax.random.normal(ks[0], (BATCH, SEQ, D_MODEL), f32),
        "x_sample": jax.random.normal(ks[1], (DEC_BATCH, DEC_SEQ, D_MODEL), f32),
        "ffn1_norm": gain(ks[2], (DEPTH, D_MODEL)),
        "ffn1_w13": nrm(ks[3], (DEPTH, D_MODEL, 2 * D_FF), D_MODEL ** -0.5),
        "ffn1_w2": nrm(ks[4], (DEPTH, D_FF, D_MODEL), D_FF ** -0.5),
        "mix_norm": gain(ks[5], (DEPTH, D_MODEL)),
        "w_in": nrm(ks[6], (DEPTH, D_MODEL, D_IN_PROJ), D_MODEL ** -0.5),
        "q_norm": gain(ks[7], (DEPTH, HEAD_DIM)),
        "k_norm": gain(ks[8], (DEPTH, HEAD_DIM)),
        "conv_w": nrm(ks[9], (DEPTH, SHORT_CONV, 3 * D_HYENA), SHORT_CONV ** -0.5),
        "conv_b": nrm(ks[10], (DEPTH, 3 * D_HYENA), 0.02),
        "filt_w1": nrm(ks[11], (DEPTH, FILTER_EMB, FILTER_HIDDEN), FILTER_EMB ** -0.5),
        "filt_b1": nrm(ks[12], (DEPTH, FILTER_HIDDEN), 0.1),
        "filt_w2": nrm(ks[13], (DEPTH, FILTER_HIDDEN, FILTER_HIDDEN), FILTER_HIDDEN ** -0.5),
        "filt_b2": nrm(ks[14], (DEPTH, FILTER_HIDDEN), 0.1),
        "filt_w3": nrm(ks[15], (DEPTH, FILTER_HIDDEN, 2 * HYENA_ORDER * D_HYENA), FILTER_HIDDEN ** -0.5),
        "filt_freq": gain(ks[16], (DEPTH, FILTER_HIDDEN)),
        "hyena_decay": rates[None, None, None, :] * (1.0 + 0.05 * jax.random.normal(ks[17], (DEPTH, 2, HYENA_ORDER, D_HYENA), f32)),
        "hyena_bias": nrm(ks[18], (DEPTH, HYENA_ORDER, D_HYENA), 1.0),
        "group_out_norm": gain(ks[19], (DEPTH, D_MODEL)),
        "w_out": nrm(ks[20], (DEPTH, D_MODEL, D_MODEL), D_MODEL ** -0.5),
        "ffn2_norm": gain(ks[21], (DEPTH, D_MODEL)),
        "ffn2_w13": nrm(ks[22], (DEPTH, D_MODEL, 2 * D_FF), D_MODEL ** -0.5),
        "ffn2_w2": nrm(ks[23], (DEPTH, D_FF, D_MODEL), D_FF ** -0.5),
        "final_norm": gain(ks[24], (D_MODEL,)),
    }


def reference(x_prompt, x_sample, ffn1_norm, ffn1_w13, ffn1_w2, mix_norm, w_in, q_norm, k_norm,
              conv_w, conv_b, filt_w1, filt_b1, filt_w2, filt_b2, filt_w3, filt_freq,
              hyena_decay, hyena_bias, group_out_norm, w_out,
              ffn2_norm, ffn2_w13, ffn2_w2, final_norm):
    def trunk(x):
        for i in range(DEPTH):
            x = encoder_layer(x, ffn1_norm[i], ffn1_w13[i], ffn1_w2[i], mix_norm[i], w_in[i],
                              q_norm[i], k_norm[i], conv_w[i], conv_b[i],
                              filt_w1[i], filt_b1[i], filt_w2[i], filt_b2[i], filt_w3[i], filt_freq[i],
                              hyena_decay[i], hyena_bias[i], group_out_norm[i], w_out[i],
                              ffn2_norm[i], ffn2_w13[i], ffn2_w2[i])
        return rmsnorm(x, final_norm)

    y_prompt = trunk(x_prompt)
    y_sample = trunk(x_sample)
    return (y_prompt, y_sample)
```

```python
import contextlib
import math
import numpy as np
import ml_dtypes
import concourse.bass as bass
import concourse.mybir as mybir
from concourse.bass_utils import run_bass_kernel_spmd

F32 = mybir.dt.float32
BF16 = mybir.dt.bfloat16
AF = mybir.ActivationFunctionType
ALU = mybir.AluOpType
AX = mybir.AxisListType

D = 2048
DFF = 5632
KC = D // 128
FC = DFF // 128
TB = 512
DA = 1024
DH = 1024
NQH = 8
NKV = 2
EPS = 1e-6
CC = 32
GRID_W = 64

ENGS = ("pe", "act", "dve", "pool", "sp")
N_DMA_SEMS = 24


class Prog:
    def __init__(self, nc):
        self.nc = nc
        self.ops = []
        self.last_write = {}
        self.readers = {}
        self.dma_sem_last = [None] * N_DMA_SEMS
        self.dma_sem_count = [0] * N_DMA_SEMS
        self.dma_rr = 0

    def _add(self, eng, fn, reads, writes, is_dma):
        idx = len(self.ops)
        deps = set()
        for k in reads:
            lw = self.last_write.get(k)
            if lw is not None:
                deps.add(lw)
        for k in writes:
            lw = self.last_write.get(k)
            if lw is not None:
                deps.add(lw)
            for r in self.readers.get(k, ()):
                deps.add(r)
        op = dict(idx=idx, eng=eng, fn=fn, is_dma=is_dma, deps=deps, ms=False)
        if is_dma:
            s = self.dma_rr
            self.dma_rr = (self.dma_rr + 1) % N_DMA_SEMS
            prev = self.dma_sem_last[s]
            if prev is not None:
                deps.add(prev)
            self.dma_sem_count[s] += 1
            op["dsem"] = s
            op["dval"] = 16 * self.dma_sem_count[s]
            self.dma_sem_last[s] = idx
        deps.discard(idx)
        self.ops.append(op)
        for k in writes:
            self.last_write[k] = idx
            self.readers[k] = []
        for k in reads:
            if k in writes:
                continue
            lst = self.readers.setdefault(k, [])
            if not is_dma:
                lst[:] = [r for r in lst if self.ops[r]["is_dma"] or self.ops[r]["eng"] != eng]
            lst.append(idx)
        return idx

    rec = None

    def op(self, eng, fn, reads=(), writes=()):
        if self.rec is not None:
            self.rec.append((eng, fn, tuple(reads), tuple(writes), False))
            return None
        return self._add(eng, fn, tuple(reads), tuple(writes), False)

    def dma(self, eng, out, in_, reads=(), writes=(), **kw):
        def fn(e):
            return e.dma_start(out=out, in_=in_, **kw)
        if self.rec is not None:
            self.rec.append((eng, fn, tuple(reads), tuple(writes), True))
            return None
        return self._add(eng, fn, tuple(reads), tuple(writes), True)

    def dma_fn(self, eng, fn, reads=(), writes=()):
        if self.rec is not None:
            self.rec.append((eng, fn, tuple(reads), tuple(writes), True))
            return None
        return self._add(eng, fn, tuple(reads), tuple(writes), True)

    def record(self, f):
        assert self.rec is None
        self.rec = []
        try:
            f()
        finally:
            lst, self.rec = self.rec, None
        return lst

    def replay(self, *lists):
        lists = [l for l in lists if l]
        pos = [0] * len(lists)
        total = sum(len(l) for l in lists)
        for _ in range(total):
            j = min((k for k in range(len(lists)) if pos[k] < len(lists[k])),
                    key=lambda k: (pos[k] + 0.5) / len(lists[k]))
            self._add(*lists[j][pos[j]])
            pos[j] += 1

    def raw(self, eng, fn):
        idx = len(self.ops)
        self.ops.append(dict(idx=idx, eng=eng, fn=fn, is_dma=False, deps=set(), ms=False, raw=True))
        return idx

    def barrier(self):
        last = {}
        for o in self.ops:
            if (not o["is_dma"]) and o["fn"] is not None and not o.get("raw"):
                last[o["eng"]] = o["idx"]
        outstanding = [i for i in self.dma_sem_last if i is not None]
        for e in ENGS:
            idx = len(self.ops)
            deps = set(last.values()) | set(outstanding)
            self.ops.append(dict(idx=idx, eng=e, fn=None, is_dma=False, deps=deps, ms=False))
        self.last_write = {}
        self.readers = {}

    def finalize(self, stack):
        nc = self.nc
        ops = self.ops
        for o in ops:
            for d in o["deps"]:
                t = ops[d]
                if not t["is_dma"]:
                    if t["eng"] == "pe" and o["eng"] == "pe" and not o["is_dma"]:
                        continue
                    t["ms"] = True
        cnt = {e: 0 for e in ENGS}
        for o in ops:
            if o["ms"]:
                cnt[o["eng"]] += 1
                o["msval"] = cnt[o["eng"]]
        known = {e: {f: -1 for f in ENGS} for e in ENGS}
        known_dma = {e: set() for e in ENGS}
        for o in ops:
            e = o["eng"]
            need = {}
            dwaits = []
            for d in sorted(o["deps"]):
                t = ops[d]
                if t["is_dma"]:
                    if d not in known_dma[e]:
                        known_dma[e].add(d)
                        dwaits.append((("d", t["dsem"]), t["dval"]))
                else:
                    f = t["eng"]
                    if f == "pe" and e == "pe" and not o["is_dma"]:
                        continue
                    if t["fn"] is None:
                        continue
                    if d > known[e][f]:
                        need[f] = max(need.get(f, -1), d)
            waits = list(dwaits)
            for f, d in need.items():
                known[e][f] = d
                waits.append((("e", f), ops[d]["msval"]))
            o["waits"] = waits
        esem = {e: stack.enter_context(nc.semaphore("sem_" + e)) for e in ENGS}
        dsem = [stack.enter_context(nc.semaphore("dsem%d" % i)) for i in range(N_DMA_SEMS)]

        def semof(key):
            return esem[key[1]] if key[0] == "e" else dsem[key[1]]

        ccsem = stack.enter_context(nc.semaphore("ccsem"))
        per = {e: [o for o in ops if o["eng"] == e] for e in ENGS}
        stack.enter_context(nc.allow_non_contiguous_dma(reason="small strided tables / halo columns"))
        block = stack.enter_context(nc.Block())

        def run(engobj, lst, ename):
            for o in lst:
                for key, val in o["waits"]:
                    engobj.wait_ge(semof(key), val)
                if o["fn"] is None:
                    continue
                if o.get("raw"):
                    o["fn"](engobj, ccsem)
                    continue
                ins = o["fn"](engobj)
                if o["is_dma"]:
                    ins.then_inc(dsem[o["dsem"]], 16)
                elif o["ms"]:
                    ins.then_inc(esem[ename], 1)

        @block.tensor
        def _(e):
            run(e, per["pe"], "pe")

        @block.scalar
        def _(e):
            run(e, per["act"], "act")

        @block.vector
        def _(e):
            run(e, per["dve"], "dve")

        @block.gpsimd
        def _(e):
            run(e, per["pool"], "pool")

        @block.sync
        def _(e):
            run(e, per["sp"], "sp")

        return {e: len(per[e]) for e in ENGS}


def sb_ap(t, off, dims):
    return bass.AP(t, off, [list(d) for d in dims])


def make_tables(T, L):
    A = T // 128
    K1 = 2 * A
    N = 2 * T
    tb = {}
    tb["ident"] = np.eye(128, dtype=np.float32)
    tb["ones"] = np.ones((128, 128), np.float32)
    half = 32
    inv = (10000.0 ** (-np.arange(0, 64, 2, dtype=np.float32) / 64.0)).astype(np.float32)
    t = np.arange(T)
    row = (t // GRID_W).astype(np.float32)
    col = (t % GRID_W).astype(np.float32)
    ang_r = row[:, None] * inv[None]
    ang_c = col[:, None] * inv[None]
    C = np.zeros((128, T), np.float32)
    S = np.zeros((128, T), np.float32)
    for d in range(128):
        ang = ang_r if d < 64 else ang_c
        j = d % 32
        C[d] = np.cos(ang[:, j])
        first = (d % 64) < 32
        S[d] = (-1.0 if first else 1.0) * np.sin(ang[:, j])
    tb["ropeC"] = C
    tb["ropeS"] = S
    Pm = np.zeros((128, 128), np.float32)
    for d in range(128):
        partner = d + 32 if (d % 64) < 32 else d - 32
        Pm[partner, d] = 1.0
    tb["Pm"] = Pm
    kb = np.zeros((T,), np.float32)
    kb[L:] = -30000.0
    tb["kbias"] = np.ascontiguousarray(kb.reshape(T // 128, 128).T)
    bands = 16
    n = np.arange(N)
    pos = np.where(n < T, n, N - n).astype(np.int64)
    valid = ((n < L) | (n > N - L)).astype(np.float32)
    valid[T] = 0.0
    posc = np.minimum(pos, L - 1)
    t01 = (np.linspace(0.0, 1.0, L, dtype=np.float32))[posc]
    fr = np.linspace(1e-4, bands - 1, bands, dtype=np.float32)[None]
    w = (2.0 * math.pi * np.arange(L, dtype=np.float32)[:, None] / L).astype(np.float32)
    feats_L = np.concatenate([np.linspace(0.0, 1.0, L, dtype=np.float32)[:, None],
                              np.cos(fr * w), -np.sin(fr * w)], axis=-1).astype(np.float32)
    feats = feats_L[posc]
    tb["featsT"] = np.ascontiguousarray(feats.T)
    tb["t01tab"] = np.ascontiguousarray(t01.reshape(K1, 128)).astype(np.float32)
    tb["masktab"] = np.ascontiguousarray(valid.reshape(K1, 128)).astype(np.float32)
    a = np.arange(K1)[:, None]
    k1 = np.arange(K1)[None]
    ph = -2.0 * np.pi * (a * k1 % K1) / K1
    tb["F1K"] = np.concatenate([np.cos(ph), np.sin(ph)], axis=1).astype(np.float32)
    b = np.arange(128)[:, None]
    ph = -2.0 * np.pi * (b * k1) / N
    tb["twr"] = np.cos(ph).astype(np.float32)
    tb["twi"] = np.sin(ph).astype(np.float32)
    tb["twcr"] = np.ascontiguousarray(np.cos(ph).T).astype(np.float32)
    tb["twci"] = np.ascontiguousarray((-np.sin(ph)).T).astype(np.float32)
    k2 = np.arange(128)[None]
    ph = -2.0 * np.pi * ((b * k2) % 128) / 128
    F2r = np.cos(ph)
    F2i = np.sin(ph)
    tb["F2"] = np.concatenate([F2r, F2i, -F2i], axis=1).astype(np.float32)
    tb["R3"] = np.concatenate([F2r, -F2i, F2i, F2r], axis=1).astype(np.float32)
    aa = np.arange(A)[None]
    kk = np.arange(K1)[:, None]
    ph = 2.0 * np.pi * ((aa * kk) % K1) / K1
    tb["G4"] = np.concatenate([np.cos(ph) / N, -np.sin(ph) / N], axis=1).astype(np.float32)
    return tb


TABLE_NAMES = ["ident", "ones", "ropeC", "ropeS", "Pm", "kbias", "featsT", "t01tab", "masktab",
               "F1K", "twr", "twi", "twcr", "twci", "F2", "R3", "G4"]

WEIGHT_SHAPES = {
    "ffn1_norm": (D,), "ffn1_w13": (D, 2 * DFF), "ffn1_w2": (DFF, D), "mix_norm": (D,),
    "w_in": (D, 4608), "q_norm": (128,), "k_norm": (128,), "conv_w": (3, 3072), "conv_b": (3072,),
    "filt_w1": (33, 64), "filt_b1": (64,), "filt_w2": (64, 64), "filt_b2": (64,),
    "filt_w3": (64, 4096), "filt_freq": (64,), "hyena_decay": (2, 2, 1024), "hyena_bias": (2, 1024),
    "group_out_norm": (D,), "w_out": (D, D), "ffn2_norm": (D,), "ffn2_w13": (D, 2 * DFF),
    "ffn2_w2": (DFF, D), "final_norm": (D,),
}


def build_program(slots, debug_outputs=(), phases="0ABCD"):
    if isinstance(slots, int):
        slots = [(slots, 1)]
    nc = bass.Bass("TRN2", target_bir_lowering=False)
    P = Prog(nc)
    dr = {}

    def dram(name, shape, dt, kind):
        if name in debug_outputs and kind == "Internal":
            kind = "ExternalOutput"
        h = nc.dram_tensor(name, list(shape), dt, kind=kind)
        dr[name] = h.ap()
        return dr[name]

    def dbgtap(name, ap, shape, dt, reads):
        if ("dbg_" + name) not in debug_outputs or ("dbg_" + name) in dr:
            return
        h = nc.dram_tensor("dbg_" + name, list(shape), dt, kind="ExternalOutput")
        dr["dbg_" + name] = h.ap()
        P.dma("sp", h.ap(), ap, reads=reads, writes=["dbg_" + name])

    W = {k: dram(k, s if len(s) > 1 else (1, s[0]), F32, "ExternalInput") for k, s in WEIGHT_SHAPES.items()
         if k in COMMON_WEIGHTS}
    TBL = {k: dram(k, shp, F32, "ExternalInput") for k, shp in (("ident", (128, 128)), ("ones", (128, 128)), ("Pm", (128, 128)))}
    w13s = [dram("w13s%d" % i, (22, 128, KC, 2, 256), BF16, "Internal") for i in range(2)]
    w2s = [dram("w2s%d" % i, (8, 128, FC, 256), BF16, "Internal") for i in range(2)]
    wins = dram("wins", (9, 128, KC, 512), BF16, "Internal")
    wouts = dram("wouts", (4, 128, KC, 512), BF16, "Internal")

    with contextlib.ExitStack() as top:
        used_names = {}

        def sb(name, shape, dt, st=top):
            n = used_names.get(name, 0)
            used_names[name] = n + 1
            if n:
                name = "%s_r%d" % (name, n)
            return st.enter_context(nc.sbuf_tensor(name, list(shape), dt))

        ident = sb("ident_sb", (128, 128), F32)
        ones_f = sb("ones_f", (128, 128), F32)
        ones_b = sb("ones_b", (128, 128), BF16)
        gains = sb("gains", (128, 5, KC), F32)
        psb = [top.enter_context(nc.psum_tensor("psb%d" % i, [128, 512], F32)) for i in range(8)]

        P.dma("sp", ident[:], TBL["ident"], writes=["ident"])
        P.dma("sp", ones_f[:], TBL["ones"], writes=["ones_f"])
        P.op("dve", lambda e: e.tensor_copy(out=ones_b[:], in_=ones_f[:]), reads=["ones_f"], writes=["ones_b"])
        for gi, nm in enumerate(["ffn1_norm", "mix_norm", "group_out_norm", "ffn2_norm", "final_norm"]):
            P.dma("sp", gains[:, gi, :], W[nm].rearrange("o (k p) -> p (o k)", p=128), writes=[("gains", gi)],
                  allow_slow_non_contiguous=True)

        cast_now, cast_later = [], []

        def cast_items():
            for fi, (n13, n2) in enumerate([("ffn1_w13", "ffn1_w2"), ("ffn2_w13", "ffn2_w2")]):
                lst = cast_now if fi == 0 else cast_later
                w13 = W[n13]
                for r in range(KC):
                    for g in range(0, 22, 4):
                        npan = min(4, 22 - g)
                        wdt = npan * 256
                        srcs = [(w13[r * 128:(r + 1) * 128, g * 256:g * 256 + wdt], 0, wdt),
                                (w13[r * 128:(r + 1) * 128, DFF + g * 256:DFF + g * 256 + wdt], 1024, wdt)]
                        dsts = []
                        for gu in range(2):
                            dsts.append((w13s[fi][g:g + npan, :, r, gu, :].rearrange("n p c -> p n c"),
                                         gu * 1024, wdt, ("p (n c) -> p n c", dict(c=256))))
                        lst.append((srcs, dsts, 2048))
                w2 = W[n2]
                for r in range(FC):
                    srcs = [(w2[r * 128:(r + 1) * 128, :], 0, 2048)]
                    dsts = [(w2s[fi][:, :, r, :].rearrange("n p c -> p n c"), 0, 2048,
                             ("p (n c) -> p n c", dict(c=256)))]
                    lst.append((srcs, dsts, 2048))
            for r in range(KC):
                for g in range(0, 9, 4):
                    npan = min(4, 9 - g)
                    wdt = npan * 512
                    srcs = [(W["w_in"][r * 128:(r + 1) * 128, g * 512:g * 512 + wdt], 0, wdt)]
                    dsts = [(wins[g:g + npan, :, r, :].rearrange("n p c -> p n c"), 0, wdt,
                             ("p (n c) -> p n c", dict(c=512)))]
                    cast_now.append((srcs, dsts, wdt))
            for r in range(KC):
                srcs = [(W["w_out"][r * 128:(r + 1) * 128, :], 0, 2048)]
                dsts = [(wouts[:, :, r, :].rearrange("n p c -> p n c"), 0, 2048,
                         ("p (n c) -> p n c", dict(c=512)))]
                cast_later.append((srcs, dsts, 2048))
        cast_items()
        cast_cnt = [0]

        def cast_block(item, stg, stb, engs=("dve", "act"), ldq="sp", stq="pool"):
            src_aps, dst_aps, ncols = item
            i = cast_cnt[0] % len(stg)
            cast_cnt[0] += 1
            for ap, c0, n in src_aps:
                P.dma(ldq, stg[i][:, c0:c0 + n], ap, writes=[("stg", i)])
            eng = engs[cast_cnt[0] % len(engs)]
            if eng == "dve":
                P.op("dve", lambda e, i=i: e.tensor_copy(out=stb[i][:, :ncols], in_=stg[i][:, :ncols]),
                     reads=[("stg", i)], writes=[("stb", i)])
            else:
                P.op("act", lambda e, i=i: e.activation(out=stb[i][:, :ncols], in_=stg[i][:, :ncols], func=AF.Copy),
                     reads=[("stg", i)], writes=[("stb", i)])
            for ap, c0, n, pat in dst_aps:
                src = stb[i][:, c0:c0 + n]
                if pat is not None:
                    src = src.rearrange(pat[0], **pat[1])
                P.dma(stq, ap, src, reads=[("stb", i)], writes=["wscratch"])

        defer_casts = ("B" in phases)
        if "0" in phases:
            with contextlib.ExitStack() as st:
                stg = [sb("stg%d" % i, (128, 2048), F32, st) for i in range(4)]
                stb = [sb("stb%d" % i, (128, 2048), BF16, st) for i in range(4)]
                for item in cast_now:
                    cast_block(item, stg, stb, stq="act")
                if not defer_casts:
                    for item in cast_later:
                        cast_block(item, stg, stb, stq="act")
                    cast_later[:] = []
                P.barrier()
        else:
            cast_later[:] = []

        def ffn(xT, hT, actT, sq, rstd, fi, gi, wq, tag):
            norm_fm(xT, hT, sq, rstd, gi, 0, KC, float(D), tag)
            w13buf, w2buf, silu_t = wq
            for m2 in range(22):
                bi = m2 % 2
                P.dma("sp", w13buf[bi][:].rearrange("p k g c -> p (k g c)"),
                      w13s[fi][m2].rearrange("p k g c -> p (k g c)"),
                      reads=["wscratch"], writes=[("w13buf", bi)])
                for hh in range(2):
                    m = 2 * m2 + hh
                    pg, pu = psb[(m % 2) * 2], psb[(m % 2) * 2 + 1]
                    for kc in range(KC):
                        P.op("pe", lambda e, bi=bi, kc=kc, hh=hh, pg=pg: e.matmul(
                            pg[:], w13buf[bi][:, kc, 0, hh * 128:(hh + 1) * 128], hT[:, kc, :],
                            start=(kc == 0), stop=(kc == KC - 1)),
                            reads=[("w13buf", bi), ("hT", kc)], writes=[("ps", (m % 2) * 2)])
                    for kc in range(KC):
                        P.op("pe", lambda e, bi=bi, kc=kc, hh=hh, pu=pu: e.matmul(
                            pu[:], w13buf[bi][:, kc, 1, hh * 128:(hh + 1) * 128], hT[:, kc, :],
                            start=(kc == 0), stop=(kc == KC - 1)),
                            reads=[("w13buf", bi), ("hT", kc)], writes=[("ps", (m % 2) * 2 + 1)])
                    sg = silu_t[m % 2]
                    P.op("act", lambda e, pg=pg, sg=sg: e.activation(out=sg[:], in_=pg[:], func=AF.Silu),
                         reads=[("ps", (m % 2) * 2)], writes=[("silu", m % 2)])
                    P.op("dve", lambda e, pu=pu, sg=sg, m=m: e.tensor_tensor(
                        out=actT[:, m, :], in0=pu[:], in1=sg[:], op=ALU.mult),
                        reads=[("ps", (m % 2) * 2 + 1), ("silu", m % 2)], writes=[("actT", m)])
            for n2 in range(8):
                bi = n2 % 2
                P.dma("sp", w2buf[bi][:].rearrange("p k c -> p (k c)"), w2s[fi][n2].rearrange("p k c -> p (k c)"),
                      reads=["wscratch"], writes=[("w2buf", bi)])
                for hh in range(2):
                    kc_out = 2 * n2 + hh
                    py = psb[4 + (kc_out % 2)]
                    for m in range(FC):
                        P.op("pe", lambda e, bi=bi, m=m, hh=hh, py=py: e.matmul(
                            py[:], w2buf[bi][:, m, hh * 128:(hh + 1) * 128], actT[:, m, :],
                            start=(m == 0), stop=(m == FC - 1)),
                            reads=[("w2buf", bi), ("actT", m)], writes=[("ps", 4 + (kc_out % 2))])
                    P.op("dve", lambda e, py=py, kc_out=kc_out: e.scalar_tensor_tensor(
                        out=xT[:, kc_out, :], in0=py[:], scalar=0.5, in1=xT[:, kc_out, :],
                        op0=ALU.mult, op1=ALU.add),
                        reads=[("ps", 4 + (kc_out % 2)), ("xT", kc_out)], writes=[("xT", kc_out)])

        def norm_fm(src, dst, sq, rstd, gi, goff, nch, nfeat, tag, src_key="xT", dst_key="hT", out_scale=None):
            pss = psb[6]
            for c in range(nch):
                P.op("act", lambda e, c=c: e.activation(out=sq[c % 2][:], in_=src[:, c, :], func=AF.Square),
                     reads=[(src_key, c)], writes=[("sqb", c % 2)])
                P.op("pe", lambda e, c=c: e.matmul(pss[:], ones_b[:], sq[c % 2][:], start=(c == 0), stop=(c == nch - 1)),
                     reads=[("sqb", c % 2), "ones_b"], writes=[("ps", 6)])
            P.op("act", lambda e: e.activation(out=rstd[:], in_=pss[:], func=AF.Sqrt, scale=1.0 / nfeat, bias=eps_t[:]),
                 reads=[("ps", 6), "eps_t"], writes=["rstd"])
            P.op("dve", lambda e: e.reciprocal(out=rstd[:], in_=rstd[:]), reads=["rstd"], writes=["rstd"])
            for c in range(nch):
                P.op("dve", lambda e, c=c: e.scalar_tensor_tensor(
                    out=dst[:, c, :], in0=src[:, c, :], scalar=gains[:, gi, goff + c:goff + c + 1], in1=rstd[:],
                    op0=ALU.mult, op1=ALU.mult),
                    reads=[(src_key, c), "rstd", ("gains", gi)], writes=[(dst_key, c)])

        eps_t = sb("eps_t", (128, 1), F32)
        P.op("dve", lambda e: e.memset(eps_t[:], EPS), writes=["eps_t"])
        eps128_t = sb("eps128_t", (128, 1), F32)
        P.op("dve", lambda e: e.memset(eps128_t[:], EPS * 128.0), writes=["eps128_t"])


        ccdummy = sb("ccdummy", (128, 8), F32)
        cc_count = [0]

        def all_gather(src_ap, dst_ap, groups, defer=False):
            R_, C_ = src_ap.shape
            G_ = len(groups[0])
            rp = piece_rows(R_, C_, G_)
            for j in range(R_ // rp):
                all_gather_1(src_ap[j * rp:(j + 1) * rp, :], dst_ap[j * G_ * rp:(j + 1) * G_ * rp, :], groups, defer)

        def all_gather_1(src_ap, dst_ap, groups, defer):
            cc_count[0] += 1
            n_ = cc_count[0]
            if not defer:
                P.barrier()

            def fn(e, ccsem, src_ap=src_ap, dst_ap=dst_ap, n_=n_, groups=groups, defer=defer):
                e.collective_compute("AllGather", ALU.bypass, replica_groups=groups,
                                     ins=[src_ap.opt()], outs=[dst_ap.opt()]).then_inc(ccsem, 1)
                if not defer:
                    e.wait_ge(ccsem, n_)
            P.raw("pool", fn)
            if not defer:
                P.op("pool", lambda e: e.memset(ccdummy[:], 0.0), writes=["ccdummy"])
                P.barrier()

        def cc_wait_all():
            n_ = cc_count[0]
            P.barrier()
            P.raw("pool", lambda e, ccsem, n_=n_: e.wait_ge(ccsem, n_))
            P.op("pool", lambda e: e.memset(ccdummy[:], 0.0), writes=["ccdummy"])
            P.barrier()

        def emit_slot(si, T, G):
            Tl = T // G
            CH = DH // G
            A = T // 128
            K1 = 2 * A
            NT = T // 128
            NBl = Tl // TB
            N2 = 2 * T
            sfx = "_s%d" % si
            groups = [list(range(i, i + G)) for i in range(0, 8, G)]
            sx_in = dram("x" + sfx, (Tl, D), F32, "ExternalInput")
            sy_out = dram("y" + sfx, (Tl, D), F32, "ExternalOutput")
            STB = {k: dram(k + sfx, shp, F32, "ExternalInput") for k, shp in slot_table_shapes(T, G).items()}
            nsg = 3 * CH // 128
            sW = {
                "cwl": dram("cwl" + sfx, (3, 3 * CH), F32, "ExternalInput"),
                "cbl": dram("cbl" + sfx, (1, 3 * CH), F32, "ExternalInput"),
                "hbl": dram("hbl" + sfx, (2, CH), F32, "ExternalInput"),
                "decl": dram("decl" + sfx, (4, CH), F32, "ExternalInput"),
                "w3l": dram("w3l" + sfx, (64, 4 * CH), F32, "ExternalInput"),
                "cidx": dram("cidx" + sfx, (128, nsg * G), mybir.dt.int32, "ExternalInput"),
                "hoidx": dram("hoidx" + sfx, (128, 8 * NBl), mybir.dt.int32, "ExternalInput"),
            }
            sX1 = dram("X1" + sfx, (D, Tl), F32, "Internal")
            sQs = dram("Qs" + sfx, (NQH * 128, Tl), BF16, "Internal")
            sKsl = dram("Ksl" + sfx, (NKV * 128, Tl), BF16, "Internal")
            sVsl = dram("Vsl" + sfx, (Tl, 256), BF16, "Internal")
            sPsl = dram("Psl" + sfx, (3072, Tl), BF16, "Internal")
            sPc = dram("Pc" + sfx, (3 * CH, T), BF16, "Internal")
            sAOs = dram("AOs" + sfx, (DA, Tl), BF16, "Internal")
            sHOsend = dram("HOsend" + sfx, (CH, T), BF16, "Internal")
            if G > 1:
                sKg = dram("Kg" + sfx, (G * NKV * 128, Tl), BF16, "Internal")
                sVg = dram("Vg" + sfx, (G * Tl, 256), BF16, "Internal")
                sPsg = dram("Psg" + sfx, (G * 3072, Tl), BF16, "Internal")
                sHOg = dram("HOg" + sfx, (G * CH, T), BF16, "Internal")
            else:
                sKg, sVg, sPsg, sHOg = sKsl, sVsl, sPsl, sHOsend
            def phase_A():
                if "A" in phases:
                    with contextlib.ExitStack() as st:
                        xT = sb("xT", (128, KC, TB), F32, st)
                        hT = sb("hT", (128, KC, TB), BF16, st)
                        actT = sb("actT", (128, FC, TB), BF16, st)
                        sq = [sb("sq%d" % i, (128, TB), F32, st) for i in range(2)]
                        sqb = [sb("sqb%d" % i, (128, TB), BF16, st) for i in range(2)]
                        rstd = sb("rstd", (128, TB), F32, st)
                        silu_t = [sb("silu%d" % i, (128, TB), F32, st) for i in range(2)]
                        w13buf = [sb("w13buf%d" % i, (128, KC, 2, 256), BF16, st) for i in range(2)]
                        w2buf = [sb("w2buf%d" % i, (128, FC, 256), BF16, st) for i in range(2)]
                        xtok = [sb("xtok%d" % i, (128, D // 2), F32, st) for i in range(2)]
                        ropeC = sb("ropeC_sb", (128, TB), F32, st)
                        ropeS = sb("ropeS_sb", (128, TB), F32, st)
                        Pm = sb("Pm_sb", (128, 128), F32, st)
                        qkg = sb("qkg", (128, 4), F32, st)
                        qraw, t1, t2, hrs = sq[1], silu_t[0], silu_t[1], rstd
                        qo = [sb("qo%d" % i, (128, TB), BF16, st) for i in range(2)]
                        vo = [sb("vo%d" % i, (128, 256), BF16, st) for i in range(2)]
                        P.dma("sp", Pm[:], TBL["Pm"], writes=["Pm"])
                        for j, nm in enumerate(["q_norm", "k_norm"]):
                            src = W[nm]
                            P.dma("sp", qkg[:, 2 * j:2 * j + 1], src.rearrange("o p -> p o"), writes=[("qkg", 2 * j)],
                                  allow_slow_non_contiguous=True)
                            for blk in range(2):
                                for hf in range(2):
                                    d0 = blk * 64 + hf * 32
                                    s0 = blk * 64 + (1 - hf) * 32
                                    P.dma("sp", qkg[d0:d0 + 32, 2 * j + 1:2 * j + 2],
                                          src[:, s0:s0 + 32].rearrange("o p -> p o"), writes=[("qkg", 2 * j + 1, d0)],
                                          allow_slow_non_contiguous=True)
                        qkg_keys = [("qkg", 0), ("qkg", 2)] + [("qkg", 2 * j + 1, d0) for j in range(2) for d0 in (0, 32, 64, 96)]

                        for blk in range(NBl):
                            t0 = blk * TB
                            for tt8 in range(8):
                                tt, hx = tt8 // 2, tt8 % 2
                                bi = tt8 % 2
                                P.dma("sp", xtok[bi][:], sx_in[t0 + tt * 128:t0 + (tt + 1) * 128, hx * 1024:(hx + 1) * 1024],
                                      writes=[("xtok", bi)])
                                for kc in range(hx * 8, hx * 8 + 8):
                                    pst = psb[kc % 4]
                                    P.op("pe", lambda e, bi=bi, kc=kc, pst=pst: e.transpose(
                                        pst[:, 0:128], xtok[bi][:, (kc % 8) * 128:(kc % 8 + 1) * 128], ident[:]),
                                        reads=[("xtok", bi), "ident"], writes=[("ps", kc % 4)])
                                    if kc % 2 == 0:
                                        P.op("act", lambda e, kc=kc, tt=tt, pst=pst: e.activation(
                                            out=xT[:, kc, tt * 128:(tt + 1) * 128], in_=pst[:, 0:128], func=AF.Copy),
                                            reads=[("ps", kc % 4)], writes=[("xT", kc)])
                                    else:
                                        P.op("dve", lambda e, kc=kc, tt=tt, pst=pst: e.tensor_copy(
                                            out=xT[:, kc, tt * 128:(tt + 1) * 128], in_=pst[:, 0:128]),
                                            reads=[("ps", kc % 4)], writes=[("xT", kc)])
                            ffn(xT, hT, actT, sqb, rstd, 0, 0, (w13buf, w2buf, silu_t), "f1")
                            P.dma("pool", sX1[:, t0:t0 + TB].rearrange("(k p) t -> p k t", p=128), xT[:],
                                  reads=[("xT", kc) for kc in range(KC)], writes=["X1"])
                            norm_fm(xT, hT, sqb, rstd, 1, 0, KC, float(D), "mix")
                            P.dma("sp", ropeC[:], STB["ropeC"][:, t0:t0 + TB], writes=["ropeC"])
                            P.dma("sp", ropeS[:], STB["ropeS"][:, t0:t0 + TB], writes=["ropeS"])
                            for pn in range(9):
                                bi = pn % 2
                                wb = w13buf[bi][:].rearrange("p k g c -> p k (g c)")
                                P.dma("sp", w13buf[bi][:].rearrange("p k g c -> p (k g c)"),
                                      wins[pn].rearrange("p k c -> p (k c)"),
                                      reads=["wscratch"], writes=[("w13buf", bi)])
                                if pn == 2:
                                    for tt in range(4):
                                        pv = psb[4 + tt % 2]
                                        for kc in range(KC):
                                            P.op("pe", lambda e, kc=kc, tt=tt, pv=pv, wb=wb: e.matmul(
                                                pv[:, 0:256], hT[:, kc, tt * 128:(tt + 1) * 128], wb[:, kc, 256:512],
                                                start=(kc == 0), stop=(kc == KC - 1)),
                                                reads=[("w13buf", bi), ("hT", kc)], writes=[("ps", 4 + tt % 2)])
                                        P.op("act", lambda e, tt=tt, pv=pv: e.activation(out=vo[tt % 2][:], in_=pv[:, 0:256], func=AF.Copy),
                                             reads=[("ps", 4 + tt % 2)], writes=[("vo", tt % 2)])
                                        P.dma("pool", sVsl[t0 + tt * 128:t0 + (tt + 1) * 128, :], vo[tt % 2][:],
                                              reads=[("vo", tt % 2)], writes=["Vs"])
                                nsub = 2 if pn == 2 else 4
                                for sub in range(nsub):
                                    pq = psb[sub % 2]
                                    for kc in range(KC):
                                        P.op("pe", lambda e, kc=kc, sub=sub, pq=pq, wb=wb: e.matmul(
                                            pq[:], wb[:, kc, sub * 128:(sub + 1) * 128], hT[:, kc, :],
                                            start=(kc == 0), stop=(kc == KC - 1)),
                                            reads=[("w13buf", bi), ("hT", kc)], writes=[("ps", sub % 2)])
                                    if pn <= 2:
                                        isq = pn < 2
                                        head = pn * 4 + sub if isq else sub
                                        gcol = 0 if isq else 2
                                        P.op("act", lambda e, pq=pq: e.activation(out=qraw[:], in_=pq[:], func=AF.Copy),
                                             reads=[("ps", sub % 2)], writes=[("sq", 1)])
                                        P.op("act", lambda e, pq=pq: e.activation(out=sq[0][:], in_=pq[:], func=AF.Square),
                                             reads=[("ps", sub % 2)], writes=[("sq", 0)])
                                        P.op("pe", lambda e: e.matmul(psb[2][:], ones_f[:], sq[0][:], start=True, stop=True),
                                             reads=[("sq", 0), "ones_f"], writes=[("ps", 2)])
                                        P.op("pe", lambda e: e.matmul(psb[3][:], Pm[:], qraw[:], start=True, stop=True),
                                             reads=[("sq", 1), "Pm"], writes=[("ps", 3)])
                                        if isq:
                                            P.op("act", lambda e: e.activation(out=hrs[:], in_=psb[2][:], func=AF.Sqrt,
                                                                               scale=1.0, bias=eps128_t[:]),
                                                 reads=[("ps", 2), "eps128_t"], writes=["rstd"])
                                        else:
                                            P.op("act", lambda e: e.activation(out=hrs[:], in_=psb[2][:], func=AF.Sqrt,
                                                                               scale=1.0 / 128.0, bias=eps_t[:]),
                                                 reads=[("ps", 2), "eps_t"], writes=["rstd"])
                                        P.op("dve", lambda e: e.reciprocal(out=hrs[:], in_=hrs[:]), reads=["rstd"], writes=["rstd"])
                                        P.op("dve", lambda e, gcol=gcol: e.scalar_tensor_tensor(
                                            out=t1[:], in0=qraw[:], scalar=qkg[:, gcol:gcol + 1], in1=ropeC[:],
                                            op0=ALU.mult, op1=ALU.mult), reads=[("sq", 1), "ropeC"] + qkg_keys, writes=[("silu", 0)])
                                        P.op("dve", lambda e, gcol=gcol: e.scalar_tensor_tensor(
                                            out=t2[:], in0=psb[3][:], scalar=qkg[:, gcol + 1:gcol + 2], in1=ropeS[:],
                                            op0=ALU.mult, op1=ALU.mult), reads=[("ps", 3), "ropeS"] + qkg_keys, writes=[("silu", 1)])
                                        P.op("pool", lambda e: e.tensor_tensor(out=t1[:], in0=t1[:], in1=t2[:], op=ALU.add),
                                             reads=[("silu", 0), ("silu", 1)], writes=[("silu", 0)])
                                        oi = head % 2
                                        P.op("dve", lambda e, oi=oi: e.tensor_tensor(out=qo[oi][:], in0=t1[:], in1=hrs[:], op=ALU.mult),
                                             reads=[("silu", 0), "rstd"], writes=[("qo", oi)])
                                        dst = (sQs if isq else sKsl)[head * 128:(head + 1) * 128, t0:t0 + TB]
                                        P.dma("pool", dst, qo[oi][:], reads=[("qo", oi)], writes=["Qs" if isq else "Ks"])
                                    else:
                                        ch = (pn - 3) * 4 + sub
                                        oi = ch % 2
                                        if ch % 2 == 0:
                                            P.op("act", lambda e, pq=pq, oi=oi: e.activation(out=qo[oi][:], in_=pq[:], func=AF.Copy),
                                                 reads=[("ps", sub % 2)], writes=[("qo", oi)])
                                        else:
                                            P.op("dve", lambda e, pq=pq, oi=oi: e.tensor_copy(out=qo[oi][:], in_=pq[:]),
                                                 reads=[("ps", sub % 2)], writes=[("qo", oi)])
                                        P.dma("pool", sPsl[ch * 128:(ch + 1) * 128, t0:t0 + TB], qo[oi][:],
                                              reads=[("qo", oi)], writes=["Ps"])
                        P.barrier()


            phase_A()
            if G > 1:
                all_gather(sKsl, sKg, groups)
                all_gather(sVsl, sVg, groups)
                all_gather(sPsl, sPsg, groups, defer=True)
            def phase_B():
                if "B" in phases:
                    with contextlib.ExitStack() as st:
                        KT = sb("KT", (128, T), BF16, st)
                        Vg = sb("Vg", (128, NT, 128), BF16, st)
                        kbias = sb("kbias_sb", (128, NT), F32, st)
                        QT = [sb("QT%d" % i, (128, TB), BF16, st) for i in range(2)]
                        PT = [sb("PT%d" % i, (128, TB), BF16, st) for i in range(3)]
                        rec = sb("rec", (128, TB), F32, st)
                        ao = [sb("ao%d" % i, (128, TB), BF16, st) for i in range(2)]
                        if cast_later:
                            stg_b = [sb("stg%d" % i, (128, 2048), F32, st) for i in range(4)]
                            stb_b = [sb("stb%d" % i, (128, 2048), BF16, st) for i in range(4)]
                            n_iter = NKV * 4 * NBl
                            per_iter = -(-len(cast_later) // n_iter)
                        P.dma("sp", kbias[:], STB["kbias"], writes=["kbias"])
                        it = 0
                        for g in range(NKV):
                            for r_ in range(G):
                                P.dma("sp", KT[:, r_ * Tl:(r_ + 1) * Tl], sKg[r_ * 256 + g * 128:r_ * 256 + (g + 1) * 128, :],
                                      reads=["Ks"], writes=["KT"])
                            vsrc = sVg[:, g * 128:(g + 1) * 128].rearrange("(n p) d -> p n d", p=128)
                            nvs = max(1, NT // 16)
                            for vq in range(nvs):
                                n0, n1 = vq * (NT // nvs), (vq + 1) * (NT // nvs)
                                P.dma("sp", Vg[:, n0:n1, :], vsrc[:, n0:n1, :], reads=["Vs"], writes=["Vg"])
                            for h in range(4):
                                hq = g * 4 + h
                                for qc in range(NBl):
                                    qi = it % 2
                                    it += 1
                                    P.dma("sp", QT[qi][:], sQs[hq * 128:(hq + 1) * 128, qc * TB:(qc + 1) * TB],
                                          reads=["Qs"], writes=[("QT", qi)])
                                    bo, bs = 3 + 2 * qi, 4 + 2 * qi

                                    def S(kt, qi=qi):
                                        P.op("pe", lambda e: e.matmul(psb[kt % 3][:], KT[:, kt * 128:(kt + 1) * 128], QT[qi][:],
                                                                      start=True, stop=True),
                                             reads=["KT", ("QT", qi)], writes=[("ps", kt % 3)])

                                    def E(kt):
                                        P.op("act", lambda e: e.activation(out=PT[kt % 3][:], in_=psb[kt % 3][:], func=AF.Exp,
                                                                           bias=kbias[:, kt:kt + 1], scale=1.0),
                                             reads=[("ps", kt % 3), "kbias"], writes=[("PT", kt % 3)])

                                    def PV(kt, bo=bo, bs=bs):
                                        P.op("pe", lambda e: e.matmul(psb[bo][:], Vg[:, kt, :], PT[kt % 3][:],
                                                                      start=(kt == 0), stop=(kt == NT - 1)),
                                             reads=["Vg", ("PT", kt % 3)], writes=[("ps", bo)])
                                        P.op("pe", lambda e: e.matmul(psb[bs][:], ones_b[:], PT[kt % 3][:],
                                                                      start=(kt == 0), stop=(kt == NT - 1)),
                                             reads=["ones_b", ("PT", kt % 3)], writes=[("ps", bs)])

                                    S(0)
                                    if NT > 1:
                                        S(1)
                                    for kt in range(NT):
                                        E(kt)
                                        if kt + 2 < NT:
                                            S(kt + 2)
                                        PV(kt)
                                    P.op("dve", lambda e, bs=bs: e.reciprocal(out=rec[:], in_=psb[bs][:]),
                                         reads=[("ps", bs)], writes=["rec"])
                                    P.op("dve", lambda e, bo=bo, qi=qi: e.tensor_tensor(out=ao[qi][:], in0=psb[bo][:], in1=rec[:], op=ALU.mult),
                                         reads=[("ps", bo), "rec"], writes=[("ao", qi)])
                                    P.dma("pool", sAOs[hq * 128:(hq + 1) * 128, qc * TB:(qc + 1) * TB], ao[qi][:],
                                          reads=[("ao", qi)], writes=["AOs"])
                                    if cast_later:
                                        for _ in range(min(per_iter, len(cast_later))):
                                            cast_block(cast_later.pop(0), stg_b, stb_b, engs=("dve",), ldq="pool")
                        P.barrier()


            phase_B()
            if G > 1:
                cc_wait_all()

            def phase_C():
                if "C" in phases:
                    with contextlib.ExitStack() as st:
                        CCc = 16
                        NCH = CH // CCc
                        nch1 = min(CCc, 512 // (2 * K1))
                        g4 = min(CCc, 512 // K1)
                        MAGIC = 12582912.0
                        TWO_PI = 2.0 * math.pi

                        def fsz(t):
                            n = 1
                            for d_ in t.shape[1:]:
                                n *= d_
                            return n

                        def vw(t, off, dims, p0=0, np_=128):
                            return bass.AP(t, p0 * fsz(t) + off, [[fsz(t), np_]] + [list(d_) for d_ in dims])

                        ld = sb("ld_tmp", (128, 512), F32, st)
                        F1Kb = sb("F1Kb", (128, 2 * K1), BF16, st)
                        F2b = sb("F2b", (128, 384), BF16, st)
                        R3b = sb("R3b", (128, 512), BF16, st)
                        G4b = sb("G4b", (128, 2 * A), BF16, st)
                        TW = sb("TW", (128, 2, 2 * K1), F32, st)
                        TWC = sb("TWC", (128, 2, 256), F32, st)
                        t01 = sb("t01", (128, 128), F32, st)
                        msk = sb("msk", (128, 128), F32, st)

                        def load_cast(dst, name, rows, cols):
                            P.dma("sp", ld[0:rows, 0:cols], STB[name], writes=["ld"])
                            P.op("dve", lambda e: e.tensor_copy(out=dst[0:rows, 0:cols], in_=ld[0:rows, 0:cols]),
                                 reads=["ld"], writes=[name])
                        load_cast(F1Kb, "F1K", K1, 2 * K1)
                        load_cast(F2b, "F2", 128, 384)
                        load_cast(R3b, "R3", 128, 512)
                        load_cast(G4b, "G4", K1, 2 * A)
                        for j, nm in enumerate(["twr", "twi"]):
                            for rep in range(2):
                                P.dma("sp", TW[:, j, rep * K1:(rep + 1) * K1], STB[nm], writes=["TW"])
                        for j, nm in enumerate(["twcr", "twci"]):
                            for rep in range(2):
                                P.dma("sp", TWC[0:K1, j, rep * 128:(rep + 1) * 128], STB[nm], writes=["TWC"])
                        P.dma("sp", t01[0:K1, :], STB["t01tab"], writes=["t01"])
                        P.dma("sp", msk[0:K1, :], STB["masktab"], writes=["msk"])

                        h2s = sb("h2s", (128, 128, K1), BF16, st)
                        w3s = sb("w3s", (128, 2, CH), BF16, st)
                        w1 = sb("w1", (33, 64), F32, st)
                        w2d = sb("w2d", (64, 128), F32, st)
                        fvec = sb("fvec", (128, 4), F32, st)
                        P.op("pool", lambda e: e.memset(h2s[:], 0.0), writes=["h2s"])
                        P.dma("sp", w1[:], W["filt_w1"], writes=["w1"])
                        for rep in range(2):
                            P.dma("sp", w2d[:, rep * 64:(rep + 1) * 64], W["filt_w2"], writes=["w2d"])
                            P.dma("sp", fvec[rep * 64:(rep + 1) * 64, 2:3], W["filt_b2"].rearrange("o p -> p o"), writes=["fvec"])
                            P.dma("sp", fvec[rep * 64:(rep + 1) * 64, 3:4], W["filt_freq"].rearrange("o p -> p o"), writes=["fvec"])
                        P.dma("sp", fvec[0:64, 0:1], W["filt_b1"].rearrange("o p -> p o"), writes=["fvec"])
                        P.dma("sp", fvec[0:64, 1:2], W["filt_freq"].rearrange("o p -> p o"), writes=["fvec"])
                        w3v = sW["w3l"].rearrange("j (d o c) -> j d o c", d=2, o=2)
                        for o_ in range(2):
                            for d_ in range(2):
                                P.dma("sp", ld[d_ * 64:(d_ + 1) * 64, 0:CH], w3v[:, d_, o_, :], writes=["ld"])
                            P.op("dve", lambda e, o_=o_: e.tensor_copy(out=w3s[:, o_, :], in_=ld[:, 0:CH]),
                                 reads=["ld"], writes=["w3s"])
                        with contextlib.ExitStack() as st2:
                            fb = [sb("fb%d" % i, (33, 512), F32, st2) for i in range(2)]
                            u = sb("u_mlp", (128, 512), F32, st2)
                            kk = sb("kk_mlp", (128, 512), F32, st2)
                            h1 = sb("h1_mlp", (64, 512), F32, st2)
                            mkblk = [sb("mkblk%d" % i, (128, 512), F32, st2) for i in range(2)]

                            def sin_layer(ps, np_, bcol, fcol, out_ap_fn):
                                P.op("dve", lambda e: e.tensor_scalar(out=u[0:np_, :], in0=ps[0:np_, :], scalar1=fvec[0:np_, bcol:bcol + 1],
                                                                      scalar2=fvec[0:np_, fcol:fcol + 1], op0=ALU.add, op1=ALU.mult),
                                     reads=[("ps", 7), "fvec"], writes=["u"])
                                P.op("dve", lambda e: e.tensor_scalar(out=kk[0:np_, :], in0=u[0:np_, :], scalar1=1.0 / TWO_PI, scalar2=MAGIC,
                                                                      op0=ALU.mult, op1=ALU.add), reads=["u"], writes=["kk"])
                                P.op("dve", lambda e: e.tensor_scalar(out=kk[0:np_, :], in0=kk[0:np_, :], scalar1=-MAGIC, scalar2=-TWO_PI,
                                                                      op0=ALU.add, op1=ALU.mult), reads=["kk"], writes=["kk"])
                                P.op("dve", lambda e: e.tensor_tensor(out=u[0:np_, :], in0=u[0:np_, :], in1=kk[0:np_, :], op=ALU.add),
                                     reads=["u", "kk"], writes=["u"])
                                out_ap_fn()

                            for j in range(N2 // 512):
                                bi = j % 2
                                P.dma("sp", fb[bi][:], STB["featsT"][:, j * 512:(j + 1) * 512], writes=[("fb", bi)])
                                P.op("pe", lambda e, bi=bi: e.matmul(psb[7][0:64, :], w1[:], fb[bi][:], start=True, stop=True),
                                     reads=["w1", ("fb", bi)], writes=[("ps", 7)])

                                def o1():
                                    P.op("act", lambda e: e.activation(out=h1[:], in_=u[0:64, :], func=AF.Sin), reads=["u"], writes=["h1"])
                                sin_layer(psb[7], 64, 0, 1, o1)
                                P.op("pe", lambda e: e.matmul(psb[7][:], w2d[:], h1[:], start=True, stop=True),
                                     reads=["w2d", "h1"], writes=[("ps", 7)])
                                fwd = (4 * j) < A
                                r0 = 0 if fwd else 64

                                P.dma("sp", mkblk[bi][:], bass.AP(STB["masktab"].tensor, j * 512, [[0, 128], [1, 512]]),
                                      writes=[("mkblk", bi)])

                                def o2(j=j, r0=r0, bi=bi):
                                    dst = vw(h2s, 4 * j, [[K1, 128], [1, 4]], p0=r0, np_=64)
                                    src = kk[r0:r0 + 64, :].rearrange("p (a b) -> p b a", a=4)
                                    mk_ = mkblk[bi][r0:r0 + 64, :].rearrange("p (a b) -> p b a", a=4)
                                    P.op("act", lambda e: e.activation(out=kk[r0:r0 + 64, :], in_=u[r0:r0 + 64, :], func=AF.Sin),
                                         reads=["u", "kk"], writes=["kk"])
                                    P.op("dve", lambda e: e.tensor_tensor(out=dst, in0=src, in1=mk_, op=ALU.mult),
                                         reads=["kk", ("mkblk", bi)], writes=["h2s"])
                                sin_layer(psb[7], 128, 2, 3, o2)
                            P.barrier()

                        with contextlib.ExitStack() as st3:
                            TQ = min(2048, T)
                            rawrow = [sb("rawrow%d" % i, (128, T + 2), BF16, st3) for i in range(2)]
                            cacc = [sb("cacc%d" % i, (128, TQ), F32, st3) for i in range(2)]
                            coutb = [sb("coutb%d" % i, (128, TQ), BF16, st3) for i in range(2)]
                            cwT = sb("cwT", (128, 3, nsg), F32, st3)
                            cbT = sb("cbT", (128, nsg), F32, st3)
                            cidx = sb("cidx_sb", (128, nsg * G), mybir.dt.int32, st3)
                            P.dma("sp", cidx[:], sW["cidx"], writes=["cidx"])
                            for i_ in range(2):
                                P.op("dve", lambda e, i_=i_: e.memset(rawrow[i_][:, 0:1], 0.0), writes=[("rawrow", i_)])
                                P.op("dve", lambda e, i_=i_: e.memset(rawrow[i_][:, T + 1:T + 2], 0.0), writes=[("rawrow", i_)])
                            for tp_ in range(3):
                                P.dma("sp", cwT[:, tp_, :], bass.AP(sW["cwl"].tensor, tp_ * 3 * CH, [[1, 128], [128, nsg]]), writes=["cwT"])
                            P.dma("sp", cbT[:], bass.AP(sW["cbl"].tensor, 0, [[1, 128], [128, nsg]]), writes=["cbT"])
                            it_ = 0
                            for sg in range(nsg):
                                rb = sg % 2
                                for r_ in range(G):
                                    P.dma_fn("pool", lambda e, rb=rb, r_=r_, sg=sg: e.indirect_dma_start(
                                        out=rawrow[rb][:, 1 + r_ * Tl:1 + (r_ + 1) * Tl], out_offset=None, in_=sPsg,
                                        in_offset=bass.IndirectOffsetOnAxis(ap=cidx[:, sg * G + r_:sg * G + r_ + 1], axis=0)),
                                        reads=["Ps", "cidx"], writes=[("rawrow", rb)])
                                for q in range(T // TQ):
                                    ai = it_ % 2
                                    it_ += 1
                                    q0 = q * TQ
                                    P.op("act", lambda e, rb=rb, ai=ai, q0=q0, sg=sg: e.activation(
                                        out=cacc[ai][:], in_=rawrow[rb][:, q0:q0 + TQ], func=AF.Identity,
                                        scale=cwT[:, 0, sg:sg + 1], bias=cbT[:, sg:sg + 1]),
                                        reads=[("rawrow", rb), "cwT", "cbT"], writes=[("cacc", ai)])
                                    P.op("dve", lambda e, rb=rb, ai=ai, q0=q0, sg=sg: e.scalar_tensor_tensor(
                                        out=cacc[ai][:], in0=rawrow[rb][:, q0 + 1:q0 + 1 + TQ], scalar=cwT[:, 1, sg:sg + 1], in1=cacc[ai][:],
                                        op0=ALU.mult, op1=ALU.add),
                                        reads=[("rawrow", rb), "cwT", ("cacc", ai)], writes=[("cacc", ai)])
                                    P.op("dve", lambda e, rb=rb, ai=ai, q0=q0, sg=sg: e.scalar_tensor_tensor(
                                        out=coutb[ai][:], in0=rawrow[rb][:, q0 + 2:q0 + 2 + TQ], scalar=cwT[:, 2, sg:sg + 1], in1=cacc[ai][:],
                                        op0=ALU.mult, op1=ALU.add),
                                        reads=[("rawrow", rb), "cwT", ("cacc", ai)], writes=[("coutb", ai)])
                                    P.dma("sp", sPc[sg * 128:(sg + 1) * 128, q0:q0 + TQ], coutb[ai][:],
                                          reads=[("coutb", ai)], writes=["Pc"])
                            P.barrier()
                        strm = [[sb("strm%d_%d" % (s_, i), (128, CCc, 128), BF16, st) for i in range(2)] for s_ in range(3)]
                        hbs = sb("hbs", (128, 2, CCc), F32, st)
                        adec = sb("adec", (128, 2, CCc), F32, st)
                        zf_ = sb("zf", (128, CCc, 128), F32, st)
                        dsk = sb("dsk", (128, CCc, 128), F32, st)
                        ctmp = sb("ctmp", (128, 4, 128), F32, st)
                        inb = sb("inb", (128, CCc, 128), BF16, st)
                        kf = sb("kf", (128, CCc, 128), F32, st)
                        Ee = sb("Ee", (128, CCc, 128), F32, st)
                        kfb = sb("kfb", (128, CCc, 128), BF16, st)
                        ksum = sb("ksum", (128, CCc), F32, st)
                        ksumb = sb("ksumb", (128, CCc), BF16, st)
                        rn2 = [sb("rn%d" % i, (128, CCc), F32, st) for i in range(2)]
                        Yp_d = sb("Yp", (128, CCc, 2, K1), BF16, st)
                        tm1_d = [sb("tm1_%d" % i, (128, 512), F32, st) for i in range(2)]
                        tm2_d = [sb("tm2_%d" % i, (128, 512), F32, st) for i in range(2)]
                        tm1, tm2 = tm1_d, tm2_d
                        Ksp2 = [sb("Ksp%d" % i, (128, CCc, 2, K1), BF16, st) for i in range(2)]
                        YpF = sb("YpF", (128, CCc, 2, K1), BF16, st)
                        tmF1 = [sb("tmF1_%d" % i, (128, 512), F32, st) for i in range(2)]
                        tmF2 = [sb("tmF2_%d" % i, (128, 512), F32, st) for i in range(2)]
                        Zs = [sb("Zs%d" % i, (128, 2 * 512), BF16, st) for i in range(2)]
                        ZsS = [sb("ZsS%d" % i, (128, 2 * 512), BF16, st) for i in range(2)]
                        Zf = sb("Zf", (128, CCc, 2, K1), BF16, st)
                        Up = sb("Up", (128, CCc, 2, 128), BF16, st)
                        oob = sb("oob", (128, CCc, 128), BF16, st)

                        def fwd_dft(src_b, krows, is_filter, tag, par):
                            banks1 = (0, 7) if is_filter else (1, 6)
                            tm1, tm2 = (tmF1, tmF2) if is_filter else (tm1_d, tm2_d)
                            Yp = YpF if is_filter else Yp_d
                            ypn = "YpF" if is_filter else "Yp"
                            t1n, t2n = ("tmF1", "tmF2") if is_filter else ("tm1", "tm2")
                            Ksp = Ksp2[par]
                            rn = rn2[par]
                            for q in range(CCc // nch1):
                                bk = banks1[q % 2]
                                ti = q % 2
                                for cl in range(nch1):
                                    c = q * nch1 + cl
                                    P.op("pe", lambda e, c=c, cl=cl, bk=bk: e.matmul(
                                        psb[bk][:, cl * 2 * K1:(cl + 1) * 2 * K1], src_b[0:krows, c, :], F1Kb[0:krows, :],
                                        start=True, stop=True), reads=[tag, "F1K"], writes=[("ps", bk)])
                                n_el = nch1 * 2 * K1
                                pin = psb[bk][:, 0:n_el].rearrange("p (c x) -> p c x", c=nch1)
                                twr2 = vw(TW, 0, [[0, nch1], [1, 2 * K1]])
                                twi2 = vw(TW, 2 * K1, [[0, nch1], [1, 2 * K1]])
                                o1_ = tm1[ti][:, 0:n_el].rearrange("p (c x) -> p c x", c=nch1)
                                o2_ = tm2[ti][:, 0:n_el].rearrange("p (c x) -> p c x", c=nch1)
                                P.op("dve", lambda e, pin=pin, twr2=twr2, o1_=o1_: e.tensor_tensor(out=o1_, in0=pin, in1=twr2, op=ALU.mult),
                                     reads=[("ps", bk), "TW"], writes=[(t1n, ti)])
                                P.op("dve", lambda e, pin=pin, twi2=twi2, o2_=o2_: e.tensor_tensor(out=o2_, in0=pin, in1=twi2, op=ALU.mult),
                                     reads=[("ps", bk), "TW"], writes=[(t2n, ti)])
                                a1 = tm1[ti][:, 0:n_el].rearrange("p (c r k) -> p c r k", c=nch1, r=2)
                                a2 = tm2[ti][:, 0:n_el].rearrange("p (c r k) -> p c r k", c=nch1, r=2)
                                c0_ = q * nch1
                                P.op("pool", lambda e, a1=a1, a2=a2, c0_=c0_: e.tensor_tensor(
                                    out=Yp[:, c0_:c0_ + nch1, 0, :], in0=a1[:, :, 0, :], in1=a2[:, :, 1, :], op=ALU.subtract),
                                    reads=[(t1n, ti), (t2n, ti)], writes=[(ypn, q)])
                                P.op("pool", lambda e, a1=a1, a2=a2, c0_=c0_: e.tensor_tensor(
                                    out=Yp[:, c0_:c0_ + nch1, 1, :], in0=a2[:, :, 0, :], in1=a1[:, :, 1, :], op=ALU.add),
                                    reads=[(t1n, ti), (t2n, ti)], writes=[(ypn, q)])
                            ypk = [(ypn, q) for q in range(CCc // nch1)]
                            for gi_ in range(CCc // g4):
                                cs = gi_ * g4
                                yr = Yp[:, cs:cs + g4, 0, :]
                                yi = Yp[:, cs:cs + g4, 1, :]
                                n_el = g4 * K1
                                b2r, b2i = (0, 7) if is_filter else (2, 3)
                                zr = psb[b2r][:, 0:n_el]
                                zi = psb[b2i][:, 0:n_el]
                                P.op("pe", lambda e, yr=yr, zr=zr: e.matmul(zr, F2b[:, 0:128], yr, start=True, stop=False),
                                     reads=ypk + ["F2"], writes=[("ps", b2r)])
                                P.op("pe", lambda e, yi=yi, zr=zr: e.matmul(zr, F2b[:, 256:384], yi, start=False, stop=True),
                                     reads=ypk + ["F2"], writes=[("ps", b2r)])
                                P.op("pe", lambda e, yr=yr, zi=zi: e.matmul(zi, F2b[:, 128:256], yr, start=True, stop=False),
                                     reads=ypk + ["F2"], writes=[("ps", b2i)])
                                P.op("pe", lambda e, yi=yi, zi=zi: e.matmul(zi, F2b[:, 0:128], yi, start=False, stop=True),
                                     reads=ypk + ["F2"], writes=[("ps", b2i)])
                                zr3 = zr.rearrange("p (c k) -> p c k", c=g4)
                                zi3 = zi.rearrange("p (c k) -> p c k", c=g4)
                                if is_filter:
                                    rnb = vw(rn, cs, [[1, g4], [0, K1]])
                                    for (src_, rsel) in ((zr3, 0), (zi3, 1)):
                                        P.op("act", lambda e, src_=src_, rsel=rsel, cs=cs: e.activation(
                                            out=Ksp[:, cs:cs + g4, rsel, :], in_=src_, func=AF.Copy),
                                            reads=[("ps", (b2r, b2i)[rsel])], writes=[("Ksp", par)])
                                else:
                                    zb = gi_ % 2
                                    zs4 = Zs[zb][:, 0:2 * n_el].rearrange("p (c r k) -> p c r k", c=g4, r=2)
                                    P.op("act", lambda e, zs4=zs4, zr3=zr3: e.activation(out=zs4[:, :, 0, :], in_=zr3, func=AF.Copy),
                                         reads=[("ps", b2r)], writes=[("Zs", zb)])
                                    P.op("act", lambda e, zs4=zs4, zi3=zi3: e.activation(out=zs4[:, :, 1, :], in_=zi3, func=AF.Copy),
                                         reads=[("ps", b2i)], writes=[("Zs", zb)])
                                    zss4 = ZsS[zb][:, 0:2 * n_el].rearrange("p (c r k) -> p c r k", c=g4, r=2)
                                    P.op("act", lambda e, zss4=zss4, zi3=zi3: e.activation(out=zss4[:, :, 0, :], in_=zi3, func=AF.Copy),
                                         reads=[("ps", b2i)], writes=[("ZsS", zb)])
                                    P.op("act", lambda e, zss4=zss4, zr3=zr3: e.activation(out=zss4[:, :, 1, :], in_=zr3, func=AF.Copy),
                                         reads=[("ps", b2r)], writes=[("ZsS", zb)])
                                    zsflat = ZsS[zb][:, 0:2 * n_el].rearrange("p (c x) -> p c x", c=g4)
                                    zflat = Zs[zb][:, 0:2 * n_el].rearrange("p (c x) -> p c x", c=g4)
                                    kfl = Ksp[:, cs:cs + g4, :, :].rearrange("p c r k -> p c (r k)")
                                    p1 = tm1[zb][:, 0:2 * n_el].rearrange("p (c x) -> p c x", c=g4) if 2 * n_el <= 512 else None
                                    pa = Zs[zb]
                                    P.op("dve", lambda e, zflat=zflat, kfl=kfl, zb=zb, n_el=n_el: e.tensor_tensor(
                                        out=PR1[zb][:, 0:2 * n_el].rearrange("p (c x) -> p c x", c=g4), in0=zflat, in1=kfl, op=ALU.mult),
                                        reads=[("Zs", zb), ("Ksp", par)], writes=[("PR1", zb)])
                                    P.op("dve", lambda e, zsflat=zsflat, kfl=kfl, zb=zb, n_el=n_el: e.tensor_tensor(
                                        out=PR2[zb][:, 0:2 * n_el].rearrange("p (c x) -> p c x", c=g4), in0=zsflat, in1=kfl, op=ALU.mult),
                                        reads=[("ZsS", zb), ("Ksp", par)], writes=[("PR2", zb)])
                                    q1 = PR1[zb][:, 0:2 * n_el].rearrange("p (c r k) -> p c r k", c=g4, r=2)
                                    q2 = PR2[zb][:, 0:2 * n_el].rearrange("p (c r k) -> p c r k", c=g4, r=2)
                                    P.op("dve", lambda e, q1=q1, cs=cs: e.tensor_tensor(
                                        out=Zf[:, cs:cs + g4, 0, :], in0=q1[:, :, 0, :], in1=q1[:, :, 1, :], op=ALU.subtract),
                                        reads=[("PR1", zb)], writes=[("Zf", gi_)])
                                    P.op("pool", lambda e, q2=q2, cs=cs: e.tensor_tensor(
                                        out=Zf[:, cs:cs + g4, 1, :], in0=q2[:, :, 0, :], in1=q2[:, :, 1, :], op=ALU.add),
                                        reads=[("PR2", zb)], writes=[("Zf", gi_)])

                        PR1 = [sb("PR1_%d" % i, (128, 1024), F32, st) for i in range(2)]
                        PR2 = [sb("PR2_%d" % i, (128, 1024), F32, st) for i in range(2)]

                        def inv_dft(epilogue):
                            zfk = [("Zf", gi_) for gi_ in range(CCc // g4)]
                            for q in range(CCc // 2):
                                bk = 4 + q % 2
                                for cl in range(2):
                                    c = 2 * q + cl
                                    P.op("pe", lambda e, c=c, cl=cl, bk=bk: e.matmul(
                                        psb[bk][0:K1, cl * 256:(cl + 1) * 256], Zf[:, c, 0, :], R3b[:, 0:256], start=True, stop=False),
                                        reads=zfk + ["R3"], writes=[("ps", bk)])
                                    P.op("pe", lambda e, c=c, cl=cl, bk=bk: e.matmul(
                                        psb[bk][0:K1, cl * 256:(cl + 1) * 256], Zf[:, c, 1, :], R3b[:, 256:512], start=False, stop=True),
                                        reads=zfk + ["R3"], writes=[("ps", bk)])
                                tb_ = q % 2
                                pin = psb[bk][0:K1, :].rearrange("p (c x) -> p c x", c=2)
                                cr2 = vw(TWC, 0, [[0, 2], [1, 256]], np_=K1)
                                ci2 = vw(TWC, 256, [[0, 2], [1, 256]], np_=K1)
                                o1_ = tm1[tb_][0:K1, :].rearrange("p (c x) -> p c x", c=2)
                                o2_ = tm2[tb_][0:K1, :].rearrange("p (c x) -> p c x", c=2)
                                P.op("dve", lambda e, pin=pin, cr2=cr2, o1_=o1_: e.tensor_tensor(out=o1_, in0=pin, in1=cr2, op=ALU.mult),
                                     reads=[("ps", bk), "TWC"], writes=[("tm1", tb_)])
                                P.op("dve", lambda e, pin=pin, ci2=ci2, o2_=o2_: e.tensor_tensor(out=o2_, in0=pin, in1=ci2, op=ALU.mult),
                                     reads=[("ps", bk), "TWC"], writes=[("tm2", tb_)])
                                a1 = tm1[tb_][0:K1, :].rearrange("p (c r k) -> p c r k", c=2, r=2)
                                a2 = tm2[tb_][0:K1, :].rearrange("p (c r k) -> p c r k", c=2, r=2)
                                P.op("pool", lambda e, a1=a1, a2=a2, q=q: e.tensor_tensor(
                                    out=Up[0:K1, 2 * q:2 * q + 2, 0, :], in0=a1[:, :, 0, :], in1=a2[:, :, 1, :], op=ALU.subtract),
                                    reads=[("tm1", tb_), ("tm2", tb_)], writes=[("Up", q // 2)])
                                P.op("pool", lambda e, a1=a1, a2=a2, q=q: e.tensor_tensor(
                                    out=Up[0:K1, 2 * q:2 * q + 2, 1, :], in0=a2[:, :, 0, :], in1=a1[:, :, 1, :], op=ALU.add),
                                    reads=[("tm1", tb_), ("tm2", tb_)], writes=[("Up", q // 2)])
                            for gq in range(CCc // 4):
                                bk = 6
                                P.op("pe", lambda e, gq=gq: e.matmul(psb[6][0:A, :], G4b[0:K1, 0:A], Up[0:K1, 4 * gq:4 * gq + 4, 0, :],
                                                                     start=True, stop=False),
                                     reads=[("Up", gq), "G4"], writes=[("ps", 6)])
                                P.op("pe", lambda e, gq=gq: e.matmul(psb[6][0:A, :], G4b[0:K1, A:2 * A], Up[0:K1, 4 * gq:4 * gq + 4, 1, :],
                                                                     start=False, stop=True),
                                     reads=[("Up", gq), "G4"], writes=[("ps", 6)])
                                epilogue(gq, psb[6][0:A, :].rearrange("p (c b) -> p c b", c=4))

                        steps = [(ci, o_) for ci in range(NCH) for o_ in range(2)]

                        def emit_filter(k):
                            ci, o_ = steps[k]
                            par = k % 2
                            c0 = ci * CCc
                            rn = rn2[par]
                            if o_ == 0:
                                for d_ in range(2):
                                    P.dma("sp", adec[d_ * A:(d_ + 1) * A], bass.AP(sW["decl"].tensor, d_ * 2 * CH + c0, [[0, A], [CH, 2], [1, CCc]]),
                                          writes=["adec"])
                                P.op("act", lambda e: e.activation(out=adec[0:K1], in_=adec[0:K1], func=AF.Abs),
                                     reads=["adec"], writes=["adec"])
                            for bg in range(8):
                                bk = (0, 7)[bg % 2]
                                for bl in range(16):
                                    b_ = bg * 16 + bl
                                    P.op("pe", lambda e, b_=b_, bl=bl, bk=bk, o_=o_, c0=c0: e.matmul(
                                        psb[bk][0:K1, bl * CCc:(bl + 1) * CCc], h2s[:, b_, :], w3s[:, o_, c0:c0 + CCc], start=True, stop=True),
                                        reads=["h2s", "w3s"], writes=[("ps", bk)])
                                P.op("act", lambda e, bg=bg, bk=bk: e.activation(
                                    out=kf[0:K1, :, bg * 16:(bg + 1) * 16],
                                    in_=psb[bk][0:K1, 0:16 * CCc].rearrange("p (b c) -> p c b", b=16), func=AF.Copy),
                                    reads=[("ps", bk)], writes=["kf"])
                            t01b = vw(t01, 0, [[0, CCc], [1, 128]], np_=K1)
                            adb = vw(adec, o_ * CCc, [[1, CCc], [0, 128]], np_=K1)
                            P.op("dve", lambda e, t01b=t01b, adb=adb: e.tensor_tensor(out=Ee[0:K1], in0=t01b, in1=adb, op=ALU.mult),
                                 reads=["t01", "adec"], writes=["Ee"])
                            P.op("act", lambda e: e.activation(out=Ee[0:K1], in_=Ee[0:K1], func=AF.Exp, scale=-1.0),
                                 reads=["Ee"], writes=["Ee"])
                            P.op("dve", lambda e: e.tensor_tensor(out=kf[0:K1], in0=kf[0:K1], in1=Ee[0:K1], op=ALU.mult),
                                 reads=["kf", "Ee"], writes=["kf"])
                            P.op("dve", lambda e: e.tensor_reduce(out=ksum[0:K1], in_=kf[0:K1], axis=AX.X, op=ALU.add,
                                                                  apply_absolute_value=True), reads=["kf"], writes=["ksum"])
                            P.op("dve", lambda e: e.tensor_copy(out=ksumb[0:K1], in_=ksum[0:K1]), reads=["ksum"], writes=["ksumb"])
                            P.op("pe", lambda e: e.matmul(psb[7][:, 0:CCc], ones_b[0:K1, :], ksumb[0:K1, :], start=True, stop=True),
                                 reads=["ksumb", "ones_b"], writes=[("ps", 7)])
                            P.op("dve", lambda e, rn=rn: e.reciprocal(out=rn[:], in_=psb[7][:, 0:CCc]), reads=[("ps", 7)], writes=[("rn", par)])
                            rnb_t = vw(rn, 0, [[1, CCc], [0, 128]], np_=K1)
                            P.op("dve", lambda e, rnb_t=rnb_t: e.tensor_tensor(out=kfb[0:K1], in0=kf[0:K1], in1=rnb_t, op=ALU.mult),
                                 reads=["kf", ("rn", par)], writes=["kfb"])
                            fwd_dft(kfb, K1, True, "kfb", par)

                        def emit_data(k):
                            ci, o_ = steps[k]
                            par = k % 2
                            c0 = ci * CCc
                            pb = ci % 2
                            hvb, hx1b, hx2b = strm[0][pb], strm[1][pb], strm[2][pb]
                            if o_ == 0:
                                P.dma("sp", hbs[0:A], bass.AP(sW["hbl"].tensor, c0, [[0, A], [CH, 2], [1, CCc]]), writes=["hbs"])
                                for s_ in range(3):
                                    P.dma("sp", strm[s_][pb][0:A], bass.AP(sPc.tensor, (s_ * CH + c0) * T, [[128, A], [T, CCc], [1, 128]]),
                                          reads=["Pc"], writes=[("strm", s_, pb)])
                            hb_bc = vw(hbs, o_ * CCc, [[1, CCc], [0, 128]], np_=A)
                            if o_ == 0:
                                P.op("pool", lambda e, hvb=hvb, hb_bc=hb_bc: e.tensor_tensor(out=dsk[0:A], in0=hvb[0:A], in1=hb_bc, op=ALU.mult),
                                     reads=[("strm", 0, pb), "hbs"], writes=["dsk"])
                                fwd_dft(hvb, A, False, ("strm", 0, pb), par)
                            else:
                                P.op("act", lambda e: e.activation(out=inb[0:A], in_=zf_[0:A], func=AF.Copy),
                                     reads=["zf"], writes=["inb"])
                                P.op("pool", lambda e, hb_bc=hb_bc: e.tensor_tensor(out=dsk[0:A], in0=zf_[0:A], in1=hb_bc, op=ALU.mult),
                                     reads=["zf", "hbs"], writes=["dsk"])
                                fwd_dft(inb, A, False, "inb", par)

                            def epi(gq, yps, o_=o_, hx1b=hx1b, hx2b=hx2b, pb=pb):
                                cs = 4 * gq
                                P.op("dve", lambda e: e.tensor_tensor(out=ctmp[0:A, 0:4, :], in0=yps, in1=dsk[0:A, cs:cs + 4, :], op=ALU.add),
                                     reads=[("ps", 6), "dsk"], writes=["ctmp"])
                                if o_ == 0:
                                    P.op("pool", lambda e: e.tensor_tensor(out=zf_[0:A, cs:cs + 4, :], in0=ctmp[0:A, 0:4, :],
                                                                           in1=hx1b[0:A, cs:cs + 4, :], op=ALU.mult),
                                         reads=["ctmp", ("strm", 1, pb)], writes=["zf"])
                                else:
                                    P.op("pool", lambda e: e.tensor_tensor(out=oob[0:A, cs:cs + 4, :], in0=ctmp[0:A, 0:4, :],
                                                                           in1=hx2b[0:A, cs:cs + 4, :], op=ALU.mult),
                                         reads=["ctmp", ("strm", 2, pb)], writes=["oob"])
                            inv_dft(epi)
                            if o_ == 1:
                                P.dma("pool", bass.AP(sHOsend.tensor, c0 * T, [[128, A], [T, CCc], [1, 128]]), oob[0:A],
                                      reads=["oob"], writes=["HOs"])

                        P.replay(P.record(lambda: emit_filter(0)))
                        for k in range(len(steps)):
                            recD = P.record(lambda: emit_data(k))
                            recF = P.record(lambda: emit_filter(k + 1)) if k + 1 < len(steps) else []
                            P.replay(recD, recF)
                        P.barrier()


            phase_C()
            if G > 1:
                all_gather(sHOsend, sHOg, groups)
            def phase_D():
                if "D" in phases:
                    with contextlib.ExitStack() as st:
                        xT = sb("xT", (128, KC, TB), F32, st)
                        hT = sb("hT", (128, KC, TB), BF16, st)
                        actT = sb("actT", (128, FC, TB), BF16, st)
                        sq = [sb("sq%d" % i, (128, TB), F32, st) for i in range(2)]
                        sqb = [sb("sqb%d" % i, (128, TB), BF16, st) for i in range(2)]
                        rstd = sb("rstd", (128, TB), F32, st)
                        silu_t = [sb("silu%d" % i, (128, TB), F32, st) for i in range(2)]
                        w13buf = [sb("w13buf%d" % i, (128, KC, 2, 256), BF16, st) for i in range(2)]
                        w2buf = [sb("w2buf%d" % i, (128, FC, 256), BF16, st) for i in range(2)]
                        mix = sb("mixin", (128, KC, TB), BF16, st)
                        ytok = [sb("ytok%d" % i, (128, D // 2), F32, st) for i in range(2)]
                        hoidx = sb("hoidx_sb", (128, 8 * NBl), mybir.dt.int32, st)
                        P.dma("sp", hoidx[:], sW["hoidx"], writes=["hoidx"])
                        for blk in range(NBl):
                            t0 = blk * TB
                            P.dma("sp", xT[:], sX1[:, t0:t0 + TB].rearrange("(k p) t -> p k t", p=128), reads=["X1"],
                                  writes=[("xT", kc) for kc in range(KC)])
                            P.dma("sp", mix[:, 0:8, :], sAOs[:, t0:t0 + TB].rearrange("(k p) t -> p k t", p=128), reads=["AOs"],
                                  writes=[("mix", kc) for kc in range(8)])
                            for k8 in range(8):
                                P.dma_fn("pool", lambda e, k8=k8, blk=blk: e.indirect_dma_start(
                                    out=mix[:, 8 + k8, :], out_offset=None, in_=sHOg.rearrange("c (q j) -> (c q) j", j=TB),
                                    in_offset=bass.IndirectOffsetOnAxis(ap=hoidx[:, k8 * NBl + blk:k8 * NBl + blk + 1], axis=0)),
                                    reads=["HOs", "hoidx"], writes=[("mix", 8 + k8)])
                            for grp in range(2):
                                srcv = mix[:, grp * 8:(grp + 1) * 8, :]
                                dstv = hT[:, grp * 8:(grp + 1) * 8, :]
                                pss = psb[6]
                                for c in range(8):
                                    cc_ = grp * 8 + c
                                    P.op("act", lambda e, c=c, cc_=cc_: e.activation(out=sqb[c % 2][:], in_=mix[:, cc_, :], func=AF.Square),
                                         reads=[("mix", cc_)], writes=[("sqb", c % 2)])
                                    P.op("pe", lambda e, c=c: e.matmul(pss[:], ones_b[:], sqb[c % 2][:], start=(c == 0), stop=(c == 7)),
                                         reads=[("sqb", c % 2), "ones_b"], writes=[("ps", 6)])
                                P.op("act", lambda e: e.activation(out=rstd[:], in_=pss[:], func=AF.Sqrt, scale=1.0 / 1024.0, bias=eps_t[:]),
                                     reads=[("ps", 6), "eps_t"], writes=["rstd"])
                                P.op("dve", lambda e: e.reciprocal(out=rstd[:], in_=rstd[:]), reads=["rstd"], writes=["rstd"])
                                for c in range(8):
                                    cc_ = grp * 8 + c
                                    P.op("dve", lambda e, cc_=cc_: e.scalar_tensor_tensor(
                                        out=hT[:, cc_, :], in0=mix[:, cc_, :], scalar=gains[:, 2, cc_:cc_ + 1], in1=rstd[:],
                                        op0=ALU.mult, op1=ALU.mult),
                                        reads=[("mix", cc_), "rstd", ("gains", 2)], writes=[("hT", cc_)])
                            for pn in range(4):
                                bi = pn % 2
                                wb = w13buf[bi][:].rearrange("p k g c -> p k (g c)")
                                P.dma("sp", w13buf[bi][:].rearrange("p k g c -> p (k g c)"), wouts[pn].rearrange("p k c -> p (k c)"),
                                      reads=["wscratch"], writes=[("w13buf", bi)])
                                for sub in range(4):
                                    kc_out = pn * 4 + sub
                                    pq = psb[sub % 2]
                                    for kc in range(KC):
                                        P.op("pe", lambda e, kc=kc, sub=sub, pq=pq, wb=wb: e.matmul(
                                            pq[:], wb[:, kc, sub * 128:(sub + 1) * 128], hT[:, kc, :],
                                            start=(kc == 0), stop=(kc == KC - 1)),
                                            reads=[("w13buf", bi), ("hT", kc)], writes=[("ps", sub % 2)])
                                    P.op("dve", lambda e, pq=pq, kc_out=kc_out: e.tensor_tensor(
                                        out=xT[:, kc_out, :], in0=pq[:], in1=xT[:, kc_out, :], op=ALU.add),
                                        reads=[("ps", sub % 2), ("xT", kc_out)], writes=[("xT", kc_out)])
                            ffn(xT, hT, actT, sqb, rstd, 1, 3, (w13buf, w2buf, silu_t), "f2")
                            pss = psb[6]
                            for c in range(KC):
                                P.op("act", lambda e, c=c: e.activation(out=sqb[c % 2][:], in_=xT[:, c, :], func=AF.Square),
                                     reads=[("xT", c)], writes=[("sqb", c % 2)])
                                P.op("pe", lambda e, c=c: e.matmul(pss[:], ones_b[:], sqb[c % 2][:], start=(c == 0), stop=(c == KC - 1)),
                                     reads=[("sqb", c % 2), "ones_b"], writes=[("ps", 6)])
                            P.op("act", lambda e: e.activation(out=rstd[:], in_=pss[:], func=AF.Sqrt, scale=1.0 / D, bias=eps_t[:]),
                                 reads=[("ps", 6), "eps_t"], writes=["rstd"])
                            P.op("dve", lambda e: e.reciprocal(out=rstd[:], in_=rstd[:]), reads=["rstd"], writes=["rstd"])
                            for c in range(KC):
                                P.op("dve", lambda e, c=c: e.scalar_tensor_tensor(
                                    out=xT[:, c, :], in0=xT[:, c, :], scalar=gains[:, 4, c:c + 1], in1=rstd[:],
                                    op0=ALU.mult, op1=ALU.mult),
                                    reads=[("xT", c), "rstd", ("gains", 4)], writes=[("xT", c)])
                            for tt8 in range(8):
                                tt, hx = tt8 // 2, tt8 % 2
                                bi = tt8 % 2
                                for kc in range(hx * 8, hx * 8 + 8):
                                    pst = psb[kc % 4]
                                    P.op("pe", lambda e, kc=kc, tt=tt, pst=pst: e.transpose(
                                        pst[:, 0:128], xT[:, kc, tt * 128:(tt + 1) * 128], ident[:]),
                                        reads=[("xT", kc), "ident"], writes=[("ps", kc % 4)])
                                    if kc % 2 == 0:
                                        P.op("act", lambda e, kc=kc, bi=bi, pst=pst: e.activation(
                                            out=ytok[bi][:, (kc % 8) * 128:(kc % 8 + 1) * 128], in_=pst[:, 0:128], func=AF.Copy),
                                            reads=[("ps", kc % 4)], writes=[("ytok", bi)])
                                    else:
                                        P.op("dve", lambda e, kc=kc, bi=bi, pst=pst: e.tensor_copy(
                                            out=ytok[bi][:, (kc % 8) * 128:(kc % 8 + 1) * 128], in_=pst[:, 0:128]),
                                            reads=[("ps", kc % 4)], writes=[("ytok", bi)])
                                P.dma("act", sy_out[t0 + tt * 128:t0 + (tt + 1) * 128, hx * 1024:(hx + 1) * 1024], ytok[bi][:],
                                      reads=[("ytok", bi)], writes=["y"])
                        P.barrier()


            phase_D()
        for si_, (T_, G_) in enumerate(slots):
            emit_slot(si_, T_, G_)
        P.barrier()
        counts = P.finalize(top)
    return nc, counts


CC_MAX_OUT_BYTES = 4 * 1024 * 1024


def piece_rows(R, C, G, elem=2):
    rp = R
    while G * rp * C * elem > CC_MAX_OUT_BYTES or R % rp:
        rp -= 1
    return rp


def gathered_row(row, r, R, C, G):
    rp = piece_rows(R, C, G)
    return (row // rp) * G * rp + r * rp + (row % rp)


COMMON_WEIGHTS = ("ffn1_norm", "ffn1_w13", "ffn1_w2", "mix_norm", "w_in", "q_norm", "k_norm",
                  "filt_w1", "filt_b1", "filt_w2", "filt_b2", "filt_freq",
                  "group_out_norm", "w_out", "ffn2_norm", "ffn2_w13", "ffn2_w2", "final_norm")
SLOT_TABLES = ("ropeC", "ropeS", "kbias", "featsT", "t01tab", "masktab", "F1K", "twr", "twi", "twcr", "twci",
               "F2", "R3", "G4")


def slot_table_shapes(T, G):
    A = T // 128
    K1 = 2 * A
    Tl = T // G
    return {"ropeC": (128, Tl), "ropeS": (128, Tl), "kbias": (128, T // 128), "featsT": (33, 2 * T),
            "t01tab": (K1, 128), "masktab": (K1, 128), "F1K": (K1, 2 * K1), "twr": (128, K1), "twi": (128, K1),
            "twcr": (K1, 128), "twci": (K1, 128), "F2": (128, 384), "R3": (128, 512), "G4": (K1, 2 * A)}


_TABLE_CACHE = {}


def slot_inputs(si, T, G, rank, x_local, w):
    Tl = T // G
    CH = DH // G
    NBl = Tl // TB
    sfx = "_s%d" % si
    if T not in _TABLE_CACHE:
        _TABLE_CACHE[T] = make_tables(T, T)
    tb = _TABLE_CACHE[T]
    m = {"x" + sfx: np.ascontiguousarray(x_local, dtype=np.float32)}
    for k in SLOT_TABLES:
        a = tb[k]
        if k in ("ropeC", "ropeS"):
            a = np.ascontiguousarray(a[:, rank * Tl:(rank + 1) * Tl])
        m[k + sfx] = a
    c0 = rank * CH
    cw = np.asarray(w["conv_w"], np.float32).reshape(3, 3, 1024)[:, :, c0:c0 + CH]
    m["cwl" + sfx] = np.ascontiguousarray(cw.reshape(3, 3 * CH))
    cb = np.asarray(w["conv_b"], np.float32).reshape(3, 1024)[:, c0:c0 + CH]
    m["cbl" + sfx] = np.ascontiguousarray(cb.reshape(1, 3 * CH))
    m["hbl" + sfx] = np.ascontiguousarray(np.asarray(w["hyena_bias"], np.float32).reshape(2, 1024)[:, c0:c0 + CH])
    m["decl" + sfx] = np.ascontiguousarray(
        np.asarray(w["hyena_decay"], np.float32).reshape(4, 1024)[:, c0:c0 + CH])
    w3 = np.asarray(w["filt_w3"], np.float32).reshape(64, 4, 1024)[:, :, c0:c0 + CH]
    m["w3l" + sfx] = np.ascontiguousarray(w3.reshape(64, 4 * CH))
    nsg = 3 * CH // 128
    gps = CH // 128
    p = np.arange(128)
    cidx = np.zeros((128, nsg * G), np.int32)
    for sg in range(nsg):
        s_, grp = sg // gps, sg % gps
        for r in range(G):
            rows = s_ * 1024 + c0 + grp * 128 + p
            cidx[:, sg * G + r] = gathered_row(rows, r, 3072, Tl, G)
    m["cidx" + sfx] = cidx
    hoidx = np.zeros((128, 8 * NBl), np.int32)
    for k8 in range(8):
        for blk in range(NBl):
            ch = k8 * 128 + p
            grow = gathered_row(ch % CH, ch // CH, CH, T, G)
            hoidx[:, k8 * NBl + blk] = grow * (T // TB) + rank * NBl + blk
    m["hoidx" + sfx] = hoidx
    return m


def common_inputs(w):
    m = {}
    for k in COMMON_WEIGHTS:
        shp = WEIGHT_SHAPES[k]
        m[k] = np.ascontiguousarray(np.asarray(w[k], dtype=np.float32).reshape(shp if len(shp) > 1 else (1, shp[0])))
    m["ident"] = np.eye(128, dtype=np.float32)
    m["ones"] = np.ones((128, 128), np.float32)
    m["Pm"] = make_tables(512, 512)["Pm"]
    return m


SLOTS = [(8192, 2), (4096, 4)]
_PROG_CACHE = {}


def kernel(**inputs):
    x_prompt = np.asarray(inputs["x_prompt"], dtype=np.float32)
    x_sample = np.asarray(inputs["x_sample"], dtype=np.float32)
    (TL, GL), (TS, GS) = SLOTS
    assert x_sample.shape == (4, TL, D) and x_prompt.shape == (2, TS, D)
    if "prog" not in _PROG_CACHE:
        _PROG_CACHE["prog"] = build_program(SLOTS)
    nc, _ = _PROG_CACHE["prog"]
    cm = common_inputs(inputs)
    in_maps = []
    for core in range(8):
        m = dict(cm)
        sq, rk = core // GL, core % GL
        tl = TL // GL
        m.update(slot_inputs(0, TL, GL, rk, x_sample[sq, rk * tl:(rk + 1) * tl], inputs))
        sq, rk = core // GS, core % GS
        tl = TS // GS
        m.update(slot_inputs(1, TS, GS, rk, x_prompt[sq, rk * tl:(rk + 1) * tl], inputs))
        in_maps.append(m)
    res = run_bass_kernel_spmd(nc, in_maps, core_ids=list(range(8)))
    y_sample = np.zeros((4, TL, D), np.float32)
    y_prompt = np.zeros((2, TS, D), np.float32)
    for core in range(8):
        sq, rk = core // GL, core % GL
        tl = TL // GL
        y_sample[sq, rk * tl:(rk + 1) * tl] = np.asarray(res.results[core]["y_s0"], dtype=np.float32)
        sq, rk = core // GS, core % GS
        tl = TS // GS
        y_prompt[sq, rk * tl:(rk + 1) * tl] = np.asarray(res.results[core]["y_s1"], dtype=np.float32)
    return (y_prompt, y_sample)
```

```python
import contextlib
import math
import numpy as np
import ml_dtypes
import concourse.bass as bass
import concourse.mybir as mybir
from concourse.bass_utils import run_bass_kernel_spmd

F32 = mybir.dt.float32
BF16 = mybir.dt.bfloat16
AF = mybir.ActivationFunctionType
ALU = mybir.AluOpType
AX = mybir.AxisListType

D = 2048
DFF = 5632
KC = D // 128
FC = DFF // 128
TB = 512
DA = 1024
DH = 1024
NQH = 8
NKV = 2
EPS = 1e-6
CC = 32
GRID_W = 64

ENGS = ("pe", "act", "dve", "pool", "sp")
N_DMA_SEMS = 24


class Prog:
    def __init__(self, nc):
        self.nc = nc
        self.ops = []
        self.last_write = {}
        self.readers = {}
        self.dma_sem_last = [None] * N_DMA_SEMS
        self.dma_sem_count = [0] * N_DMA_SEMS
        self.dma_rr = 0

    def _add(self, eng, fn, reads, writes, is_dma):
        idx = len(self.ops)
        deps = set()
        for k in reads:
            lw = self.last_write.get(k)
            if lw is not None:
                deps.add(lw)
        for k in writes:
            lw = self.last_write.get(k)
            if lw is not None:
                deps.add(lw)
            for r in self.readers.get(k, ()):
                deps.add(r)
        op = dict(idx=idx, eng=eng, fn=fn, is_dma=is_dma, deps=deps, ms=False)
        if is_dma:
            s = self.dma_rr
            self.dma_rr = (self.dma_rr + 1) % N_DMA_SEMS
            prev = self.dma_sem_last[s]
            if prev is not None:
                deps.add(prev)
            self.dma_sem_count[s] += 1
            op["dsem"] = s
            op["dval"] = 16 * self.dma_sem_count[s]
            self.dma_sem_last[s] = idx
        deps.discard(idx)
        self.ops.append(op)
        for k in writes:
            self.last_write[k] = idx
            self.readers[k] = []
        for k in reads:
            if k in writes:
                continue
            lst = self.readers.setdefault(k, [])
            if not is_dma:
                lst[:] = [r for r in lst if self.ops[r]["is_dma"] or self.ops[r]["eng"] != eng]
            lst.append(idx)
        return idx

    rec = None

    def op(self, eng, fn, reads=(), writes=()):
        if self.rec is not None:
            self.rec.append((eng, fn, tuple(reads), tuple(writes), False))
            return None
        return self._add(eng, fn, tuple(reads), tuple(writes), False)

    def dma(self, eng, out, in_, reads=(), writes=(), **kw):
        def fn(e):
            return e.dma_start(out=out, in_=in_, **kw)
        if self.rec is not None:
            self.rec.append((eng, fn, tuple(reads), tuple(writes), True))
            return None
        return self._add(eng, fn, tuple(reads), tuple(writes), True)

    def dma_fn(self, eng, fn, reads=(), writes=()):
        if self.rec is not None:
            self.rec.append((eng, fn, tuple(reads), tuple(writes), True))
            return None
        return self._add(eng, fn, tuple(reads), tuple(writes), True)

    def record(self, f):
        assert self.rec is None
        self.rec = []
        try:
            f()
        finally:
            lst, self.rec = self.rec, None
        return lst

    def replay(self, *lists):
        lists = [l for l in lists if l]
        pos = [0] * len(lists)
        total = sum(len(l) for l in lists)
        for _ in range(total):
            j = min((k for k in range(len(lists)) if pos[k] < len(lists[k])),
                    key=lambda k: (pos[k] + 0.5) / len(lists[k]))
            self._add(*lists[j][pos[j]])
            pos[j] += 1

    def raw(self, eng, fn):
        idx = len(self.ops)
        self.ops.append(dict(idx=idx, eng=eng, fn=fn, is_dma=False, deps=set(), ms=False, raw=True))
        return idx

    def barrier(self):
        last = {}
        for o in self.ops:
            if (not o["is_dma"]) and o["fn"] is not None and not o.get("raw"):
                last[o["eng"]] = o["idx"]
        outstanding = [i for i in self.dma_sem_last if i is not None]
        for e in ENGS:
            idx = len(self.ops)
            deps = set(last.values()) | set(outstanding)
            self.ops.append(dict(idx=idx, eng=e, fn=None, is_dma=False, deps=deps, ms=False))
        self.last_write = {}
        self.readers = {}

    def finalize(self, stack):
        nc = self.nc
        ops = self.ops
        for o in ops:
            for d in o["deps"]:
                t = ops[d]
                if not t["is_dma"]:
                    if t["eng"] == "pe" and o["eng"] == "pe" and not o["is_dma"]:
                        continue
                    t["ms"] = True
        cnt = {e: 0 for e in ENGS}
        for o in ops:
            if o["ms"]:
                cnt[o["eng"]] += 1
                o["msval"] = cnt[o["eng"]]
        known = {e: {f: -1 for f in ENGS} for e in ENGS}
        known_dma = {e: set() for e in ENGS}
        for o in ops:
            e = o["eng"]
            need = {}
            dwaits = []
            for d in sorted(o["deps"]):
                t = ops[d]
                if t["is_dma"]:
                    if d not in known_dma[e]:
                        known_dma[e].add(d)
                        dwaits.append((("d", t["dsem"]), t["dval"]))
                else:
                    f = t["eng"]
                    if f == "pe" and e == "pe" and not o["is_dma"]:
                        continue
                    if t["fn"] is None:
                        continue
                    if d > known[e][f]:
                        need[f] = max(need.get(f, -1), d)
            waits = list(dwaits)
            for f, d in need.items():
                known[e][f] = d
                waits.append((("e", f), ops[d]["msval"]))
            o["waits"] = waits
        esem = {e: stack.enter_context(nc.semaphore("sem_" + e)) for e in ENGS}
        dsem = [stack.enter_context(nc.semaphore("dsem%d" % i)) for i in range(N_DMA_SEMS)]

        def semof(key):
            return esem[key[1]] if key[0] == "e" else dsem[key[1]]

        ccsem = stack.enter_context(nc.semaphore("ccsem"))
        per = {e: [o for o in ops if o["eng"] == e] for e in ENGS}
        stack.enter_context(nc.allow_non_contiguous_dma(reason="small strided tables / halo columns"))
        block = stack.enter_context(nc.Block())

        def run(engobj, lst, ename):
            for o in lst:
                for key, val in o["waits"]:
                    engobj.wait_ge(semof(key), val)
                if o["fn"] is None:
                    continue
                if o.get("raw"):
                    o["fn"](engobj, ccsem)
                    continue
                ins = o["fn"](engobj)
                if o["is_dma"]:
                    ins.then_inc(dsem[o["dsem"]], 16)
                elif o["ms"]:
                    ins.then_inc(esem[ename], 1)

        @block.tensor
        def _(e):
            run(e, per["pe"], "pe")

        @block.scalar
        def _(e):
            run(e, per["act"], "act")

        @block.vector
        def _(e):
            run(e, per["dve"], "dve")

        @block.gpsimd
        def _(e):
            run(e, per["pool"], "pool")

        @block.sync
        def _(e):
            run(e, per["sp"], "sp")

        return {e: len(per[e]) for e in ENGS}


def sb_ap(t, off, dims):
    return bass.AP(t, off, [list(d) for d in dims])


def make_tables(T, L):
    A = T // 128
    K1 = 2 * A
    N = 2 * T
    tb = {}
    tb["ident"] = np.eye(128, dtype=np.float32)
    tb["ones"] = np.ones((128, 128), np.float32)
    half = 32
    inv = (10000.0 ** (-np.arange(0, 64, 2, dtype=np.float32) / 64.0)).astype(np.float32)
    t = np.arange(T)
    row = (t // GRID_W).astype(np.float32)
    col = (t % GRID_W).astype(np.float32)
    ang_r = row[:, None] * inv[None]
    ang_c = col[:, None] * inv[None]
    C = np.zeros((128, T), np.float32)
    S = np.zeros((128, T), np.float32)
    for d in range(128):
        ang = ang_r if d < 64 else ang_c
        j = d % 32
        C[d] = np.cos(ang[:, j])
        first = (d % 64) < 32
        S[d] = (-1.0 if first else 1.0) * np.sin(ang[:, j])
    tb["ropeC"] = C
    tb["ropeS"] = S
    Pm = np.zeros((128, 128), np.float32)
    for d in range(128):
        partner = d + 32 if (d % 64) < 32 else d - 32
        Pm[partner, d] = 1.0
    tb["Pm"] = Pm
    kb = np.zeros((T,), np.float32)
    kb[L:] = -30000.0
    tb["kbias"] = np.ascontiguousarray(kb.reshape(T // 128, 128).T)
    bands = 16
    n = np.arange(N)
    pos = np.where(n < T, n, N - n).astype(np.int64)
    valid = ((n < L) | (n > N - L)).astype(np.float32)
    valid[T] = 0.0
    posc = np.minimum(pos, L - 1)
    t01 = (np.linspace(0.0, 1.0, L, dtype=np.float32))[posc]
    fr = np.linspace(1e-4, bands - 1, bands, dtype=np.float32)[None]
    w = (2.0 * math.pi * np.arange(L, dtype=np.float32)[:, None] / L).astype(np.float32)
    feats_L = np.concatenate([np.linspace(0.0, 1.0, L, dtype=np.float32)[:, None],
                              np.cos(fr * w), -np.sin(fr * w)], axis=-1).astype(np.float32)
    feats = feats_L[posc]
    tb["featsT"] = np.ascontiguousarray(feats.T)
    tb["t01tab"] = np.ascontiguousarray(t01.reshape(K1, 128)).astype(np.float32)
    tb["masktab"] = np.ascontiguousarray(valid.reshape(K1, 128)).astype(np.float32)
    a = np.arange(K1)[:, None]
    k1 = np.arange(K1)[None]
    ph = -2.0 * np.pi * (a * k1 % K1) / K1
    tb["F1K"] = np.concatenate([np.cos(ph), np.sin(ph)], axis=1).astype(np.float32)
    b = np.arange(128)[:, None]
    ph = -2.0 * np.pi * (b * k1) / N
    tb["twr"] = np.cos(ph).astype(np.float32)
    tb["twi"] = np.sin(ph).astype(np.float32)
    tb["twcr"] = np.ascontiguousarray(np.cos(ph).T).astype(np.float32)
    tb["twci"] = np.ascontiguousarray((-np.sin(ph)).T).astype(np.float32)
    k2 = np.arange(128)[None]
    ph = -2.0 * np.pi * ((b * k2) % 128) / 128
    F2r = np.cos(ph)
    F2i = np.sin(ph)
    tb["F2"] = np.concatenate([F2r, F2i, -F2i], axis=1).astype(np.float32)
    tb["R3"] = np.concatenate([F2r, -F2i, F2i, F2r], axis=1).astype(np.float32)
    aa = np.arange(A)[None]
    kk = np.arange(K1)[:, None]
    ph = 2.0 * np.pi * ((aa * kk) % K1) / K1
    tb["G4"] = np.concatenate([np.cos(ph) / N, -np.sin(ph) / N], axis=1).astype(np.float32)
    return tb


TABLE_NAMES = ["ident", "ones", "ropeC", "ropeS", "Pm", "kbias", "featsT", "t01tab", "masktab",
               "F1K", "twr", "twi", "twcr", "twci", "F2", "R3", "G4"]

WEIGHT_SHAPES = {
    "ffn1_norm": (D,), "ffn1_w13": (D, 2 * DFF), "ffn1_w2": (DFF, D), "mix_norm": (D,),
    "w_in": (D, 4608), "q_norm": (128,), "k_norm": (128,), "conv_w": (3, 3072), "conv_b": (3072,),
    "filt_w1": (33, 64), "filt_b1": (64,), "filt_w2": (64, 64), "filt_b2": (64,),
    "filt_w3": (64, 4096), "filt_freq": (64,), "hyena_decay": (2, 2, 1024), "hyena_bias": (2, 1024),
    "group_out_norm": (D,), "w_out": (D, D), "ffn2_norm": (D,), "ffn2_w13": (D, 2 * DFF),
    "ffn2_w2": (DFF, D), "final_norm": (D,),
}


def build_program(slots, debug_outputs=(), phases="0ABCD"):
    if isinstance(slots, int):
        slots = [(slots, 1)]
    nc = bass.Bass("TRN2", target_bir_lowering=False)
    P = Prog(nc)
    dr = {}

    def dram(name, shape, dt, kind):
        if name in debug_outputs and kind == "Internal":
            kind = "ExternalOutput"
        h = nc.dram_tensor(name, list(shape), dt, kind=kind)
        dr[name] = h.ap()
        return dr[name]

    def dbgtap(name, ap, shape, dt, reads):
        if ("dbg_" + name) not in debug_outputs or ("dbg_" + name) in dr:
            return
        h = nc.dram_tensor("dbg_" + name, list(shape), dt, kind="ExternalOutput")
        dr["dbg_" + name] = h.ap()
        P.dma("sp", h.ap(), ap, reads=reads, writes=["dbg_" + name])

    W = {k: dram(k, s if len(s) > 1 else (1, s[0]), F32, "ExternalInput") for k, s in WEIGHT_SHAPES.items()
         if k in COMMON_WEIGHTS}
    TBL = {k: dram(k, shp, F32, "ExternalInput") for k, shp in (("ident", (128, 128)), ("ones", (128, 128)), ("Pm", (128, 128)))}
    w13s = [dram("w13s%d" % i, (22, 128, KC, 2, 256), BF16, "Internal") for i in range(2)]
    w2s = [dram("w2s%d" % i, (8, 128, FC, 256), BF16, "Internal") for i in range(2)]
    wins = dram("wins", (9, 128, KC, 512), BF16, "Internal")
    wouts = dram("wouts", (4, 128, KC, 512), BF16, "Internal")

    with contextlib.ExitStack() as top:
        used_names = {}

        def sb(name, shape, dt, st=top):
            n = used_names.get(name, 0)
            used_names[name] = n + 1
            if n:
                name = "%s_r%d" % (name, n)
            return st.enter_context(nc.sbuf_tensor(name, list(shape), dt))

        ident = sb("ident_sb", (128, 128), F32)
        ones_f = sb("ones_f", (128, 128), F32)
        ones_b = sb("ones_b", (128, 128), BF16)
        gains = sb("gains", (128, 5, KC), F32)
        psb = [top.enter_context(nc.psum_tensor("psb%d" % i, [128, 512], F32)) for i in range(8)]

        P.dma("sp", ident[:], TBL["ident"], writes=["ident"])
        P.dma("sp", ones_f[:], TBL["ones"], writes=["ones_f"])
        P.op("dve", lambda e: e.tensor_copy(out=ones_b[:], in_=ones_f[:]), reads=["ones_f"], writes=["ones_b"])
        for gi, nm in enumerate(["ffn1_norm", "mix_norm", "group_out_norm", "ffn2_norm", "final_norm"]):
            P.dma("sp", gains[:, gi, :], W[nm].rearrange("o (k p) -> p (o k)", p=128), writes=[("gains", gi)],
                  allow_slow_non_contiguous=True)

        cast_now, cast_later = [], []

        def cast_items():
            for fi, (n13, n2) in enumerate([("ffn1_w13", "ffn1_w2"), ("ffn2_w13", "ffn2_w2")]):
                lst = cast_now if fi == 0 else cast_later
                w13 = W[n13]
                for r in range(KC):
                    for g in range(0, 22, 4):
                        npan = min(4, 22 - g)
                        wdt = npan * 256
                        srcs = [(w13[r * 128:(r + 1) * 128, g * 256:g * 256 + wdt], 0, wdt),
                                (w13[r * 128:(r + 1) * 128, DFF + g * 256:DFF + g * 256 + wdt], 1024, wdt)]
                        dsts = []
                        for gu in range(2):
                            dsts.append((w13s[fi][g:g + npan, :, r, gu, :].rearrange("n p c -> p n c"),
                                         gu * 1024, wdt, ("p (n c) -> p n c", dict(c=256))))
                        lst.append((srcs, dsts, 2048))
                w2 = W[n2]
                for r in range(FC):
                    srcs = [(w2[r * 128:(r + 1) * 128, :], 0, 2048)]
                    dsts = [(w2s[fi][:, :, r, :].rearrange("n p c -> p n c"), 0, 2048,
                             ("p (n c) -> p n c", dict(c=256)))]
                    lst.append((srcs, dsts, 2048))
            for r in range(KC):
                for g in range(0, 9, 4):
                    npan = min(4, 9 - g)
                    wdt = npan * 512
                    srcs = [(W["w_in"][r * 128:(r + 1) * 128, g * 512:g * 512 + wdt], 0, wdt)]
                    dsts = [(wins[g:g + npan, :, r, :].rearrange("n p c -> p n c"), 0, wdt,
                             ("p (n c) -> p n c", dict(c=512)))]
                    cast_now.append((srcs, dsts, wdt))
            for r in range(KC):
                srcs = [(W["w_out"][r * 128:(r + 1) * 128, :], 0, 2048)]
                dsts = [(wouts[:, :, r, :].rearrange("n p c -> p n c"), 0, 2048,
                         ("p (n c) -> p n c", dict(c=512)))]
                cast_later.append((srcs, dsts, 2048))
        cast_items()
        cast_cnt = [0]

        def cast_block(item, stg, stb, engs=("dve", "act"), ldq="sp", stq="pool"):
            src_aps, dst_aps, ncols = item
            i = cast_cnt[0] % len(stg)
            cast_cnt[0] += 1
            for ap, c0, n in src_aps:
                P.dma(ldq, stg[i][:, c0:c0 + n], ap, writes=[("stg", i)])
            eng = engs[cast_cnt[0] % len(engs)]
            if eng == "dve":
                P.op("dve", lambda e, i=i: e.tensor_copy(out=stb[i][:, :ncols], in_=stg[i][:, :ncols]),
                     reads=[("stg", i)], writes=[("stb", i)])
            else:
                P.op("act", lambda e, i=i: e.activation(out=stb[i][:, :ncols], in_=stg[i][:, :ncols], func=AF.Copy),
                     reads=[("stg", i)], writes=[("stb", i)])
            for ap, c0, n, pat in dst_aps:
                src = stb[i][:, c0:c0 + n]
                if pat is not None:
                    src = src.rearrange(pat[0], **pat[1])
                P.dma(stq, ap, src, reads=[("stb", i)], writes=["wscratch"])

        defer_casts = ("B" in phases)
        if "0" in phases:
            with contextlib.ExitStack() as st:
                stg = [sb("stg%d" % i, (128, 2048), F32, st) for i in range(4)]
                stb = [sb("stb%d" % i, (128, 2048), BF16, st) for i in range(4)]
                for item in cast_now:
                    cast_block(item, stg, stb, stq="act")
                if not defer_casts:
                    for item in cast_later:
                        cast_block(item, stg, stb, stq="act")
                    cast_later[:] = []
                P.barrier()
        else:
            cast_later[:] = []

        def ffn(xT, hT, actT, sq, rstd, fi, gi, wq, tag):
            norm_fm(xT, hT, sq, rstd, gi, 0, KC, float(D), tag)
            w13buf, w2buf, silu_t = wq
            for m2 in range(22):
                bi = m2 % 2
                P.dma("sp", w13buf[bi][:].rearrange("p k g c -> p (k g c)"),
                      w13s[fi][m2].rearrange("p k g c -> p (k g c)"),
                      reads=["wscratch"], writes=[("w13buf", bi)])
                for hh in range(2):
                    m = 2 * m2 + hh
                    pg, pu = psb[(m % 2) * 2], psb[(m % 2) * 2 + 1]
                    for kc in range(KC):
                        P.op("pe", lambda e, bi=bi, kc=kc, hh=hh, pg=pg: e.matmul(
                            pg[:], w13buf[bi][:, kc, 0, hh * 128:(hh + 1) * 128], hT[:, kc, :],
                            start=(kc == 0), stop=(kc == KC - 1)),
                            reads=[("w13buf", bi), ("hT", kc)], writes=[("ps", (m % 2) * 2)])
                    for kc in range(KC):
                        P.op("pe", lambda e, bi=bi, kc=kc, hh=hh, pu=pu: e.matmul(
                            pu[:], w13buf[bi][:, kc, 1, hh * 128:(hh + 1) * 128], hT[:, kc, :],
                            start=(kc == 0), stop=(kc == KC - 1)),
                            reads=[("w13buf", bi), ("hT", kc)], writes=[("ps", (m % 2) * 2 + 1)])
                    sg = silu_t[m % 2]
                    P.op("act", lambda e, pg=pg, sg=sg: e.activation(out=sg[:], in_=pg[:], func=AF.Silu),
                         reads=[("ps", (m % 2) * 2)], writes=[("silu", m % 2)])
                    P.op("dve", lambda e, pu=pu, sg=sg, m=m: e.tensor_tensor(
                        out=actT[:, m, :], in0=pu[:], in1=sg[:], op=ALU.mult),
                        reads=[("ps", (m % 2) * 2 + 1), ("silu", m % 2)], writes=[("actT", m)])
            for n2 in range(8):
                bi = n2 % 2
                P.dma("sp", w2buf[bi][:].rearrange("p k c -> p (k c)"), w2s[fi][n2].rearrange("p k c -> p (k c)"),
                      reads=["wscratch"], writes=[("w2buf", bi)])
                for hh in range(2):
                    kc_out = 2 * n2 + hh
                    py = psb[4 + (kc_out % 2)]
                    for m in range(FC):
                        P.op("pe", lambda e, bi=bi, m=m, hh=hh, py=py: e.matmul(
                            py[:], w2buf[bi][:, m, hh * 128:(hh + 1) * 128], actT[:, m, :],
                            start=(m == 0), stop=(m == FC - 1)),
                            reads=[("w2buf", bi), ("actT", m)], writes=[("ps", 4 + (kc_out % 2))])
                    P.op("dve", lambda e, py=py, kc_out=kc_out: e.scalar_tensor_tensor(
                        out=xT[:, kc_out, :], in0=py[:], scalar=0.5, in1=xT[:, kc_out, :],
                        op0=ALU.mult, op1=ALU.add),
                        reads=[("ps", 4 + (kc_out % 2)), ("xT", kc_out)], writes=[("xT", kc_out)])

        def norm_fm(src, dst, sq, rstd, gi, goff, nch, nfeat, tag, src_key="xT", dst_key="hT", out_scale=None):
            pss = psb[6]
            for c in range(nch):
                P.op("act", lambda e, c=c: e.activation(out=sq[c % 2][:], in_=src[:, c, :], func=AF.Square),
                     reads=[(src_key, c)], writes=[("sqb", c % 2)])
                P.op("pe", lambda e, c=c: e.matmul(pss[:], ones_b[:], sq[c % 2][:], start=(c == 0), stop=(c == nch - 1)),
                     reads=[("sqb", c % 2), "ones_b"], writes=[("ps", 6)])
            P.op("act", lambda e: e.activation(out=rstd[:], in_=pss[:], func=AF.Sqrt, scale=1.0 / nfeat, bias=eps_t[:]),
                 reads=[("ps", 6), "eps_t"], writes=["rstd"])
            P.op("dve", lambda e: e.reciprocal(out=rstd[:], in_=rstd[:]), reads=["rstd"], writes=["rstd"])
            for c in range(nch):
                P.op("dve", lambda e, c=c: e.scalar_tensor_tensor(
                    out=dst[:, c, :], in0=src[:, c, :], scalar=gains[:, gi, goff + c:goff + c + 1], in1=rstd[:],
                    op0=ALU.mult, op1=ALU.mult),
                    reads=[(src_key, c), "rstd", ("gains", gi)], writes=[(dst_key, c)])

        eps_t = sb("eps_t", (128, 1), F32)
        P.op("dve", lambda e: e.memset(eps_t[:], EPS), writes=["eps_t"])
        eps128_t = sb("eps128_t", (128, 1), F32)
        P.op("dve", lambda e: e.memset(eps128_t[:], EPS * 128.0), writes=["eps128_t"])


        ccdummy = sb("ccdummy", (128, 8), F32)
        cc_count = [0]

        def all_gather(src_ap, dst_ap, groups, defer=False):
            R_, C_ = src_ap.shape
            G_ = len(groups[0])
            rp = piece_rows(R_, C_, G_)
            for j in range(R_ // rp):
                all_gather_1(src_ap[j * rp:(j + 1) * rp, :], dst_ap[j * G_ * rp:(j + 1) * G_ * rp, :], groups, defer)

        def all_gather_1(src_ap, dst_ap, groups, defer):
            cc_count[0] += 1
            n_ = cc_count[0]
            if not defer:
                P.barrier()

            def fn(e, ccsem, src_ap=src_ap, dst_ap=dst_ap, n_=n_, groups=groups, defer=defer):
                e.collective_compute("AllGather", ALU.bypass, replica_groups=groups,
                                     ins=[src_ap.opt()], outs=[dst_ap.opt()]).then_inc(ccsem, 1)
                if not defer:
                    e.wait_ge(ccsem, n_)
            P.raw("pool", fn)
            if not defer:
                P.op("pool", lambda e: e.memset(ccdummy[:], 0.0), writes=["ccdummy"])
                P.barrier()

        def cc_wait_all():
            n_ = cc_count[0]
            P.barrier()
            P.raw("pool", lambda e, ccsem, n_=n_: e.wait_ge(ccsem, n_))
            P.op("pool", lambda e: e.memset(ccdummy[:], 0.0), writes=["ccdummy"])
            P.barrier()

        def emit_slot(si, T, G):
            Tl = T // G
            CH = DH // G
            A = T // 128
            K1 = 2 * A
            NT = T // 128
            NBl = Tl // TB
            N2 = 2 * T
            sfx = "_s%d" % si
            groups = [list(range(i, i + G)) for i in range(0, 8, G)]
            sx_in = dram("x" + sfx, (Tl, D), F32, "ExternalInput")
            sy_out = dram("y" + sfx, (Tl, D), F32, "ExternalOutput")
            STB = {k: dram(k + sfx, shp, F32, "ExternalInput") for k, shp in slot_table_shapes(T, G).items()}
            nsg = 3 * CH // 128
            sW = {
                "cwl": dram("cwl" + sfx, (3, 3 * CH), F32, "ExternalInput"),
                "cbl": dram("cbl" + sfx, (1, 3 * CH), F32, "ExternalInput"),
                "hbl": dram("hbl" + sfx, (2, CH), F32, "ExternalInput"),
                "decl": dram("decl" + sfx, (4, CH), F32, "ExternalInput"),
                "w3l": dram("w3l" + sfx, (64, 4 * CH), F32, "ExternalInput"),
                "cidx": dram("cidx" + sfx, (128, nsg * G), mybir.dt.int32, "ExternalInput"),
                "hoidx": dram("hoidx" + sfx, (128, 8 * NBl), mybir.dt.int32, "ExternalInput"),
            }
            sX1 = dram("X1" + sfx, (D, Tl), F32, "Internal")
            sQs = dram("Qs" + sfx, (NQH * 128, Tl), BF16, "Internal")
            sKsl = dram("Ksl" + sfx, (NKV * 128, Tl), BF16, "Internal")
            sVsl = dram("Vsl" + sfx, (Tl, 256), BF16, "Internal")
            sPsl = dram("Psl" + sfx, (3072, Tl), BF16, "Internal")
            sPc = dram("Pc" + sfx, (3 * CH, T), BF16, "Internal")
            sAOs = dram("AOs" + sfx, (DA, Tl), BF16, "Internal")
            sHOsend = dram("HOsend" + sfx, (CH, T), BF16, "Internal")
            if G > 1:
                sKg = dram("Kg" + sfx, (G * NKV * 128, Tl), BF16, "Internal")
                sVg = dram("Vg" + sfx, (G * Tl, 256), BF16, "Internal")
                sPsg = dram("Psg" + sfx, (G * 3072, Tl), BF16, "Internal")
                sHOg = dram("HOg" + sfx, (G * CH, T), BF16, "Internal")
            else:
                sKg, sVg, sPsg, sHOg = sKsl, sVsl, sPsl, sHOsend
            def phase_A():
                if "A" in phases:
                    with contextlib.ExitStack() as st:
                        xT = sb("xT", (128, KC, TB), F32, st)
                        hT = sb("hT", (128, KC, TB), BF16, st)
                        actT = sb("actT", (128, FC, TB), BF16, st)
                        sq = [sb("sq%d" % i, (128, TB), F32, st) for i in range(2)]
                        sqb = [sb("sqb%d" % i, (128, TB), BF16, st) for i in range(2)]
                        rstd = sb("rstd", (128, TB), F32, st)
                        silu_t = [sb("silu%d" % i, (128, TB), F32, st) for i in range(2)]
                        w13buf = [sb("w13buf%d" % i, (128, KC, 2, 256), BF16, st) for i in range(2)]
                        w2buf = [sb("w2buf%d" % i, (128, FC, 256), BF16, st) for i in range(2)]
                        xtok = [sb("xtok%d" % i, (128, D // 2), F32, st) for i in range(2)]
                        ropeC = sb("ropeC_sb", (128, TB), F32, st)
                        ropeS = sb("ropeS_sb", (128, TB), F32, st)
                        Pm = sb("Pm_sb", (128, 128), F32, st)
                        qkg = sb("qkg", (128, 4), F32, st)
                        qraw, t1, t2, hrs = sq[1], silu_t[0], silu_t[1], rstd
                        qo = [sb("qo%d" % i, (128, TB), BF16, st) for i in range(2)]
                        vo = [sb("vo%d" % i, (128, 256), BF16, st) for i in range(2)]
                        P.dma("sp", Pm[:], TBL["Pm"], writes=["Pm"])
                        for j, nm in enumerate(["q_norm", "k_norm"]):
                            src = W[nm]
                            P.dma("sp", qkg[:, 2 * j:2 * j + 1], src.rearrange("o p -> p o"), writes=[("qkg", 2 * j)],
                                  allow_slow_non_contiguous=True)
                            for blk in range(2):
                                for hf in range(2):
                                    d0 = blk * 64 + hf * 32
                                    s0 = blk * 64 + (1 - hf) * 32
                                    P.dma("sp", qkg[d0:d0 + 32, 2 * j + 1:2 * j + 2],
                                          src[:, s0:s0 + 32].rearrange("o p -> p o"), writes=[("qkg", 2 * j + 1, d0)],
                                          allow_slow_non_contiguous=True)
                        qkg_keys = [("qkg", 0), ("qkg", 2)] + [("qkg", 2 * j + 1, d0) for j in range(2) for d0 in (0, 32, 64, 96)]

                        for blk in range(NBl):
                            t0 = blk * TB
                            for tt8 in range(8):
                                tt, hx = tt8 // 2, tt8 % 2
                                bi = tt8 % 2
                                P.dma("sp", xtok[bi][:], sx_in[t0 + tt * 128:t0 + (tt + 1) * 128, hx * 1024:(hx + 1) * 1024],
                                      writes=[("xtok", bi)])
                                for kc in range(hx * 8, hx * 8 + 8):
                                    pst = psb[kc % 4]
                                    P.op("pe", lambda e, bi=bi, kc=kc, pst=pst: e.transpose(
                                        pst[:, 0:128], xtok[bi][:, (kc % 8) * 128:(kc % 8 + 1) * 128], ident[:]),
                                        reads=[("xtok", bi), "ident"], writes=[("ps", kc % 4)])
                                    if kc % 2 == 0:
                                        P.op("act", lambda e, kc=kc, tt=tt, pst=pst: e.activation(
                                            out=xT[:, kc, tt * 128:(tt + 1) * 128], in_=pst[:, 0:128], func=AF.Copy),
                                            reads=[("ps", kc % 4)], writes=[("xT", kc)])
                                    else:
                                        P.op("dve", lambda e, kc=kc, tt=tt, pst=pst: e.tensor_copy(
                                            out=xT[:, kc, tt * 128:(tt + 1) * 128], in_=pst[:, 0:128]),
                                            reads=[("ps", kc % 4)], writes=[("xT", kc)])
                            ffn(xT, hT, actT, sqb, rstd, 0, 0, (w13buf, w2buf, silu_t), "f1")
                            P.dma("pool", sX1[:, t0:t0 + TB].rearrange("(k p) t -> p k t", p=128), xT[:],
                                  reads=[("xT", kc) for kc in range(KC)], writes=["X1"])
                            norm_fm(xT, hT, sqb, rstd, 1, 0, KC, float(D), "mix")
                            P.dma("sp", ropeC[:], STB["ropeC"][:, t0:t0 + TB], writes=["ropeC"])
                            P.dma("sp", ropeS[:], STB["ropeS"][:, t0:t0 + TB], writes=["ropeS"])
                            for pn in range(9):
                                bi = pn % 2
                                wb = w13buf[bi][:].rearrange("p k g c -> p k (g c)")
                                P.dma("sp", w13buf[bi][:].rearrange("p k g c -> p (k g c)"),
                                      wins[pn].rearrange("p k c -> p (k c)"),
                                      reads=["wscratch"], writes=[("w13buf", bi)])
                                if pn == 2:
                                    for tt in range(4):
                                        pv = psb[4 + tt % 2]
                                        for kc in range(KC):
                                            P.op("pe", lambda e, kc=kc, tt=tt, pv=pv, wb=wb: e.matmul(
                                                pv[:, 0:256], hT[:, kc, tt * 128:(tt + 1) * 128], wb[:, kc, 256:512],
                                                start=(kc == 0), stop=(kc == KC - 1)),
                                                reads=[("w13buf", bi), ("hT", kc)], writes=[("ps", 4 + tt % 2)])
                                        P.op("act", lambda e, tt=tt, pv=pv: e.activation(out=vo[tt % 2][:], in_=pv[:, 0:256], func=AF.Copy),
                                             reads=[("ps", 4 + tt % 2)], writes=[("vo", tt % 2)])
                                        P.dma("pool", sVsl[t0 + tt * 128:t0 + (tt + 1) * 128, :], vo[tt % 2][:],
                                              reads=[("vo", tt % 2)], writes=["Vs"])
                                nsub = 2 if pn == 2 else 4
                                for sub in range(nsub):
                                    pq = psb[sub % 2]
                                    for kc in range(KC):
                                        P.op("pe", lambda e, kc=kc, sub=sub, pq=pq, wb=wb: e.matmul(
                                            pq[:], wb[:, kc, sub * 128:(sub + 1) * 128], hT[:, kc, :],
                                            start=(kc == 0), stop=(kc == KC - 1)),
                                            reads=[("w13buf", bi), ("hT", kc)], writes=[("ps", sub % 2)])
                                    if pn <= 2:
                                        isq = pn < 2
                                        head = pn * 4 + sub if isq else sub
                                        gcol = 0 if isq else 2
                                        P.op("act", lambda e, pq=pq: e.activation(out=qraw[:], in_=pq[:], func=AF.Copy),
                                             reads=[("ps", sub % 2)], writes=[("sq", 1)])
                                        P.op("act", lambda e, pq=pq: e.activation(out=sqb[0][:], in_=pq[:], func=AF.Square),
                                             reads=[("ps", sub % 2)], writes=[("sqb", 0)])
                                        P.op("pe", lambda e: e.matmul(psb[2][:], ones_b[:], sqb[0][:], start=True, stop=True),
                                             reads=[("sqb", 0), "ones_b"], writes=[("ps", 2)])
                                        P.op("pe", lambda e: e.matmul(psb[3][:], Pm[:], qraw[:], start=True, stop=True),
                                             reads=[("sq", 1), "Pm"], writes=[("ps", 3)])
                                        if isq:
                                            P.op("act", lambda e: e.activation(out=hrs[:], in_=psb[2][:], func=AF.Sqrt,
                                                                               scale=1.0, bias=eps128_t[:]),
                                                 reads=[("ps", 2), "eps128_t"], writes=["rstd"])
                                        else:
                                            P.op("act", lambda e: e.activation(out=hrs[:], in_=psb[2][:], func=AF.Sqrt,
                                                                               scale=1.0 / 128.0, bias=eps_t[:]),
                                                 reads=[("ps", 2), "eps_t"], writes=["rstd"])
                                        P.op("dve", lambda e: e.reciprocal(out=hrs[:], in_=hrs[:]), reads=["rstd"], writes=["rstd"])
                                        P.op("dve", lambda e, gcol=gcol: e.scalar_tensor_tensor(
                                            out=t1[:], in0=qraw[:], scalar=qkg[:, gcol:gcol + 1], in1=ropeC[:],
                                            op0=ALU.mult, op1=ALU.mult), reads=[("sq", 1), "ropeC"] + qkg_keys, writes=[("silu", 0)])
                                        P.op("dve", lambda e, gcol=gcol: e.scalar_tensor_tensor(
                                            out=t2[:], in0=psb[3][:], scalar=qkg[:, gcol + 1:gcol + 2], in1=ropeS[:],
                                            op0=ALU.mult, op1=ALU.mult), reads=[("ps", 3), "ropeS"] + qkg_keys, writes=[("silu", 1)])
                                        P.op("pool", lambda e: e.tensor_tensor(out=t1[:], in0=t1[:], in1=t2[:], op=ALU.add),
                                             reads=[("silu", 0), ("silu", 1)], writes=[("silu", 0)])
                                        oi = head % 2
                                        P.op("dve", lambda e, oi=oi: e.tensor_tensor(out=qo[oi][:], in0=t1[:], in1=hrs[:], op=ALU.mult),
                                             reads=[("silu", 0), "rstd"], writes=[("qo", oi)])
                                        dst = (sQs if isq else sKsl)[head * 128:(head + 1) * 128, t0:t0 + TB]
                                        P.dma("pool", dst, qo[oi][:], reads=[("qo", oi)], writes=["Qs" if isq else "Ks"])
                                    else:
                                        ch = (pn - 3) * 4 + sub
                                        oi = ch % 2
                                        if ch % 2 == 0:
                                            P.op("act", lambda e, pq=pq, oi=oi: e.activation(out=qo[oi][:], in_=pq[:], func=AF.Copy),
                                                 reads=[("ps", sub % 2)], writes=[("qo", oi)])
                                        else:
                                            P.op("dve", lambda e, pq=pq, oi=oi: e.tensor_copy(out=qo[oi][:], in_=pq[:]),
                                                 reads=[("ps", sub % 2)], writes=[("qo", oi)])
                                        P.dma("pool", sPsl[ch * 128:(ch + 1) * 128, t0:t0 + TB], qo[oi][:],
                                              reads=[("qo", oi)], writes=["Ps"])
                        P.barrier()


            phase_A()
            if G > 1:
                all_gather(sKsl, sKg, groups)
                all_gather(sVsl, sVg, groups)
                all_gather(sPsl, sPsg, groups, defer=True)
            def phase_B():
                if "B" in phases:
                    with contextlib.ExitStack() as st:
                        KT = sb("KT", (128, T), BF16, st)
                        Vg = sb("Vg", (128, NT, 128), BF16, st)
                        kbias = sb("kbias_sb", (128, NT), F32, st)
                        QT = [sb("QT%d" % i, (128, TB), BF16, st) for i in range(2)]
                        PT = [sb("PT%d" % i, (128, TB), BF16, st) for i in range(3)]
                        rec = sb("rec", (128, TB), F32, st)
                        ao = [sb("ao%d" % i, (128, TB), BF16, st) for i in range(2)]
                        if cast_later:
                            stg_b = [sb("stg%d" % i, (128, 2048), F32, st) for i in range(4)]
                            stb_b = [sb("stb%d" % i, (128, 2048), BF16, st) for i in range(4)]
                            n_iter = NKV * 4 * NBl
                            per_iter = -(-len(cast_later) // n_iter)
                        P.dma("sp", kbias[:], STB["kbias"], writes=["kbias"])
                        it = 0
                        for g in range(NKV):
                            for r_ in range(G):
                                P.dma("sp", KT[:, r_ * Tl:(r_ + 1) * Tl], sKg[r_ * 256 + g * 128:r_ * 256 + (g + 1) * 128, :],
                                      reads=["Ks"], writes=["KT"])
                            vsrc = sVg[:, g * 128:(g + 1) * 128].rearrange("(n p) d -> p n d", p=128)
                            nvs = max(1, NT // 16)
                            for vq in range(nvs):
                                n0, n1 = vq * (NT // nvs), (vq + 1) * (NT // nvs)
                                P.dma("sp", Vg[:, n0:n1, :], vsrc[:, n0:n1, :], reads=["Vs"], writes=["Vg"])
                            for h in range(4):
                                hq = g * 4 + h
                                for qc in range(NBl):
                                    qi = it % 2
                                    it += 1
                                    P.dma("sp", QT[qi][:], sQs[hq * 128:(hq + 1) * 128, qc * TB:(qc + 1) * TB],
                                          reads=["Qs"], writes=[("QT", qi)])
                                    bo, bs = 3 + 2 * qi, 4 + 2 * qi

                                    def S(kt, qi=qi):
                                        P.op("pe", lambda e: e.matmul(psb[kt % 3][:], KT[:, kt * 128:(kt + 1) * 128], QT[qi][:],
                                                                      start=True, stop=True),
                                             reads=["KT", ("QT", qi)], writes=[("ps", kt % 3)])

                                    def E(kt):
                                        P.op("act", lambda e: e.activation(out=PT[kt % 3][:], in_=psb[kt % 3][:], func=AF.Exp,
                                                                           bias=kbias[:, kt:kt + 1], scale=1.0),
                                             reads=[("ps", kt % 3), "kbias"], writes=[("PT", kt % 3)])

                                    def PV(kt, bo=bo, bs=bs):
                                        P.op("pe", lambda e: e.matmul(psb[bo][:], Vg[:, kt, :], PT[kt % 3][:],
                                                                      start=(kt == 0), stop=(kt == NT - 1)),
                                             reads=["Vg", ("PT", kt % 3)], writes=[("ps", bo)])
                                        P.op("pe", lambda e: e.matmul(psb[bs][:], ones_b[:], PT[kt % 3][:],
                                                                      start=(kt == 0), stop=(kt == NT - 1)),
                                             reads=["ones_b", ("PT", kt % 3)], writes=[("ps", bs)])

                                    S(0)
                                    if NT > 1:
                                        S(1)
                                    for kt in range(NT):
                                        E(kt)
                                        if kt + 2 < NT:
                                            S(kt + 2)
                                        PV(kt)
                                    P.op("dve", lambda e, bs=bs: e.reciprocal(out=rec[:], in_=psb[bs][:]),
                                         reads=[("ps", bs)], writes=["rec"])
                                    P.op("dve", lambda e, bo=bo, qi=qi: e.tensor_tensor(out=ao[qi][:], in0=psb[bo][:], in1=rec[:], op=ALU.mult),
                                         reads=[("ps", bo), "rec"], writes=[("ao", qi)])
                                    P.dma("pool", sAOs[hq * 128:(hq + 1) * 128, qc * TB:(qc + 1) * TB], ao[qi][:],
                                          reads=[("ao", qi)], writes=["AOs"])
                                    if cast_later:
                                        for _ in range(min(per_iter, len(cast_later))):
                                            cast_block(cast_later.pop(0), stg_b, stb_b, engs=("dve",), ldq="pool")
                        P.barrier()


            phase_B()
            if G > 1:
                cc_wait_all()

            def phase_C():
                if "C" in phases:
                    with contextlib.ExitStack() as st:
                        CCc = 16
                        NCH = CH // CCc
                        nch1 = min(CCc, 512 // (2 * K1))
                        g4 = min(CCc, 512 // K1)
                        MAGIC = 12582912.0
                        TWO_PI = 2.0 * math.pi

                        def fsz(t):
                            n = 1
                            for d_ in t.shape[1:]:
                                n *= d_
                            return n

                        def vw(t, off, dims, p0=0, np_=128):
                            return bass.AP(t, p0 * fsz(t) + off, [[fsz(t), np_]] + [list(d_) for d_ in dims])

                        ld = sb("ld_tmp", (128, 512), F32, st)
                        F1Kb = sb("F1Kb", (128, 2 * K1), BF16, st)
                        F2b = sb("F2b", (128, 384), BF16, st)
                        R3b = sb("R3b", (128, 512), BF16, st)
                        G4b = sb("G4b", (128, 2 * A), BF16, st)
                        TW = sb("TW", (128, 2, 2 * K1), F32, st)
                        TWC = sb("TWC", (128, 2, 256), F32, st)
                        t01 = sb("t01", (128, 128), F32, st)
                        msk = sb("msk", (128, 128), F32, st)

                        def load_cast(dst, name, rows, cols):
                            P.dma("sp", ld[0:rows, 0:cols], STB[name], writes=["ld"])
                            P.op("dve", lambda e: e.tensor_copy(out=dst[0:rows, 0:cols], in_=ld[0:rows, 0:cols]),
                                 reads=["ld"], writes=[name])
                        load_cast(F1Kb, "F1K", K1, 2 * K1)
                        load_cast(F2b, "F2", 128, 384)
                        load_cast(R3b, "R3", 128, 512)
                        load_cast(G4b, "G4", K1, 2 * A)
                        for j, nm in enumerate(["twr", "twi"]):
                            for rep in range(2):
                                P.dma("sp", TW[:, j, rep * K1:(rep + 1) * K1], STB[nm], writes=["TW"])
                        for j, nm in enumerate(["twcr", "twci"]):
                            for rep in range(2):
                                P.dma("sp", TWC[0:K1, j, rep * 128:(rep + 1) * 128], STB[nm], writes=["TWC"])
                        P.dma("sp", t01[0:K1, :], STB["t01tab"], writes=["t01"])
                        P.dma("sp", msk[0:K1, :], STB["masktab"], writes=["msk"])

                        h2s = sb("h2s", (128, 128, K1), BF16, st)
                        w3s = sb("w3s", (128, 2, CH), BF16, st)
                        w1 = sb("w1", (33, 64), F32, st)
                        w2d = sb("w2d", (64, 128), F32, st)
                        fvec = sb("fvec", (128, 4), F32, st)
                        P.op("pool", lambda e: e.memset(h2s[:], 0.0), writes=["h2s"])
                        P.dma("sp", w1[:], W["filt_w1"], writes=["w1"])
                        for rep in range(2):
                            P.dma("sp", w2d[:, rep * 64:(rep + 1) * 64], W["filt_w2"], writes=["w2d"])
                            P.dma("sp", fvec[rep * 64:(rep + 1) * 64, 2:3], W["filt_b2"].rearrange("o p -> p o"), writes=["fvec"])
                            P.dma("sp", fvec[rep * 64:(rep + 1) * 64, 3:4], W["filt_freq"].rearrange("o p -> p o"), writes=["fvec"])
                        P.dma("sp", fvec[0:64, 0:1], W["filt_b1"].rearrange("o p -> p o"), writes=["fvec"])
                        P.dma("sp", fvec[0:64, 1:2], W["filt_freq"].rearrange("o p -> p o"), writes=["fvec"])
                        w3v = sW["w3l"].rearrange("j (d o c) -> j d o c", d=2, o=2)
                        for o_ in range(2):
                            for d_ in range(2):
                                P.dma("sp", ld[d_ * 64:(d_ + 1) * 64, 0:CH], w3v[:, d_, o_, :], writes=["ld"])
                            P.op("dve", lambda e, o_=o_: e.tensor_copy(out=w3s[:, o_, :], in_=ld[:, 0:CH]),
                                 reads=["ld"], writes=["w3s"])
                        with contextlib.ExitStack() as st2:
                            fb = [sb("fb%d" % i, (33, 512), F32, st2) for i in range(2)]
                            u = sb("u_mlp", (128, 512), F32, st2)
                            kk = sb("kk_mlp", (128, 512), F32, st2)
                            h1 = sb("h1_mlp", (64, 512), F32, st2)
                            mkblk = [sb("mkblk%d" % i, (128, 512), F32, st2) for i in range(2)]

                            def sin_layer(ps, np_, bcol, fcol, out_ap_fn):
                                P.op("dve", lambda e: e.tensor_scalar(out=u[0:np_, :], in0=ps[0:np_, :], scalar1=fvec[0:np_, bcol:bcol + 1],
                                                                      scalar2=fvec[0:np_, fcol:fcol + 1], op0=ALU.add, op1=ALU.mult),
                                     reads=[("ps", 7), "fvec"], writes=["u"])
                                P.op("dve", lambda e: e.tensor_scalar(out=kk[0:np_, :], in0=u[0:np_, :], scalar1=1.0 / TWO_PI, scalar2=MAGIC,
                                                                      op0=ALU.mult, op1=ALU.add), reads=["u"], writes=["kk"])
                                P.op("dve", lambda e: e.tensor_scalar(out=kk[0:np_, :], in0=kk[0:np_, :], scalar1=-MAGIC, scalar2=-TWO_PI,
                                                                      op0=ALU.add, op1=ALU.mult), reads=["kk"], writes=["kk"])
                                P.op("dve", lambda e: e.tensor_tensor(out=u[0:np_, :], in0=u[0:np_, :], in1=kk[0:np_, :], op=ALU.add),
                                     reads=["u", "kk"], writes=["u"])
                                out_ap_fn()

                            for j in range(N2 // 512):
                                bi = j % 2
                                P.dma("sp", fb[bi][:], STB["featsT"][:, j * 512:(j + 1) * 512], writes=[("fb", bi)])
                                P.op("pe", lambda e, bi=bi: e.matmul(psb[7][0:64, :], w1[:], fb[bi][:], start=True, stop=True),
                                     reads=["w1", ("fb", bi)], writes=[("ps", 7)])

                                def o1():
                                    P.op("act", lambda e: e.activation(out=h1[:], in_=u[0:64, :], func=AF.Sin), reads=["u"], writes=["h1"])
                                sin_layer(psb[7], 64, 0, 1, o1)
                                P.op("pe", lambda e: e.matmul(psb[7][:], w2d[:], h1[:], start=True, stop=True),
                                     reads=["w2d", "h1"], writes=[("ps", 7)])
                                fwd = (4 * j) < A
                                r0 = 0 if fwd else 64

                                P.dma("sp", mkblk[bi][:], bass.AP(STB["masktab"].tensor, j * 512, [[0, 128], [1, 512]]),
                                      writes=[("mkblk", bi)])

                                def o2(j=j, r0=r0, bi=bi):
                                    dst = vw(h2s, 4 * j, [[K1, 128], [1, 4]], p0=r0, np_=64)
                                    src = kk[r0:r0 + 64, :].rearrange("p (a b) -> p b a", a=4)
                                    mk_ = mkblk[bi][r0:r0 + 64, :].rearrange("p (a b) -> p b a", a=4)
                                    P.op("act", lambda e: e.activation(out=kk[r0:r0 + 64, :], in_=u[r0:r0 + 64, :], func=AF.Sin),
                                         reads=["u", "kk"], writes=["kk"])
                                    P.op("dve", lambda e: e.tensor_tensor(out=dst, in0=src, in1=mk_, op=ALU.mult),
                                         reads=["kk", ("mkblk", bi)], writes=["h2s"])
                                sin_layer(psb[7], 128, 2, 3, o2)
                            P.barrier()

                        with contextlib.ExitStack() as st3:
                            TQ = min(2048, T)
                            rawrow = [sb("rawrow%d" % i, (128, T + 2), BF16, st3) for i in range(2)]
                            cacc = [sb("cacc%d" % i, (128, TQ), F32, st3) for i in range(2)]
                            coutb = [sb("coutb%d" % i, (128, TQ), BF16, st3) for i in range(2)]
                            cwT = sb("cwT", (128, 3, nsg), F32, st3)
                            cbT = sb("cbT", (128, nsg), F32, st3)
                            cidx = sb("cidx_sb", (128, nsg * G), mybir.dt.int32, st3)
                            P.dma("sp", cidx[:], sW["cidx"], writes=["cidx"])
                            for i_ in range(2):
                                P.op("dve", lambda e, i_=i_: e.memset(rawrow[i_][:, 0:1], 0.0), writes=[("rawrow", i_)])
                                P.op("dve", lambda e, i_=i_: e.memset(rawrow[i_][:, T + 1:T + 2], 0.0), writes=[("rawrow", i_)])
                            for tp_ in range(3):
                                P.dma("sp", cwT[:, tp_, :], bass.AP(sW["cwl"].tensor, tp_ * 3 * CH, [[1, 128], [128, nsg]]), writes=["cwT"])
                            P.dma("sp", cbT[:], bass.AP(sW["cbl"].tensor, 0, [[1, 128], [128, nsg]]), writes=["cbT"])
                            it_ = 0
                            for sg in range(nsg):
                                rb = sg % 2
                                for r_ in range(G):
                                    P.dma_fn("pool", lambda e, rb=rb, r_=r_, sg=sg: e.indirect_dma_start(
                                        out=rawrow[rb][:, 1 + r_ * Tl:1 + (r_ + 1) * Tl], out_offset=None, in_=sPsg,
                                        in_offset=bass.IndirectOffsetOnAxis(ap=cidx[:, sg * G + r_:sg * G + r_ + 1], axis=0)),
                                        reads=["Ps", "cidx"], writes=[("rawrow", rb)])
                                for q in range(T // TQ):
                                    ai = it_ % 2
                                    it_ += 1
                                    q0 = q * TQ
                                    P.op("act", lambda e, rb=rb, ai=ai, q0=q0, sg=sg: e.activation(
                                        out=cacc[ai][:], in_=rawrow[rb][:, q0:q0 + TQ], func=AF.Identity,
                                        scale=cwT[:, 0, sg:sg + 1], bias=cbT[:, sg:sg + 1]),
                                        reads=[("rawrow", rb), "cwT", "cbT"], writes=[("cacc", ai)])
                                    P.op("dve", lambda e, rb=rb, ai=ai, q0=q0, sg=sg: e.scalar_tensor_tensor(
                                        out=cacc[ai][:], in0=rawrow[rb][:, q0 + 1:q0 + 1 + TQ], scalar=cwT[:, 1, sg:sg + 1], in1=cacc[ai][:],
                                        op0=ALU.mult, op1=ALU.add),
                                        reads=[("rawrow", rb), "cwT", ("cacc", ai)], writes=[("cacc", ai)])
                                    P.op("dve", lambda e, rb=rb, ai=ai, q0=q0, sg=sg: e.scalar_tensor_tensor(
                                        out=coutb[ai][:], in0=rawrow[rb][:, q0 + 2:q0 + 2 + TQ], scalar=cwT[:, 2, sg:sg + 1], in1=cacc[ai][:],
                                        op0=ALU.mult, op1=ALU.add),
                                        reads=[("rawrow", rb), "cwT", ("cacc", ai)], writes=[("coutb", ai)])
                                    P.dma("sp", sPc[sg * 128:(sg + 1) * 128, q0:q0 + TQ], coutb[ai][:],
                                          reads=[("coutb", ai)], writes=["Pc"])
                            P.barrier()
                        strm = [[sb("strm%d_%d" % (s_, i), (128, CCc, 128), BF16, st) for i in range(2)] for s_ in range(3)]
                        hbs = sb("hbs", (128, 2, CCc), F32, st)
                        adec = sb("adec", (128, 2, CCc), F32, st)
                        zf_ = sb("zf", (128, CCc, 128), F32, st)
                        dsk = sb("dsk", (128, CCc, 128), F32, st)
                        ctmp = sb("ctmp", (128, 4, 128), F32, st)
                        inb = sb("inb", (128, CCc, 128), BF16, st)
                        kf = sb("kf", (128, CCc, 128), F32, st)
                        Ee = sb("Ee", (128, CCc, 128), F32, st)
                        kfb = sb("kfb", (128, CCc, 128), BF16, st)
                        ksum = sb("ksum", (128, CCc), F32, st)
                        ksumb = sb("ksumb", (128, CCc), BF16, st)
                        rn2 = [sb("rn%d" % i, (128, CCc), F32, st) for i in range(2)]
                        Yp_d = sb("Yp", (128, CCc, 2, K1), BF16, st)
                        tm1_d = [sb("tm1_%d" % i, (128, 512), F32, st) for i in range(2)]
                        tm2_d = [sb("tm2_%d" % i, (128, 512), F32, st) for i in range(2)]
                        tm1, tm2 = tm1_d, tm2_d
                        Ksp2 = [sb("Ksp%d" % i, (128, CCc, 2, K1), BF16, st) for i in range(2)]
                        YpF = sb("YpF", (128, CCc, 2, K1), BF16, st)
                        tmF1 = [sb("tmF1_%d" % i, (128, 512), F32, st) for i in range(2)]
                        tmF2 = [sb("tmF2_%d" % i, (128, 512), F32, st) for i in range(2)]
                        Zs = [sb("Zs%d" % i, (128, 2 * 512), BF16, st) for i in range(2)]
                        ZsS = [sb("ZsS%d" % i, (128, 2 * 512), BF16, st) for i in range(2)]
                        Zf = sb("Zf", (128, CCc, 2, K1), BF16, st)
                        Up = sb("Up", (128, CCc, 2, 128), BF16, st)
                        oob = sb("oob", (128, CCc, 128), BF16, st)

                        def fwd_dft(src_b, krows, is_filter, tag, par):
                            banks1 = (0, 7) if is_filter else (1, 6)
                            tm1, tm2 = (tmF1, tmF2) if is_filter else (tm1_d, tm2_d)
                            Yp = YpF if is_filter else Yp_d
                            ypn = "YpF" if is_filter else "Yp"
                            t1n, t2n = ("tmF1", "tmF2") if is_filter else ("tm1", "tm2")
                            Ksp = Ksp2[par]
                            rn = rn2[par]
                            for q in range(CCc // nch1):
                                bk = banks1[q % 2]
                                ti = q % 2
                                for cl in range(nch1):
                                    c = q * nch1 + cl
                                    P.op("pe", lambda e, c=c, cl=cl, bk=bk: e.matmul(
                                        psb[bk][:, cl * 2 * K1:(cl + 1) * 2 * K1], src_b[0:krows, c, :], F1Kb[0:krows, :],
                                        start=True, stop=True), reads=[tag, "F1K"], writes=[("ps", bk)])
                                n_el = nch1 * 2 * K1
                                pin = psb[bk][:, 0:n_el].rearrange("p (c x) -> p c x", c=nch1)
                                twr2 = vw(TW, 0, [[0, nch1], [1, 2 * K1]])
                                twi2 = vw(TW, 2 * K1, [[0, nch1], [1, 2 * K1]])
                                o1_ = tm1[ti][:, 0:n_el].rearrange("p (c x) -> p c x", c=nch1)
                                o2_ = tm2[ti][:, 0:n_el].rearrange("p (c x) -> p c x", c=nch1)
                                P.op("dve", lambda e, pin=pin, twr2=twr2, o1_=o1_: e.tensor_tensor(out=o1_, in0=pin, in1=twr2, op=ALU.mult),
                                     reads=[("ps", bk), "TW"], writes=[(t1n, ti)])
                                P.op("dve", lambda e, pin=pin, twi2=twi2, o2_=o2_: e.tensor_tensor(out=o2_, in0=pin, in1=twi2, op=ALU.mult),
                                     reads=[("ps", bk), "TW"], writes=[(t2n, ti)])
                                a1 = tm1[ti][:, 0:n_el].rearrange("p (c r k) -> p c r k", c=nch1, r=2)
                                a2 = tm2[ti][:, 0:n_el].rearrange("p (c r k) -> p c r k", c=nch1, r=2)
                                c0_ = q * nch1
                                P.op("pool", lambda e, a1=a1, a2=a2, c0_=c0_: e.tensor_tensor(
                                    out=Yp[:, c0_:c0_ + nch1, 0, :], in0=a1[:, :, 0, :], in1=a2[:, :, 1, :], op=ALU.subtract),
                                    reads=[(t1n, ti), (t2n, ti)], writes=[(ypn, q)])
                                P.op("pool", lambda e, a1=a1, a2=a2, c0_=c0_: e.tensor_tensor(
                                    out=Yp[:, c0_:c0_ + nch1, 1, :], in0=a2[:, :, 0, :], in1=a1[:, :, 1, :], op=ALU.add),
                                    reads=[(t1n, ti), (t2n, ti)], writes=[(ypn, q)])
                            ypk = [(ypn, q) for q in range(CCc // nch1)]
                            for gi_ in range(CCc // g4):
                                cs = gi_ * g4
                                yr = Yp[:, cs:cs + g4, 0, :]
                                yi = Yp[:, cs:cs + g4, 1, :]
                                n_el = g4 * K1
                                b2r, b2i = (0, 7) if is_filter else (2, 3)
                                zr = psb[b2r][:, 0:n_el]
                                zi = psb[b2i][:, 0:n_el]
                                P.op("pe", lambda e, yr=yr, zr=zr: e.matmul(zr, F2b[:, 0:128], yr, start=True, stop=False),
                                     reads=ypk + ["F2"], writes=[("ps", b2r)])
                                P.op("pe", lambda e, yi=yi, zr=zr: e.matmul(zr, F2b[:, 256:384], yi, start=False, stop=True),
                                     reads=ypk + ["F2"], writes=[("ps", b2r)])
                                P.op("pe", lambda e, yr=yr, zi=zi: e.matmul(zi, F2b[:, 128:256], yr, start=True, stop=False),
                                     reads=ypk + ["F2"], writes=[("ps", b2i)])
                                P.op("pe", lambda e, yi=yi, zi=zi: e.matmul(zi, F2b[:, 0:128], yi, start=False, stop=True),
                                     reads=ypk + ["F2"], writes=[("ps", b2i)])
                                zr3 = zr.rearrange("p (c k) -> p c k", c=g4)
                                zi3 = zi.rearrange("p (c k) -> p c k", c=g4)
                                if is_filter:
                                    rnb = vw(rn, cs, [[1, g4], [0, K1]])
                                    for (src_, rsel) in ((zr3, 0), (zi3, 1)):
                                        P.op("dve", lambda e, src_=src_, rsel=rsel, rnb=rnb, cs=cs: e.tensor_tensor(
                                            out=Ksp[:, cs:cs + g4, rsel, :], in0=src_, in1=rnb, op=ALU.mult),
                                            reads=[("ps", (b2r, b2i)[rsel]), ("rn", par)], writes=[("Ksp", par)])
                                else:
                                    zb = gi_ % 2
                                    zs4 = Zs[zb][:, 0:2 * n_el].rearrange("p (c r k) -> p c r k", c=g4, r=2)
                                    P.op("act", lambda e, zs4=zs4, zr3=zr3: e.activation(out=zs4[:, :, 0, :], in_=zr3, func=AF.Copy),
                                         reads=[("ps", b2r)], writes=[("Zs", zb)])
                                    P.op("act", lambda e, zs4=zs4, zi3=zi3: e.activation(out=zs4[:, :, 1, :], in_=zi3, func=AF.Copy),
                                         reads=[("ps", b2i)], writes=[("Zs", zb)])
                                    zss4 = ZsS[zb][:, 0:2 * n_el].rearrange("p (c r k) -> p c r k", c=g4, r=2)
                                    P.op("act", lambda e, zss4=zss4, zi3=zi3: e.activation(out=zss4[:, :, 0, :], in_=zi3, func=AF.Copy),
                                         reads=[("ps", b2i)], writes=[("ZsS", zb)])
                                    P.op("act", lambda e, zss4=zss4, zr3=zr3: e.activation(out=zss4[:, :, 1, :], in_=zr3, func=AF.Copy),
                                         reads=[("ps", b2r)], writes=[("ZsS", zb)])
                                    zsflat = ZsS[zb][:, 0:2 * n_el].rearrange("p (c x) -> p c x", c=g4)
                                    zflat = Zs[zb][:, 0:2 * n_el].rearrange("p (c x) -> p c x", c=g4)
                                    kfl = Ksp[:, cs:cs + g4, :, :].rearrange("p c r k -> p c (r k)")
                                    p1 = tm1[zb][:, 0:2 * n_el].rearrange("p (c x) -> p c x", c=g4) if 2 * n_el <= 512 else None
                                    pa = Zs[zb]
                                    P.op("dve", lambda e, zflat=zflat, kfl=kfl, zb=zb, n_el=n_el: e.tensor_tensor(
                                        out=PR1[zb][:, 0:2 * n_el].rearrange("p (c x) -> p c x", c=g4), in0=zflat, in1=kfl, op=ALU.mult),
                                        reads=[("Zs", zb), ("Ksp", par)], writes=[("PR1", zb)])
                                    P.op("dve", lambda e, zsflat=zsflat, kfl=kfl, zb=zb, n_el=n_el: e.tensor_tensor(
                                        out=PR2[zb][:, 0:2 * n_el].rearrange("p (c x) -> p c x", c=g4), in0=zsflat, in1=kfl, op=ALU.mult),
                                        reads=[("ZsS", zb), ("Ksp", par)], writes=[("PR2", zb)])
                                    q1 = PR1[zb][:, 0:2 * n_el].rearrange("p (c r k) -> p c r k", c=g4, r=2)
                                    q2 = PR2[zb][:, 0:2 * n_el].rearrange("p (c r k) -> p c r k", c=g4, r=2)
                                    P.op("dve", lambda e, q1=q1, cs=cs: e.tensor_tensor(
                                        out=Zf[:, cs:cs + g4, 0, :], in0=q1[:, :, 0, :], in1=q1[:, :, 1, :], op=ALU.subtract),
                                        reads=[("PR1", zb)], writes=[("Zf", gi_)])
                                    P.op("pool", lambda e, q2=q2, cs=cs: e.tensor_tensor(
                                        out=Zf[:, cs:cs + g4, 1, :], in0=q2[:, :, 0, :], in1=q2[:, :, 1, :], op=ALU.add),
                                        reads=[("PR2", zb)], writes=[("Zf", gi_)])

                        PR1 = [sb("PR1_%d" % i, (128, 1024), F32, st) for i in range(2)]
                        PR2 = [sb("PR2_%d" % i, (128, 1024), F32, st) for i in range(2)]

                        def inv_dft(epilogue):
                            zfk = [("Zf", gi_) for gi_ in range(CCc // g4)]
                            for q in range(CCc // 2):
                                bk = 4 + q % 2
                                for cl in range(2):
                                    c = 2 * q + cl
                                    P.op("pe", lambda e, c=c, cl=cl, bk=bk: e.matmul(
                                        psb[bk][0:K1, cl * 256:(cl + 1) * 256], Zf[:, c, 0, :], R3b[:, 0:256], start=True, stop=False),
                                        reads=zfk + ["R3"], writes=[("ps", bk)])
                                    P.op("pe", lambda e, c=c, cl=cl, bk=bk: e.matmul(
                                        psb[bk][0:K1, cl * 256:(cl + 1) * 256], Zf[:, c, 1, :], R3b[:, 256:512], start=False, stop=True),
                                        reads=zfk + ["R3"], writes=[("ps", bk)])
                                tb_ = q % 2
                                pin = psb[bk][0:K1, :].rearrange("p (c x) -> p c x", c=2)
                                cr2 = vw(TWC, 0, [[0, 2], [1, 256]], np_=K1)
                                ci2 = vw(TWC, 256, [[0, 2], [1, 256]], np_=K1)
                                o1_ = tm1[tb_][0:K1, :].rearrange("p (c x) -> p c x", c=2)
                                o2_ = tm2[tb_][0:K1, :].rearrange("p (c x) -> p c x", c=2)
                                P.op("dve", lambda e, pin=pin, cr2=cr2, o1_=o1_: e.tensor_tensor(out=o1_, in0=pin, in1=cr2, op=ALU.mult),
                                     reads=[("ps", bk), "TWC"], writes=[("tm1", tb_)])
                                P.op("dve", lambda e, pin=pin, ci2=ci2, o2_=o2_: e.tensor_tensor(out=o2_, in0=pin, in1=ci2, op=ALU.mult),
                                     reads=[("ps", bk), "TWC"], writes=[("tm2", tb_)])
                                a1 = tm1[tb_][0:K1, :].rearrange("p (c r k) -> p c r k", c=2, r=2)
                                a2 = tm2[tb_][0:K1, :].rearrange("p (c r k) -> p c r k", c=2, r=2)
                                P.op("pool", lambda e, a1=a1, a2=a2, q=q: e.tensor_tensor(
                                    out=Up[0:K1, 2 * q:2 * q + 2, 0, :], in0=a1[:, :, 0, :], in1=a2[:, :, 1, :], op=ALU.subtract),
                                    reads=[("tm1", tb_), ("tm2", tb_)], writes=[("Up", q // 2)])
                                P.op("pool", lambda e, a1=a1, a2=a2, q=q: e.tensor_tensor(
                                    out=Up[0:K1, 2 * q:2 * q + 2, 1, :], in0=a2[:, :, 0, :], in1=a1[:, :, 1, :], op=ALU.add),
                                    reads=[("tm1", tb_), ("tm2", tb_)], writes=[("Up", q // 2)])
                            for gq in range(CCc // 4):
                                bk = 6
                                P.op("pe", lambda e, gq=gq: e.matmul(psb[6][0:A, :], G4b[0:K1, 0:A], Up[0:K1, 4 * gq:4 * gq + 4, 0, :],
                                                                     start=True, stop=False),
                                     reads=[("Up", gq), "G4"], writes=[("ps", 6)])
                                P.op("pe", lambda e, gq=gq: e.matmul(psb[6][0:A, :], G4b[0:K1, A:2 * A], Up[0:K1, 4 * gq:4 * gq + 4, 1, :],
                                                                     start=False, stop=True),
                                     reads=[("Up", gq), "G4"], writes=[("ps", 6)])
                                epilogue(gq, psb[6][0:A, :].rearrange("p (c b) -> p c b", c=4))

                        steps = [(ci, o_) for ci in range(NCH) for o_ in range(2)]

                        def emit_filter(k):
                            ci, o_ = steps[k]
                            par = k % 2
                            c0 = ci * CCc
                            rn = rn2[par]
                            if o_ == 0:
                                for d_ in range(2):
                                    P.dma("sp", adec[d_ * A:(d_ + 1) * A], bass.AP(sW["decl"].tensor, d_ * 2 * CH + c0, [[0, A], [CH, 2], [1, CCc]]),
                                          writes=["adec"])
                                P.op("act", lambda e: e.activation(out=adec[0:K1], in_=adec[0:K1], func=AF.Abs),
                                     reads=["adec"], writes=["adec"])
                            for bg in range(8):
                                bk = (0, 7)[bg % 2]
                                for bl in range(16):
                                    b_ = bg * 16 + bl
                                    P.op("pe", lambda e, b_=b_, bl=bl, bk=bk, o_=o_, c0=c0: e.matmul(
                                        psb[bk][0:K1, bl * CCc:(bl + 1) * CCc], h2s[:, b_, :], w3s[:, o_, c0:c0 + CCc], start=True, stop=True),
                                        reads=["h2s", "w3s"], writes=[("ps", bk)])
                                P.op("act", lambda e, bg=bg, bk=bk: e.activation(
                                    out=kf[0:K1, :, bg * 16:(bg + 1) * 16],
                                    in_=psb[bk][0:K1, 0:16 * CCc].rearrange("p (b c) -> p c b", b=16), func=AF.Copy),
                                    reads=[("ps", bk)], writes=["kf"])
                            t01b = vw(t01, 0, [[0, CCc], [1, 128]], np_=K1)
                            adb = vw(adec, o_ * CCc, [[1, CCc], [0, 128]], np_=K1)
                            P.op("dve", lambda e, t01b=t01b, adb=adb: e.tensor_tensor(out=Ee[0:K1], in0=t01b, in1=adb, op=ALU.mult),
                                 reads=["t01", "adec"], writes=["Ee"])
                            P.op("act", lambda e: e.activation(out=Ee[0:K1], in_=Ee[0:K1], func=AF.Exp, scale=-1.0),
                                 reads=["Ee"], writes=["Ee"])
                            P.op("dve", lambda e: e.tensor_tensor(out=kf[0:K1], in0=kf[0:K1], in1=Ee[0:K1], op=ALU.mult),
                                 reads=["kf", "Ee"], writes=["kf"])
                            P.op("dve", lambda e: e.tensor_reduce(out=ksum[0:K1], in_=kf[0:K1], axis=AX.X, op=ALU.add,
                                                                  apply_absolute_value=True), reads=["kf"], writes=["ksum"])
                            P.op("dve", lambda e: e.tensor_copy(out=ksumb[0:K1], in_=ksum[0:K1]), reads=["ksum"], writes=["ksumb"])
                            P.op("act", lambda e: e.activation(out=kfb[0:K1], in_=kf[0:K1], func=AF.Copy), reads=["kf"], writes=["kfb"])
                            P.op("pe", lambda e: e.matmul(psb[7][:, 0:CCc], ones_b[0:K1, :], ksumb[0:K1, :], start=True, stop=True),
                                 reads=["ksumb", "ones_b"], writes=[("ps", 7)])
                            P.op("dve", lambda e, rn=rn: e.reciprocal(out=rn[:], in_=psb[7][:, 0:CCc]), reads=[("ps", 7)], writes=[("rn", par)])
                            fwd_dft(kfb, K1, True, "kfb", par)

                        def emit_data(k):
                            ci, o_ = steps[k]
                            par = k % 2
                            c0 = ci * CCc
                            pb = ci % 2
                            hvb, hx1b, hx2b = strm[0][pb], strm[1][pb], strm[2][pb]
                            if o_ == 0:
                                P.dma("sp", hbs[0:A], bass.AP(sW["hbl"].tensor, c0, [[0, A], [CH, 2], [1, CCc]]), writes=["hbs"])
                                for s_ in range(3):
                                    P.dma("sp", strm[s_][pb][0:A], bass.AP(sPc.tensor, (s_ * CH + c0) * T, [[128, A], [T, CCc], [1, 128]]),
                                          reads=["Pc"], writes=[("strm", s_, pb)])
                            hb_bc = vw(hbs, o_ * CCc, [[1, CCc], [0, 128]], np_=A)
                            if o_ == 0:
                                P.op("pool", lambda e, hvb=hvb, hb_bc=hb_bc: e.tensor_tensor(out=dsk[0:A], in0=hvb[0:A], in1=hb_bc, op=ALU.mult),
                                     reads=[("strm", 0, pb), "hbs"], writes=["dsk"])
                                fwd_dft(hvb, A, False, ("strm", 0, pb), par)
                            else:
                                P.op("act", lambda e: e.activation(out=inb[0:A], in_=zf_[0:A], func=AF.Copy),
                                     reads=["zf"], writes=["inb"])
                                P.op("pool", lambda e, hb_bc=hb_bc: e.tensor_tensor(out=dsk[0:A], in0=zf_[0:A], in1=hb_bc, op=ALU.mult),
                                     reads=["zf", "hbs"], writes=["dsk"])
                                fwd_dft(inb, A, False, "inb", par)

                            def epi(gq, yps, o_=o_, hx1b=hx1b, hx2b=hx2b, pb=pb):
                                cs = 4 * gq
                                P.op("dve", lambda e: e.tensor_tensor(out=ctmp[0:A, 0:4, :], in0=yps, in1=dsk[0:A, cs:cs + 4, :], op=ALU.add),
                                     reads=[("ps", 6), "dsk"], writes=["ctmp"])
                                if o_ == 0:
                                    P.op("pool", lambda e: e.tensor_tensor(out=zf_[0:A, cs:cs + 4, :], in0=ctmp[0:A, 0:4, :],
                                                                           in1=hx1b[0:A, cs:cs + 4, :], op=ALU.mult),
                                         reads=["ctmp", ("strm", 1, pb)], writes=["zf"])
                                else:
                                    P.op("pool", lambda e: e.tensor_tensor(out=oob[0:A, cs:cs + 4, :], in0=ctmp[0:A, 0:4, :],
                                                                           in1=hx2b[0:A, cs:cs + 4, :], op=ALU.mult),
                                         reads=["ctmp", ("strm", 2, pb)], writes=["oob"])
                            inv_dft(epi)
                            if o_ == 1:
                                P.dma("pool", bass.AP(sHOsend.tensor, c0 * T, [[128, A], [T, CCc], [1, 128]]), oob[0:A],
                                      reads=["oob"], writes=["HOs"])

                        P.replay(P.record(lambda: emit_filter(0)))
                        for k in range(len(steps)):
                            recD = P.record(lambda: emit_data(k))
                            recF = P.record(lambda: emit_filter(k + 1)) if k + 1 < len(steps) else []
                            P.replay(recD, recF)
                        P.barrier()


            phase_C()
            if G > 1:
                all_gather(sHOsend, sHOg, groups)
            def phase_D():
                if "D" in phases:
                    with contextlib.ExitStack() as st:
                        xT = sb("xT", (128, KC, TB), F32, st)
                        hT = sb("hT", (128, KC, TB), BF16, st)
                        actT = sb("actT", (128, FC, TB), BF16, st)
                        sq = [sb("sq%d" % i, (128, TB), F32, st) for i in range(2)]
                        sqb = [sb("sqb%d" % i, (128, TB), BF16, st) for i in range(2)]
                        rstd = sb("rstd", (128, TB), F32, st)
                        silu_t = [sb("silu%d" % i, (128, TB), F32, st) for i in range(2)]
                        w13buf = [sb("w13buf%d" % i, (128, KC, 2, 256), BF16, st) for i in range(2)]
                        w2buf = [sb("w2buf%d" % i, (128, FC, 256), BF16, st) for i in range(2)]
                        mix = sb("mixin", (128, KC, TB), BF16, st)
                        ytok = [sb("ytok%d" % i, (128, D // 2), F32, st) for i in range(2)]
                        hoidx = sb("hoidx_sb", (128, 8 * NBl), mybir.dt.int32, st)
                        P.dma("sp", hoidx[:], sW["hoidx"], writes=["hoidx"])
                        for blk in range(NBl):
                            t0 = blk * TB
                            P.dma("sp", xT[:], sX1[:, t0:t0 + TB].rearrange("(k p) t -> p k t", p=128), reads=["X1"],
                                  writes=[("xT", kc) for kc in range(KC)])
                            P.dma("sp", mix[:, 0:8, :], sAOs[:, t0:t0 + TB].rearrange("(k p) t -> p k t", p=128), reads=["AOs"],
                                  writes=[("mix", kc) for kc in range(8)])
                            for k8 in range(8):
                                P.dma_fn("pool", lambda e, k8=k8, blk=blk: e.indirect_dma_start(
                                    out=mix[:, 8 + k8, :], out_offset=None, in_=sHOg.rearrange("c (q j) -> (c q) j", j=TB),
                                    in_offset=bass.IndirectOffsetOnAxis(ap=hoidx[:, k8 * NBl + blk:k8 * NBl + blk + 1], axis=0)),
                                    reads=["HOs", "hoidx"], writes=[("mix", 8 + k8)])
                            for grp in range(2):
                                srcv = mix[:, grp * 8:(grp + 1) * 8, :]
                                dstv = hT[:, grp * 8:(grp + 1) * 8, :]
                                pss = psb[6]
                                for c in range(8):
                                    cc_ = grp * 8 + c
                                    P.op("act", lambda e, c=c, cc_=cc_: e.activation(out=sqb[c % 2][:], in_=mix[:, cc_, :], func=AF.Square),
                                         reads=[("mix", cc_)], writes=[("sqb", c % 2)])
                                    P.op("pe", lambda e, c=c: e.matmul(pss[:], ones_b[:], sqb[c % 2][:], start=(c == 0), stop=(c == 7)),
                                         reads=[("sqb", c % 2), "ones_b"], writes=[("ps", 6)])
                                P.op("act", lambda e: e.activation(out=rstd[:], in_=pss[:], func=AF.Sqrt, scale=1.0 / 1024.0, bias=eps_t[:]),
                                     reads=[("ps", 6), "eps_t"], writes=["rstd"])
                                P.op("dve", lambda e: e.reciprocal(out=rstd[:], in_=rstd[:]), reads=["rstd"], writes=["rstd"])
                                for c in range(8):
                                    cc_ = grp * 8 + c
                                    P.op("dve", lambda e, cc_=cc_: e.scalar_tensor_tensor(
                                        out=hT[:, cc_, :], in0=mix[:, cc_, :], scalar=gains[:, 2, cc_:cc_ + 1], in1=rstd[:],
                                        op0=ALU.mult, op1=ALU.mult),
                                        reads=[("mix", cc_), "rstd", ("gains", 2)], writes=[("hT", cc_)])
                            for pn in range(4):
                                bi = pn % 2
                                wb = w13buf[bi][:].rearrange("p k g c -> p k (g c)")
                                P.dma("sp", w13buf[bi][:].rearrange("p k g c -> p (k g c)"), wouts[pn].rearrange("p k c -> p (k c)"),
                                      reads=["wscratch"], writes=[("w13buf", bi)])
                                for sub in range(4):
                                    kc_out = pn * 4 + sub
                                    pq = psb[sub % 2]
                                    for kc in range(KC):
                                        P.op("pe", lambda e, kc=kc, sub=sub, pq=pq, wb=wb: e.matmul(
                                            pq[:], wb[:, kc, sub * 128:(sub + 1) * 128], hT[:, kc, :],
                                            start=(kc == 0), stop=(kc == KC - 1)),
                                            reads=[("w13buf", bi), ("hT", kc)], writes=[("ps", sub % 2)])
                                    P.op("dve", lambda e, pq=pq, kc_out=kc_out: e.tensor_tensor(
                                        out=xT[:, kc_out, :], in0=pq[:], in1=xT[:, kc_out, :], op=ALU.add),
                                        reads=[("ps", sub % 2), ("xT", kc_out)], writes=[("xT", kc_out)])
                            ffn(xT, hT, actT, sqb, rstd, 1, 3, (w13buf, w2buf, silu_t), "f2")
                            pss = psb[6]
                            for c in range(KC):
                                P.op("act", lambda e, c=c: e.activation(out=sqb[c % 2][:], in_=xT[:, c, :], func=AF.Square),
                                     reads=[("xT", c)], writes=[("sqb", c % 2)])
                                P.op("pe", lambda e, c=c: e.matmul(pss[:], ones_b[:], sqb[c % 2][:], start=(c == 0), stop=(c == KC - 1)),
                                     reads=[("sqb", c % 2), "ones_b"], writes=[("ps", 6)])
                            P.op("act", lambda e: e.activation(out=rstd[:], in_=pss[:], func=AF.Sqrt, scale=1.0 / D, bias=eps_t[:]),
                                 reads=[("ps", 6), "eps_t"], writes=["rstd"])
                            P.op("dve", lambda e: e.reciprocal(out=rstd[:], in_=rstd[:]), reads=["rstd"], writes=["rstd"])
                            for c in range(KC):
                                P.op("dve", lambda e, c=c: e.scalar_tensor_tensor(
                                    out=xT[:, c, :], in0=xT[:, c, :], scalar=gains[:, 4, c:c + 1], in1=rstd[:],
                                    op0=ALU.mult, op1=ALU.mult),
                                    reads=[("xT", c), "rstd", ("gains", 4)], writes=[("xT", c)])
                            for tt8 in range(8):
                                tt, hx = tt8 // 2, tt8 % 2
                                bi = tt8 % 2
                                for kc in range(hx * 8, hx * 8 + 8):
                                    pst = psb[kc % 4]
                                    P.op("pe", lambda e, kc=kc, tt=tt, pst=pst: e.transpose(
                                        pst[:, 0:128], xT[:, kc, tt * 128:(tt + 1) * 128], ident[:]),
                                        reads=[("xT", kc), "ident"], writes=[("ps", kc % 4)])
                                    if kc % 2 == 0:
                                        P.op("act", lambda e, kc=kc, bi=bi, pst=pst: e.activation(
                                            out=ytok[bi][:, (kc % 8) * 128:(kc % 8 + 1) * 128], in_=pst[:, 0:128], func=AF.Copy),
                                            reads=[("ps", kc % 4)], writes=[("ytok", bi)])
                                    else:
                                        P.op("dve", lambda e, kc=kc, bi=bi, pst=pst: e.tensor_copy(
                                            out=ytok[bi][:, (kc % 8) * 128:(kc % 8 + 1) * 128], in_=pst[:, 0:128]),
                                            reads=[("ps", kc % 4)], writes=[("ytok", bi)])
                                P.dma("act", sy_out[t0 + tt * 128:t0 + (tt + 1) * 128, hx * 1024:(hx + 1) * 1024], ytok[bi][:],
                                      reads=[("ytok", bi)], writes=["y"])
                        P.barrier()


            phase_D()
        for si_, (T_, G_) in enumerate(slots):
            emit_slot(si_, T_, G_)
        P.barrier()
        counts = P.finalize(top)
    return nc, counts


CC_MAX_OUT_BYTES = 4 * 1024 * 1024


def piece_rows(R, C, G, elem=2):
    rp = R
    while G * rp * C * elem > CC_MAX_OUT_BYTES or R % rp:
        rp -= 1
    return rp


def gathered_row(row, r, R, C, G):
    rp = piece_rows(R, C, G)
    return (row // rp) * G * rp + r * rp + (row % rp)


COMMON_WEIGHTS = ("ffn1_norm", "ffn1_w13", "ffn1_w2", "mix_norm", "w_in", "q_norm", "k_norm",
                  "filt_w1", "filt_b1", "filt_w2", "filt_b2", "filt_freq",
                  "group_out_norm", "w_out", "ffn2_norm", "ffn2_w13", "ffn2_w2", "final_norm")
SLOT_TABLES = ("ropeC", "ropeS", "kbias", "featsT", "t01tab", "masktab", "F1K", "twr", "twi", "twcr", "twci",
               "F2", "R3", "G4")


def slot_table_shapes(T, G):
    A = T // 128
    K1 = 2 * A
    Tl = T // G
    return {"ropeC": (128, Tl), "ropeS": (128, Tl), "kbias": (128, T // 128), "featsT": (33, 2 * T),
            "t01tab": (K1, 128), "masktab": (K1, 128), "F1K": (K1, 2 * K1), "twr": (128, K1), "twi": (128, K1),
            "twcr": (K1, 128), "twci": (K1, 128), "F2": (128, 384), "R3": (128, 512), "G4": (K1, 2 * A)}


_TABLE_CACHE = {}


def slot_inputs(si, T, G, rank, x_local, w):
    Tl = T // G
    CH = DH // G
    NBl = Tl // TB
    sfx = "_s%d" % si
    if T not in _TABLE_CACHE:
        _TABLE_CACHE[T] = make_tables(T, T)
    tb = _TABLE_CACHE[T]
    m = {"x" + sfx: np.ascontiguousarray(x_local, dtype=np.float32)}
    for k in SLOT_TABLES:
        a = tb[k]
        if k in ("ropeC", "ropeS"):
            a = np.ascontiguousarray(a[:, rank * Tl:(rank + 1) * Tl])
        m[k + sfx] = a
    c0 = rank * CH
    cw = np.asarray(w["conv_w"], np.float32).reshape(3, 3, 1024)[:, :, c0:c0 + CH]
    m["cwl" + sfx] = np.ascontiguousarray(cw.reshape(3, 3 * CH))
    cb = np.asarray(w["conv_b"], np.float32).reshape(3, 1024)[:, c0:c0 + CH]
    m["cbl" + sfx] = np.ascontiguousarray(cb.reshape(1, 3 * CH))
    m["hbl" + sfx] = np.ascontiguousarray(np.asarray(w["hyena_bias"], np.float32).reshape(2, 1024)[:, c0:c0 + CH])
    m["decl" + sfx] = np.ascontiguousarray(
        np.asarray(w["hyena_decay"], np.float32).reshape(4, 1024)[:, c0:c0 + CH])
    w3 = np.asarray(w["filt_w3"], np.float32).reshape(64, 4, 1024)[:, :, c0:c0 + CH]
    m["w3l" + sfx] = np.ascontiguousarray(w3.reshape(64, 4 * CH))
    nsg = 3 * CH // 128
    gps = CH // 128
    p = np.arange(128)
    cidx = np.zeros((128, nsg * G), np.int32)
    for sg in range(nsg):
        s_, grp = sg // gps, sg % gps
        for r in range(G):
            rows = s_ * 1024 + c0 + grp * 128 + p
            cidx[:, sg * G + r] = gathered_row(rows, r, 3072, Tl, G)
    m["cidx" + sfx] = cidx
    hoidx = np.zeros((128, 8 * NBl), np.int32)
    for k8 in range(8):
        for blk in range(NBl):
            ch = k8 * 128 + p
            grow = gathered_row(ch % CH, ch // CH, CH, T, G)
            hoidx[:, k8 * NBl + blk] = grow * (T // TB) + rank * NBl + blk
    m["hoidx" + sfx] = hoidx
    return m


def common_inputs(w):
    m = {}
    for k in COMMON_WEIGHTS:
        shp = WEIGHT_SHAPES[k]
        m[k] = np.ascontiguousarray(np.asarray(w[k], dtype=np.float32).reshape(shp if len(shp) > 1 else (1, shp[0])))
    m["ident"] = np.eye(128, dtype=np.float32)
    m["ones"] = np.ones((128, 128), np.float32)
    m["Pm"] = make_tables(512, 512)["Pm"]
    return m


SLOTS = [(8192, 2), (4096, 4)]
_PROG_CACHE = {}


def kernel(**inputs):
    x_prompt = np.asarray(inputs["x_prompt"], dtype=np.float32)
    x_sample = np.asarray(inputs["x_sample"], dtype=np.float32)
    (TL, GL), (TS, GS) = SLOTS
    assert x_sample.shape == (4, TL, D) and x_prompt.shape == (2, TS, D)
    if "prog" not in _PROG_CACHE:
        _PROG_CACHE["prog"] = build_program(SLOTS)
    nc, _ = _PROG_CACHE["prog"]
    cm = common_inputs(inputs)
    in_maps = []
    for core in range(8):
        m = dict(cm)
        sq, rk = core // GL, core % GL
        tl = TL // GL
        m.update(slot_inputs(0, TL, GL, rk, x_sample[sq, rk * tl:(rk + 1) * tl], inputs))
        sq, rk = core // GS, core % GS
        tl = TS // GS
        m.update(slot_inputs(1, TS, GS, rk, x_prompt[sq, rk * tl:(rk + 1) * tl], inputs))
        in_maps.append(m)
    res = run_bass_kernel_spmd(nc, in_maps, core_ids=list(range(8)))
    y_sample = np.zeros((4, TL, D), np.float32)
    y_prompt = np.zeros((2, TS, D), np.float32)
    for core in range(8):
        sq, rk = core // GL, core % GL
        tl = TL // GL
        y_sample[sq, rk * tl:(rk + 1) * tl] = np.asarray(res.results[core]["y_s0"], dtype=np.float32)
        sq, rk = core // GS, core % GS
        tl = TS // GS
        y_prompt[sq, rk * tl:(rk + 1) * tl] = np.asarray(res.results[core]["y_s1"], dtype=np.float32)
    return (y_prompt, y_sample)
```

```python
import contextlib
import math
import numpy as np
import ml_dtypes
import concourse.bass as bass
import concourse.mybir as mybir
from concourse.bass_utils import run_bass_kernel_spmd

F32 = mybir.dt.float32
BF16 = mybir.dt.bfloat16
AF = mybir.ActivationFunctionType
ALU = mybir.AluOpType
AX = mybir.AxisListType

D = 2048
DFF = 5632
KC = D // 128
FC = DFF // 128
TB = 512
DA = 1024
DH = 1024
NQH = 8
NKV = 2
EPS = 1e-6
CC = 32
GRID_W = 64

ENGS = ("pe", "act", "dve", "pool", "sp")
N_DMA_SEMS = 24


class Prog:
    def __init__(self, nc):
        self.nc = nc
        self.ops = []
        self.last_write = {}
        self.readers = {}
        self.dma_sem_last = [None] * N_DMA_SEMS
        self.dma_sem_count = [0] * N_DMA_SEMS
        self.dma_rr = 0

    def _add(self, eng, fn, reads, writes, is_dma):
        idx = len(self.ops)
        deps = set()
        for k in reads:
            lw = self.last_write.get(k)
            if lw is not None:
                deps.add(lw)
        for k in writes:
            lw = self.last_write.get(k)
            if lw is not None:
                deps.add(lw)
            for r in self.readers.get(k, ()):
                deps.add(r)
        op = dict(idx=idx, eng=eng, fn=fn, is_dma=is_dma, deps=deps, ms=False)
        if is_dma:
            s = self.dma_rr
            self.dma_rr = (self.dma_rr + 1) % N_DMA_SEMS
            prev = self.dma_sem_last[s]
            if prev is not None:
                deps.add(prev)
            self.dma_sem_count[s] += 1
            op["dsem"] = s
            op["dval"] = 16 * self.dma_sem_count[s]
            self.dma_sem_last[s] = idx
        deps.discard(idx)
        self.ops.append(op)
        for k in writes:
            self.last_write[k] = idx
            self.readers[k] = []
        for k in reads:
            if k in writes:
                continue
            lst = self.readers.setdefault(k, [])
            if not is_dma:
                lst[:] = [r for r in lst if self.ops[r]["is_dma"] or self.ops[r]["eng"] != eng]
            lst.append(idx)
        return idx

    rec = None

    def op(self, eng, fn, reads=(), writes=()):
        if self.rec is not None:
            self.rec.append((eng, fn, tuple(reads), tuple(writes), False))
            return None
        return self._add(eng, fn, tuple(reads), tuple(writes), False)

    def dma(self, eng, out, in_, reads=(), writes=(), **kw):
        def fn(e):
            return e.dma_start(out=out, in_=in_, **kw)
        if self.rec is not None:
            self.rec.append((eng, fn, tuple(reads), tuple(writes), True))
            return None
        return self._add(eng, fn, tuple(reads), tuple(writes), True)

    def dma_fn(self, eng, fn, reads=(), writes=()):
        if self.rec is not None:
            self.rec.append((eng, fn, tuple(reads), tuple(writes), True))
            return None
        return self._add(eng, fn, tuple(reads), tuple(writes), True)

    def record(self, f):
        assert self.rec is None
        self.rec = []
        try:
            f()
        finally:
            lst, self.rec = self.rec, None
        return lst

    def replay(self, *lists):
        lists = [l for l in lists if l]
        pos = [0] * len(lists)
        total = sum(len(l) for l in lists)
        for _ in range(total):
            j = min((k for k in range(len(lists)) if pos[k] < len(lists[k])),
                    key=lambda k: (pos[k] + 0.5) / len(lists[k]))
            self._add(*lists[j][pos[j]])
            pos[j] += 1

    def raw(self, eng, fn):
        idx = len(self.ops)
        self.ops.append(dict(idx=idx, eng=eng, fn=fn, is_dma=False, deps=set(), ms=False, raw=True))
        return idx

    def barrier(self):
        last = {}
        for o in self.ops:
            if (not o["is_dma"]) and o["fn"] is not None and not o.get("raw"):
                last[o["eng"]] = o["idx"]
        outstanding = [i for i in self.dma_sem_last if i is not None]
        for e in ENGS:
            idx = len(self.ops)
            deps = set(last.values()) | set(outstanding)
            self.ops.append(dict(idx=idx, eng=e, fn=None, is_dma=False, deps=deps, ms=False))
        self.last_write = {}
        self.readers = {}

    def finalize(self, stack):
        nc = self.nc
        ops = self.ops
        for o in ops:
            for d in o["deps"]:
                t = ops[d]
                if not t["is_dma"]:
                    if t["eng"] == "pe" and o["eng"] == "pe" and not o["is_dma"]:
                        continue
                    t["ms"] = True
        cnt = {e: 0 for e in ENGS}
        for o in ops:
            if o["ms"]:
                cnt[o["eng"]] += 1
                o["msval"] = cnt[o["eng"]]
        known = {e: {f: -1 for f in ENGS} for e in ENGS}
        known_dma = {e: set() for e in ENGS}
        for o in ops:
            e = o["eng"]
            need = {}
            dwaits = []
            for d in sorted(o["deps"]):
                t = ops[d]
                if t["is_dma"]:
                    if d not in known_dma[e]:
                        known_dma[e].add(d)
                        dwaits.append((("d", t["dsem"]), t["dval"]))
                else:
                    f = t["eng"]
                    if f == "pe" and e == "pe" and not o["is_dma"]:
                        continue
                    if t["fn"] is None:
                        continue
                    if d > known[e][f]:
                        need[f] = max(need.get(f, -1), d)
            waits = list(dwaits)
            for f, d in need.items():
                known[e][f] = d
                waits.append((("e", f), ops[d]["msval"]))
            o["waits"] = waits
        esem = {e: stack.enter_context(nc.semaphore("sem_" + e)) for e in ENGS}
        dsem = [stack.enter_context(nc.semaphore("dsem%d" % i)) for i in range(N_DMA_SEMS)]

        def semof(key):
            return esem[key[1]] if key[0] == "e" else dsem[key[1]]

        ccsem = stack.enter_context(nc.semaphore("ccsem"))
        per = {e: [o for o in ops if o["eng"] == e] for e in ENGS}
        stack.enter_context(nc.allow_non_contiguous_dma(reason="small strided tables / halo columns"))
        block = stack.enter_context(nc.Block())

        def run(engobj, lst, ename):
            for o in lst:
                for key, val in o["waits"]:
                    engobj.wait_ge(semof(key), val)
                if o["fn"] is None:
                    continue
                if o.get("raw"):
                    o["fn"](engobj, ccsem)
                    continue
                ins = o["fn"](engobj)
                if o["is_dma"]:
                    ins.then_inc(dsem[o["dsem"]], 16)
                elif o["ms"]:
                    ins.then_inc(esem[ename], 1)

        @block.tensor
        def _(e):
            run(e, per["pe"], "pe")

        @block.scalar
        def _(e):
            run(e, per["act"], "act")

        @block.vector
        def _(e):
            run(e, per["dve"], "dve")

        @block.gpsimd
        def _(e):
            run(e, per["pool"], "pool")

        @block.sync
        def _(e):
            run(e, per["sp"], "sp")

        return {e: len(per[e]) for e in ENGS}


def sb_ap(t, off, dims):
    return bass.AP(t, off, [list(d) for d in dims])


def make_tables(T, L):
    A = T // 128
    K1 = 2 * A
    N = 2 * T
    tb = {}
    tb["ident"] = np.eye(128, dtype=np.float32)
    tb["ones"] = np.ones((128, 128), np.float32)
    half = 32
    inv = (10000.0 ** (-np.arange(0, 64, 2, dtype=np.float32) / 64.0)).astype(np.float32)
    t = np.arange(T)
    row = (t // GRID_W).astype(np.float32)
    col = (t % GRID_W).astype(np.float32)
    ang_r = row[:, None] * inv[None]
    ang_c = col[:, None] * inv[None]
    C = np.zeros((128, T), np.float32)
    S = np.zeros((128, T), np.float32)
    for d in range(128):
        ang = ang_r if d < 64 else ang_c
        j = d % 32
        C[d] = np.cos(ang[:, j])
        first = (d % 64) < 32
        S[d] = (-1.0 if first else 1.0) * np.sin(ang[:, j])
    tb["ropeC"] = C
    tb["ropeS"] = S
    Pm = np.zeros((128, 128), np.float32)
    for d in range(128):
        partner = d + 32 if (d % 64) < 32 else d - 32
        Pm[partner, d] = 1.0
    tb["Pm"] = Pm
    kb = np.zeros((T,), np.float32)
    kb[L:] = -30000.0
    tb["kbias"] = np.ascontiguousarray(kb.reshape(T // 128, 128).T)
    bands = 16
    n = np.arange(N)
    pos = np.where(n < T, n, N - n).astype(np.int64)
    valid = ((n < L) | (n > N - L)).astype(np.float32)
    valid[T] = 0.0
    posc = np.minimum(pos, L - 1)
    t01 = (np.linspace(0.0, 1.0, L, dtype=np.float32))[posc]
    fr = np.linspace(1e-4, bands - 1, bands, dtype=np.float32)[None]
    w = (2.0 * math.pi * np.arange(L, dtype=np.float32)[:, None] / L).astype(np.float32)
    feats_L = np.concatenate([np.linspace(0.0, 1.0, L, dtype=np.float32)[:, None],
                              np.cos(fr * w), -np.sin(fr * w)], axis=-1).astype(np.float32)
    feats = feats_L[posc]
    tb["featsT"] = np.ascontiguousarray(feats.T)
    tb["t01tab"] = np.ascontiguousarray(t01.reshape(K1, 128)).astype(np.float32)
    tb["masktab"] = np.ascontiguousarray(valid.reshape(K1, 128)).astype(np.float32)
    a = np.arange(K1)[:, None]
    k1 = np.arange(K1)[None]
    ph = -2.0 * np.pi * (a * k1 % K1) / K1
    tb["F1K"] = np.concatenate([np.cos(ph), np.sin(ph)], axis=1).astype(np.float32)
    b = np.arange(128)[:, None]
    ph = -2.0 * np.pi * (b * k1) / N
    tb["twr"] = np.cos(ph).astype(np.float32)
    tb["twi"] = np.sin(ph).astype(np.float32)
    tb["twcr"] = np.ascontiguousarray(np.cos(ph).T).astype(np.float32)
    tb["twci"] = np.ascontiguousarray((-np.sin(ph)).T).astype(np.float32)
    k2 = np.arange(128)[None]
    ph = -2.0 * np.pi * ((b * k2) % 128) / 128
    F2r = np.cos(ph)
    F2i = np.sin(ph)
    tb["F2"] = np.concatenate([F2r, F2i, -F2i], axis=1).astype(np.float32)
    tb["R3"] = np.concatenate([F2r, -F2i, F2i, F2r], axis=1).astype(np.float32)
    aa = np.arange(A)[None]
    kk = np.arange(K1)[:, None]
    ph = 2.0 * np.pi * ((aa * kk) % K1) / K1
    tb["G4"] = np.concatenate([np.cos(ph) / N, -np.sin(ph) / N], axis=1).astype(np.float32)
    return tb


TABLE_NAMES = ["ident", "ones", "ropeC", "ropeS", "Pm", "kbias", "featsT", "t01tab", "masktab",
               "F1K", "twr", "twi", "twcr", "twci", "F2", "R3", "G4"]

WEIGHT_SHAPES = {
    "ffn1_norm": (D,), "ffn1_w13": (D, 2 * DFF), "ffn1_w2": (DFF, D), "mix_norm": (D,),
    "w_in": (D, 4608), "q_norm": (128,), "k_norm": (128,), "conv_w": (3, 3072), "conv_b": (3072,),
    "filt_w1": (33, 64), "filt_b1": (64,), "filt_w2": (64, 64), "filt_b2": (64,),
    "filt_w3": (64, 4096), "filt_freq": (64,), "hyena_decay": (2, 2, 1024), "hyena_bias": (2, 1024),
    "group_out_norm": (D,), "w_out": (D, D), "ffn2_norm": (D,), "ffn2_w13": (D, 2 * DFF),
    "ffn2_w2": (DFF, D), "final_norm": (D,),
}


def build_program(slots, debug_outputs=(), phases="0ABCD"):
    if isinstance(slots, int):
        slots = [(slots, 1)]
    nc = bass.Bass("TRN2", target_bir_lowering=False)
    P = Prog(nc)
    dr = {}

    def dram(name, shape, dt, kind):
        if name in debug_outputs and kind == "Internal":
            kind = "ExternalOutput"
        h = nc.dram_tensor(name, list(shape), dt, kind=kind)
        dr[name] = h.ap()
        return dr[name]

    def dbgtap(name, ap, shape, dt, reads):
        if ("dbg_" + name) not in debug_outputs or ("dbg_" + name) in dr:
            return
        h = nc.dram_tensor("dbg_" + name, list(shape), dt, kind="ExternalOutput")
        dr["dbg_" + name] = h.ap()
        P.dma("sp", h.ap(), ap, reads=reads, writes=["dbg_" + name])

    W = {k: dram(k, s if len(s) > 1 else (1, s[0]), F32, "ExternalInput") for k, s in WEIGHT_SHAPES.items()
         if k in COMMON_WEIGHTS}
    TBL = {k: dram(k, shp, F32, "ExternalInput") for k, shp in (("ident", (128, 128)), ("ones", (128, 128)), ("Pm", (128, 128)))}
    w13s = [dram("w13s%d" % i, (22, 128, KC, 2, 256), BF16, "Internal") for i in range(2)]
    w2s = [dram("w2s%d" % i, (8, 128, FC, 256), BF16, "Internal") for i in range(2)]
    wins = dram("wins", (9, 128, KC, 512), BF16, "Internal")
    wouts = dram("wouts", (4, 128, KC, 512), BF16, "Internal")

    with contextlib.ExitStack() as top:
        used_names = {}

        def sb(name, shape, dt, st=top):
            n = used_names.get(name, 0)
            used_names[name] = n + 1
            if n:
                name = "%s_r%d" % (name, n)
            return st.enter_context(nc.sbuf_tensor(name, list(shape), dt))

        ident = sb("ident_sb", (128, 128), F32)
        ones_f = sb("ones_f", (128, 128), F32)
        ones_b = sb("ones_b", (128, 128), BF16)
        gains = sb("gains", (128, 5, KC), F32)
        psb = [top.enter_context(nc.psum_tensor("psb%d" % i, [128, 512], F32)) for i in range(8)]

        P.dma("sp", ident[:], TBL["ident"], writes=["ident"])
        P.dma("sp", ones_f[:], TBL["ones"], writes=["ones_f"])
        P.op("dve", lambda e: e.tensor_copy(out=ones_b[:], in_=ones_f[:]), reads=["ones_f"], writes=["ones_b"])
        for gi, nm in enumerate(["ffn1_norm", "mix_norm", "group_out_norm", "ffn2_norm", "final_norm"]):
            P.dma("sp", gains[:, gi, :], W[nm].rearrange("o (k p) -> p (o k)", p=128), writes=[("gains", gi)],
                  allow_slow_non_contiguous=True)

        cast_now, cast_later = [], []

        def cast_items():
            for fi, (n13, n2) in enumerate([("ffn1_w13", "ffn1_w2"), ("ffn2_w13", "ffn2_w2")]):
                lst = cast_now if fi == 0 else cast_later
                w13 = W[n13]
                for r in range(KC):
                    for g in range(0, 22, 4):
                        npan = min(4, 22 - g)
                        wdt = npan * 256
                        srcs = [(w13[r * 128:(r + 1) * 128, g * 256:g * 256 + wdt], 0, wdt),
                                (w13[r * 128:(r + 1) * 128, DFF + g * 256:DFF + g * 256 + wdt], 1024, wdt)]
                        dsts = []
                        for gu in range(2):
                            dsts.append((w13s[fi][g:g + npan, :, r, gu, :].rearrange("n p c -> p n c"),
                                         gu * 1024, wdt, ("p (n c) -> p n c", dict(c=256))))
                        lst.append((srcs, dsts, 2048))
                w2 = W[n2]
                for r in range(FC):
                    srcs = [(w2[r * 128:(r + 1) * 128, :], 0, 2048)]
                    dsts = [(w2s[fi][:, :, r, :].rearrange("n p c -> p n c"), 0, 2048,
                             ("p (n c) -> p n c", dict(c=256)))]
                    lst.append((srcs, dsts, 2048))
            for r in range(KC):
                for g in range(0, 9, 4):
                    npan = min(4, 9 - g)
                    wdt = npan * 512
                    srcs = [(W["w_in"][r * 128:(r + 1) * 128, g * 512:g * 512 + wdt], 0, wdt)]
                    dsts = [(wins[g:g + npan, :, r, :].rearrange("n p c -> p n c"), 0, wdt,
                             ("p (n c) -> p n c", dict(c=512)))]
                    cast_now.append((srcs, dsts, wdt))
            for r in range(KC):
                srcs = [(W["w_out"][r * 128:(r + 1) * 128, :], 0, 2048)]
                dsts = [(wouts[:, :, r, :].rearrange("n p c -> p n c"), 0, 2048,
                         ("p (n c) -> p n c", dict(c=512)))]
                cast_later.append((srcs, dsts, 2048))
        cast_items()
        cast_cnt = [0]

        def cast_block(item, stg, stb, engs=("dve", "act"), ldq="sp", stq="pool"):
            src_aps, dst_aps, ncols = item
            i = cast_cnt[0] % len(stg)
            cast_cnt[0] += 1
            for ap, c0, n in src_aps:
                P.dma(ldq, stg[i][:, c0:c0 + n], ap, writes=[("stg", i)])
            eng = engs[cast_cnt[0] % len(engs)]
            if eng == "dve":
                P.op("dve", lambda e, i=i: e.tensor_copy(out=stb[i][:, :ncols], in_=stg[i][:, :ncols]),
                     reads=[("stg", i)], writes=[("stb", i)])
            else:
                P.op("act", lambda e, i=i: e.activation(out=stb[i][:, :ncols], in_=stg[i][:, :ncols], func=AF.Copy),
                     reads=[("stg", i)], writes=[("stb", i)])
            for ap, c0, n, pat in dst_aps:
                src = stb[i][:, c0:c0 + n]
                if pat is not None:
                    src = src.rearrange(pat[0], **pat[1])
                P.dma(stq, ap, src, reads=[("stb", i)], writes=["wscratch"])

        defer_casts = ("B" in phases)
        if "0" in phases:
            with contextlib.ExitStack() as st:
                stg = [sb("stg%d" % i, (128, 2048), F32, st) for i in range(4)]
                stb = [sb("stb%d" % i, (128, 2048), BF16, st) for i in range(4)]
                for item in cast_now:
                    cast_block(item, stg, stb, stq="act")
                if not defer_casts:
                    for item in cast_later:
                        cast_block(item, stg, stb, stq="act")
                    cast_later[:] = []
                P.barrier()
        else:
            cast_later[:] = []

        def ffn(xT, hT, actT, sq, rstd, fi, gi, wq, tag):
            norm_fm(xT, hT, sq, rstd, gi, 0, KC, float(D), tag)
            w13buf, w2buf, silu_t = wq
            for m2 in range(22):
                bi = m2 % 2
                P.dma("sp", w13buf[bi][:].rearrange("p k g c -> p (k g c)"),
                      w13s[fi][m2].rearrange("p k g c -> p (k g c)"),
                      reads=["wscratch"], writes=[("w13buf", bi)])
                for hh in range(2):
                    m = 2 * m2 + hh
                    pg, pu = psb[(m % 2) * 2], psb[(m % 2) * 2 + 1]
                    for kc in range(KC):
                        P.op("pe", lambda e, bi=bi, kc=kc, hh=hh, pg=pg: e.matmul(
                            pg[:], w13buf[bi][:, kc, 0, hh * 128:(hh + 1) * 128], hT[:, kc, :],
                            start=(kc == 0), stop=(kc == KC - 1)),
                            reads=[("w13buf", bi), ("hT", kc)], writes=[("ps", (m % 2) * 2)])
                    for kc in range(KC):
                        P.op("pe", lambda e, bi=bi, kc=kc, hh=hh, pu=pu: e.matmul(
                            pu[:], w13buf[bi][:, kc, 1, hh * 128:(hh + 1) * 128], hT[:, kc, :],
                            start=(kc == 0), stop=(kc == KC - 1)),
                            reads=[("w13buf", bi), ("hT", kc)], writes=[("ps", (m % 2) * 2 + 1)])
                    sg = silu_t[m % 2]
                    P.op("act", lambda e, pg=pg, sg=sg: e.activation(out=sg[:], in_=pg[:], func=AF.Silu),
                         reads=[("ps", (m % 2) * 2)], writes=[("silu", m % 2)])
                    P.op("dve", lambda e, pu=pu, sg=sg, m=m: e.tensor_tensor(
                        out=actT[:, m, :], in0=pu[:], in1=sg[:], op=ALU.mult),
                        reads=[("ps", (m % 2) * 2 + 1), ("silu", m % 2)], writes=[("actT", m)])
            for n2 in range(8):
                bi = n2 % 2
                P.dma("sp", w2buf[bi][:].rearrange("p k c -> p (k c)"), w2s[fi][n2].rearrange("p k c -> p (k c)"),
                      reads=["wscratch"], writes=[("w2buf", bi)])
                for hh in range(2):
                    kc_out = 2 * n2 + hh
                    py = psb[4 + (kc_out % 2)]
                    for m in range(FC):
                        P.op("pe", lambda e, bi=bi, m=m, hh=hh, py=py: e.matmul(
                            py[:], w2buf[bi][:, m, hh * 128:(hh + 1) * 128], actT[:, m, :],
                            start=(m == 0), stop=(m == FC - 1)),
                            reads=[("w2buf", bi), ("actT", m)], writes=[("ps", 4 + (kc_out % 2))])
                    P.op("dve", lambda e, py=py, kc_out=kc_out: e.scalar_tensor_tensor(
                        out=xT[:, kc_out, :], in0=py[:], scalar=0.5, in1=xT[:, kc_out, :],
                        op0=ALU.mult, op1=ALU.add),
                        reads=[("ps", 4 + (kc_out % 2)), ("xT", kc_out)], writes=[("xT", kc_out)])

        def norm_fm(src, dst, sq, rstd, gi, goff, nch, nfeat, tag, src_key="xT", dst_key="hT", out_scale=None):
            pss = psb[6]
            for c in range(nch):
                P.op("act", lambda e, c=c: e.activation(out=sq[c % 2][:], in_=src[:, c, :], func=AF.Square),
                     reads=[(src_key, c)], writes=[("sqb", c % 2)])
                P.op("pe", lambda e, c=c: e.matmul(pss[:], ones_b[:], sq[c % 2][:], start=(c == 0), stop=(c == nch - 1)),
                     reads=[("sqb", c % 2), "ones_b"], writes=[("ps", 6)])
            P.op("act", lambda e: e.activation(out=rstd[:], in_=pss[:], func=AF.Sqrt, scale=1.0 / nfeat, bias=eps_t[:]),
                 reads=[("ps", 6), "eps_t"], writes=["rstd"])
            P.op("dve", lambda e: e.reciprocal(out=rstd[:], in_=rstd[:]), reads=["rstd"], writes=["rstd"])
            for c in range(nch):
                P.op("dve", lambda e, c=c: e.scalar_tensor_tensor(
                    out=dst[:, c, :], in0=src[:, c, :], scalar=gains[:, gi, goff + c:goff + c + 1], in1=rstd[:],
                    op0=ALU.mult, op1=ALU.mult),
                    reads=[(src_key, c), "rstd", ("gains", gi)], writes=[(dst_key, c)])

        eps_t = sb("eps_t", (128, 1), F32)
        P.op("dve", lambda e: e.memset(eps_t[:], EPS), writes=["eps_t"])
        eps128_t = sb("eps128_t", (128, 1), F32)
        P.op("dve", lambda e: e.memset(eps128_t[:], EPS * 128.0), writes=["eps128_t"])


        ccdummy = sb("ccdummy", (128, 8), F32)
        cc_count = [0]

        def all_gather(src_ap, dst_ap, groups, defer=False):
            R_, C_ = src_ap.shape
            G_ = len(groups[0])
            rp = piece_rows(R_, C_, G_)
            for j in range(R_ // rp):
                all_gather_1(src_ap[j * rp:(j + 1) * rp, :], dst_ap[j * G_ * rp:(j + 1) * G_ * rp, :], groups, defer)

        def all_gather_1(src_ap, dst_ap, groups, defer):
            cc_count[0] += 1
            n_ = cc_count[0]
            if not defer:
                P.barrier()

            def fn(e, ccsem, src_ap=src_ap, dst_ap=dst_ap, n_=n_, groups=groups, defer=defer):
                e.collective_compute("AllGather", ALU.bypass, replica_groups=groups,
                                     ins=[src_ap.opt()], outs=[dst_ap.opt()]).then_inc(ccsem, 1)
                if not defer:
                    e.wait_ge(ccsem, n_)
            P.raw("pool", fn)
            if not defer:
                P.op("pool", lambda e: e.memset(ccdummy[:], 0.0), writes=["ccdummy"])
                P.barrier()

        def cc_wait_all():
            n_ = cc_count[0]
            P.barrier()
            P.raw("pool", lambda e, ccsem, n_=n_: e.wait_ge(ccsem, n_))
            P.op("pool", lambda e: e.memset(ccdummy[:], 0.0), writes=["ccdummy"])
            P.barrier()

        def emit_slot(si, T, G):
            Tl = T // G
            CH = DH // G
            A = T // 128
            K1 = 2 * A
            NT = T // 128
            NBl = Tl // TB
            N2 = 2 * T
            sfx = "_s%d" % si
            groups = [list(range(i, i + G)) for i in range(0, 8, G)]
            sx_in = dram("x" + sfx, (Tl, D), F32, "ExternalInput")
            sy_out = dram("y" + sfx, (Tl, D), F32, "ExternalOutput")
            STB = {k: dram(k + sfx, shp, F32, "ExternalInput") for k, shp in slot_table_shapes(T, G).items()}
            nsg = 3 * CH // 128
            sW = {
                "cwl": dram("cwl" + sfx, (3, 3 * CH), F32, "ExternalInput"),
                "cbl": dram("cbl" + sfx, (1, 3 * CH), F32, "ExternalInput"),
                "hbl": dram("hbl" + sfx, (2, CH), F32, "ExternalInput"),
                "decl": dram("decl" + sfx, (4, CH), F32, "ExternalInput"),
                "w3l": dram("w3l" + sfx, (64, 4 * CH), F32, "ExternalInput"),
                "cidx": dram("cidx" + sfx, (128, nsg * G), mybir.dt.int32, "ExternalInput"),
                "hoidx": dram("hoidx" + sfx, (128, 8 * NBl), mybir.dt.int32, "ExternalInput"),
            }
            sX1 = dram("X1" + sfx, (D, Tl), F32, "Internal")
            sQs = dram("Qs" + sfx, (NQH * 128, Tl), BF16, "Internal")
            sKsl = dram("Ksl" + sfx, (NKV * 128, Tl), BF16, "Internal")
            sVsl = dram("Vsl" + sfx, (Tl, 256), BF16, "Internal")
            sPsl = dram("Psl" + sfx, (3072, Tl), BF16, "Internal")
            sPc = dram("Pc" + sfx, (3 * CH, T), BF16, "Internal")
            sAOs = dram("AOs" + sfx, (DA, Tl), BF16, "Internal")
            sHOsend = dram("HOsend" + sfx, (CH, T), BF16, "Internal")
            if G > 1:
                sKg = dram("Kg" + sfx, (G * NKV * 128, Tl), BF16, "Internal")
                sVg = dram("Vg" + sfx, (G * Tl, 256), BF16, "Internal")
                sPsg = dram("Psg" + sfx, (G * 3072, Tl), BF16, "Internal")
                sHOg = dram("HOg" + sfx, (G * CH, T), BF16, "Internal")
            else:
                sKg, sVg, sPsg, sHOg = sKsl, sVsl, sPsl, sHOsend
            def phase_A():
                if "A" in phases:
                    with contextlib.ExitStack() as st:
                        xT = sb("xT", (128, KC, TB), F32, st)
                        hT = sb("hT", (128, KC, TB), BF16, st)
                        actT = sb("actT", (128, FC, TB), BF16, st)
                        sq = [sb("sq%d" % i, (128, TB), F32, st) for i in range(2)]
                        sqb = [sb("sqb%d" % i, (128, TB), BF16, st) for i in range(2)]
                        rstd = sb("rstd", (128, TB), F32, st)
                        silu_t = [sb("silu%d" % i, (128, TB), F32, st) for i in range(2)]
                        w13buf = [sb("w13buf%d" % i, (128, KC, 2, 256), BF16, st) for i in range(2)]
                        w2buf = [sb("w2buf%d" % i, (128, FC, 256), BF16, st) for i in range(2)]
                        xtok = [sb("xtok%d" % i, (128, D // 2), F32, st) for i in range(2)]
                        ropeC = sb("ropeC_sb", (128, TB), F32, st)
                        ropeS = sb("ropeS_sb", (128, TB), F32, st)
                        Pm = sb("Pm_sb", (128, 128), F32, st)
                        Pm_b = sb("Pm_b", (128, 128), BF16, st)
                        qkg = sb("qkg", (128, 4), F32, st)
                        qraw, t1, t2, hrs = sq[1], silu_t[0], silu_t[1], rstd
                        qo = [sb("qo%d" % i, (128, TB), BF16, st) for i in range(2)]
                        vo = [sb("vo%d" % i, (128, 256), BF16, st) for i in range(2)]
                        P.dma("sp", Pm[:], TBL["Pm"], writes=["Pm"])
                        P.op("dve", lambda e: e.tensor_copy(out=Pm_b[:], in_=Pm[:]), reads=["Pm"], writes=["Pm_b"])
                        for j, nm in enumerate(["q_norm", "k_norm"]):
                            src = W[nm]
                            P.dma("sp", qkg[:, 2 * j:2 * j + 1], src.rearrange("o p -> p o"), writes=[("qkg", 2 * j)],
                                  allow_slow_non_contiguous=True)
                            for blk in range(2):
                                for hf in range(2):
                                    d0 = blk * 64 + hf * 32
                                    s0 = blk * 64 + (1 - hf) * 32
                                    P.dma("sp", qkg[d0:d0 + 32, 2 * j + 1:2 * j + 2],
                                          src[:, s0:s0 + 32].rearrange("o p -> p o"), writes=[("qkg", 2 * j + 1, d0)],
                                          allow_slow_non_contiguous=True)
                        qkg_keys = [("qkg", 0), ("qkg", 2)] + [("qkg", 2 * j + 1, d0) for j in range(2) for d0 in (0, 32, 64, 96)]

                        for blk in range(NBl):
                            t0 = blk * TB
                            for tt8 in range(8):
                                tt, hx = tt8 // 2, tt8 % 2
                                bi = tt8 % 2
                                P.dma("sp", xtok[bi][:], sx_in[t0 + tt * 128:t0 + (tt + 1) * 128, hx * 1024:(hx + 1) * 1024],
                                      writes=[("xtok", bi)])
                                for kc in range(hx * 8, hx * 8 + 8):
                                    pst = psb[kc % 4]
                                    P.op("pe", lambda e, bi=bi, kc=kc, pst=pst: e.transpose(
                                        pst[:, 0:128], xtok[bi][:, (kc % 8) * 128:(kc % 8 + 1) * 128], ident[:]),
                                        reads=[("xtok", bi), "ident"], writes=[("ps", kc % 4)])
                                    if kc % 2 == 0:
                                        P.op("act", lambda e, kc=kc, tt=tt, pst=pst: e.activation(
                                            out=xT[:, kc, tt * 128:(tt + 1) * 128], in_=pst[:, 0:128], func=AF.Copy),
                                            reads=[("ps", kc % 4)], writes=[("xT", kc)])
                                    else:
                                        P.op("dve", lambda e, kc=kc, tt=tt, pst=pst: e.tensor_copy(
                                            out=xT[:, kc, tt * 128:(tt + 1) * 128], in_=pst[:, 0:128]),
                                            reads=[("ps", kc % 4)], writes=[("xT", kc)])
                            ffn(xT, hT, actT, sqb, rstd, 0, 0, (w13buf, w2buf, silu_t), "f1")
                            P.dma("pool", sX1[:, t0:t0 + TB].rearrange("(k p) t -> p k t", p=128), xT[:],
                                  reads=[("xT", kc) for kc in range(KC)], writes=["X1"])
                            norm_fm(xT, hT, sqb, rstd, 1, 0, KC, float(D), "mix")
                            P.dma("sp", ropeC[:], STB["ropeC"][:, t0:t0 + TB], writes=["ropeC"])
                            P.dma("sp", ropeS[:], STB["ropeS"][:, t0:t0 + TB], writes=["ropeS"])
                            for pn in range(9):
                                bi = pn % 2
                                wb = w13buf[bi][:].rearrange("p k g c -> p k (g c)")
                                P.dma("sp", w13buf[bi][:].rearrange("p k g c -> p (k g c)"),
                                      wins[pn].rearrange("p k c -> p (k c)"),
                                      reads=["wscratch"], writes=[("w13buf", bi)])
                                if pn == 2:
                                    for tt in range(4):
                                        pv = psb[4 + tt % 2]
                                        for kc in range(KC):
                                            P.op("pe", lambda e, kc=kc, tt=tt, pv=pv, wb=wb: e.matmul(
                                                pv[:, 0:256], hT[:, kc, tt * 128:(tt + 1) * 128], wb[:, kc, 256:512],
                                                start=(kc == 0), stop=(kc == KC - 1)),
                                                reads=[("w13buf", bi), ("hT", kc)], writes=[("ps", 4 + tt % 2)])
                                        P.op("act", lambda e, tt=tt, pv=pv: e.activation(out=vo[tt % 2][:], in_=pv[:, 0:256], func=AF.Copy),
                                             reads=[("ps", 4 + tt % 2)], writes=[("vo", tt % 2)])
                                        P.dma("pool", sVsl[t0 + tt * 128:t0 + (tt + 1) * 128, :], vo[tt % 2][:],
                                              reads=[("vo", tt % 2)], writes=["Vs"])
                                nsub = 2 if pn == 2 else 4
                                for sub in range(nsub):
                                    pq = psb[sub % 2]
                                    for kc in range(KC):
                                        P.op("pe", lambda e, kc=kc, sub=sub, pq=pq, wb=wb: e.matmul(
                                            pq[:], wb[:, kc, sub * 128:(sub + 1) * 128], hT[:, kc, :],
                                            start=(kc == 0), stop=(kc == KC - 1)),
                                            reads=[("w13buf", bi), ("hT", kc)], writes=[("ps", sub % 2)])
                                    if pn <= 2:
                                        isq = pn < 2
                                        head = pn * 4 + sub if isq else sub
                                        gcol = 0 if isq else 2
                                        P.op("act", lambda e, pq=pq: e.activation(out=qraw[:], in_=pq[:], func=AF.Copy),
                                             reads=[("ps", sub % 2)], writes=[("sq", 1)])
                                        P.op("act", lambda e, pq=pq: e.activation(out=sqb[0][:], in_=pq[:], func=AF.Square),
                                             reads=[("ps", sub % 2)], writes=[("sqb", 0)])
                                        P.op("pe", lambda e: e.matmul(psb[2][:], ones_b[:], sqb[0][:], start=True, stop=True),
                                             reads=[("sqb", 0), "ones_b"], writes=[("ps", 2)])
                                        P.op("act", lambda e, pq=pq: e.activation(out=sqb[1][:], in_=pq[:], func=AF.Copy),
                                             reads=[("ps", sub % 2)], writes=[("sqb", 1)])
                                        P.op("pe", lambda e: e.matmul(psb[3][:], Pm_b[:], sqb[1][:], start=True, stop=True),
                                             reads=[("sqb", 1), "Pm_b"], writes=[("ps", 3)])
                                        if isq:
                                            P.op("act", lambda e: e.activation(out=hrs[:], in_=psb[2][:], func=AF.Sqrt,
                                                                               scale=1.0, bias=eps128_t[:]),
                                                 reads=[("ps", 2), "eps128_t"], writes=["rstd"])
                                        else:
                                            P.op("act", lambda e: e.activation(out=hrs[:], in_=psb[2][:], func=AF.Sqrt,
                                                                               scale=1.0 / 128.0, bias=eps_t[:]),
                                                 reads=[("ps", 2), "eps_t"], writes=["rstd"])
                                        P.op("dve", lambda e: e.reciprocal(out=hrs[:], in_=hrs[:]), reads=["rstd"], writes=["rstd"])
                                        P.op("dve", lambda e, gcol=gcol: e.scalar_tensor_tensor(
                                            out=t1[:], in0=qraw[:], scalar=qkg[:, gcol:gcol + 1], in1=ropeC[:],
                                            op0=ALU.mult, op1=ALU.mult), reads=[("sq", 1), "ropeC"] + qkg_keys, writes=[("silu", 0)])
                                        P.op("dve", lambda e, gcol=gcol: e.scalar_tensor_tensor(
                                            out=t2[:], in0=psb[3][:], scalar=qkg[:, gcol + 1:gcol + 2], in1=ropeS[:],
                                            op0=ALU.mult, op1=ALU.mult), reads=[("ps", 3), "ropeS"] + qkg_keys, writes=[("silu", 1)])
                                        P.op("pool", lambda e: e.tensor_tensor(out=t1[:], in0=t1[:], in1=t2[:], op=ALU.add),
                                             reads=[("silu", 0), ("silu", 1)], writes=[("silu", 0)])
                                        oi = head % 2
                                        P.op("dve", lambda e, oi=oi: e.tensor_tensor(out=qo[oi][:], in0=t1[:], in1=hrs[:], op=ALU.mult),
                                             reads=[("silu", 0), "rstd"], writes=[("qo", oi)])
                                        dst = (sQs if isq else sKsl)[head * 128:(head + 1) * 128, t0:t0 + TB]
                                        P.dma("pool", dst, qo[oi][:], reads=[("qo", oi)], writes=["Qs" if isq else "Ks"])
                                    else:
                                        ch = (pn - 3) * 4 + sub
                                        oi = ch % 2
                                        if ch % 2 == 0:
                                            P.op("act", lambda e, pq=pq, oi=oi: e.activation(out=qo[oi][:], in_=pq[:], func=AF.Copy),
                                                 reads=[("ps", sub % 2)], writes=[("qo", oi)])
                                        else:
                                            P.op("dve", lambda e, pq=pq, oi=oi: e.tensor_copy(out=qo[oi][:], in_=pq[:]),
                                                 reads=[("ps", sub % 2)], writes=[("qo", oi)])
                                        P.dma("pool", sPsl[ch * 128:(ch + 1) * 128, t0:t0 + TB], qo[oi][:],
                                              reads=[("qo", oi)], writes=["Ps"])
                        P.barrier()


            phase_A()
            if G > 1:
                all_gather(sKsl, sKg, groups)
                all_gather(sVsl, sVg, groups)
                all_gather(sPsl, sPsg, groups, defer=True)
            def phase_B():
                if "B" in phases:
                    with contextlib.ExitStack() as st:
                        KT = sb("KT", (128, T), BF16, st)
                        Vg = sb("Vg", (128, NT, 128), BF16, st)
                        kbias = sb("kbias_sb", (128, NT), F32, st)
                        QT = [sb("QT%d" % i, (128, TB), BF16, st) for i in range(2)]
                        PT = [sb("PT%d" % i, (128, TB), BF16, st) for i in range(3)]
                        rec = sb("rec", (128, TB), F32, st)
                        ao = [sb("ao%d" % i, (128, TB), BF16, st) for i in range(2)]
                        if cast_later:
                            stg_b = [sb("stg%d" % i, (128, 2048), F32, st) for i in range(4)]
                            stb_b = [sb("stb%d" % i, (128, 2048), BF16, st) for i in range(4)]
                            n_iter = NKV * 4 * NBl
                            per_iter = -(-len(cast_later) // n_iter)
                        P.dma("sp", kbias[:], STB["kbias"], writes=["kbias"])
                        it = 0
                        for g in range(NKV):
                            for r_ in range(G):
                                P.dma("sp", KT[:, r_ * Tl:(r_ + 1) * Tl], sKg[r_ * 256 + g * 128:r_ * 256 + (g + 1) * 128, :],
                                      reads=["Ks"], writes=["KT"])
                            vsrc = sVg[:, g * 128:(g + 1) * 128].rearrange("(n p) d -> p n d", p=128)
                            nvs = max(1, NT // 16)
                            for vq in range(nvs):
                                n0, n1 = vq * (NT // nvs), (vq + 1) * (NT // nvs)
                                P.dma("sp", Vg[:, n0:n1, :], vsrc[:, n0:n1, :], reads=["Vs"], writes=["Vg"])
                            for h in range(4):
                                hq = g * 4 + h
                                for qc in range(NBl):
                                    qi = it % 2
                                    it += 1
                                    P.dma("sp", QT[qi][:], sQs[hq * 128:(hq + 1) * 128, qc * TB:(qc + 1) * TB],
                                          reads=["Qs"], writes=[("QT", qi)])
                                    bo, bs = 3 + 2 * qi, 4 + 2 * qi

                                    def S(kt, qi=qi):
                                        P.op("pe", lambda e: e.matmul(psb[kt % 3][:], KT[:, kt * 128:(kt + 1) * 128], QT[qi][:],
                                                                      start=True, stop=True),
                                             reads=["KT", ("QT", qi)], writes=[("ps", kt % 3)])

                                    def E(kt):
                                        P.op("act", lambda e: e.activation(out=PT[kt % 3][:], in_=psb[kt % 3][:], func=AF.Exp,
                                                                           bias=kbias[:, kt:kt + 1], scale=1.0),
                                             reads=[("ps", kt % 3), "kbias"], writes=[("PT", kt % 3)])

                                    def PV(kt, bo=bo, bs=bs):
                                        P.op("pe", lambda e: e.matmul(psb[bo][:], Vg[:, kt, :], PT[kt % 3][:],
                                                                      start=(kt == 0), stop=(kt == NT - 1)),
                                             reads=["Vg", ("PT", kt % 3)], writes=[("ps", bo)])
                                        P.op("pe", lambda e: e.matmul(psb[bs][:], ones_b[:], PT[kt % 3][:],
                                                                      start=(kt == 0), stop=(kt == NT - 1)),
                                             reads=["ones_b", ("PT", kt % 3)], writes=[("ps", bs)])

                                    S(0)
                                    if NT > 1:
                                        S(1)
                                    for kt in range(NT):
                                        E(kt)
                                        if kt + 2 < NT:
                                            S(kt + 2)
                                        PV(kt)
                                    P.op("dve", lambda e, bs=bs: e.reciprocal(out=rec[:], in_=psb[bs][:]),
                                         reads=[("ps", bs)], writes=["rec"])
                                    P.op("dve", lambda e, bo=bo, qi=qi: e.tensor_tensor(out=ao[qi][:], in0=psb[bo][:], in1=rec[:], op=ALU.mult),
                                         reads=[("ps", bo), "rec"], writes=[("ao", qi)])
                                    P.dma("pool", sAOs[hq * 128:(hq + 1) * 128, qc * TB:(qc + 1) * TB], ao[qi][:],
                                          reads=[("ao", qi)], writes=["AOs"])
                                    if cast_later:
                                        for _ in range(min(per_iter, len(cast_later))):
                                            cast_block(cast_later.pop(0), stg_b, stb_b, engs=("dve",), ldq="pool")
                        P.barrier()


            phase_B()
            if G > 1:
                cc_wait_all()

            def phase_C():
                if "C" in phases:
                    with contextlib.ExitStack() as st:
                        CCc = 16
                        NCH = CH // CCc
                        nch1 = min(CCc, 512 // (2 * K1))
                        g4 = min(CCc, 512 // K1)
                        MAGIC = 12582912.0
                        TWO_PI = 2.0 * math.pi

                        def fsz(t):
                            n = 1
                            for d_ in t.shape[1:]:
                                n *= d_
                            return n

                        def vw(t, off, dims, p0=0, np_=128):
                            return bass.AP(t, p0 * fsz(t) + off, [[fsz(t), np_]] + [list(d_) for d_ in dims])

                        ld = sb("ld_tmp", (128, 512), F32, st)
                        F1Kb = sb("F1Kb", (128, 2 * K1), BF16, st)
                        F2b = sb("F2b", (128, 384), BF16, st)
                        R3b = sb("R3b", (128, 512), BF16, st)
                        G4b = sb("G4b", (128, 2 * A), BF16, st)
                        TW = sb("TW", (128, 2, 2 * K1), F32, st)
                        TWC = sb("TWC", (128, 2, 256), F32, st)
                        t01 = sb("t01", (128, 128), F32, st)
                        msk = sb("msk", (128, 128), F32, st)

                        def load_cast(dst, name, rows, cols):
                            P.dma("sp", ld[0:rows, 0:cols], STB[name], writes=["ld"])
                            P.op("dve", lambda e: e.tensor_copy(out=dst[0:rows, 0:cols], in_=ld[0:rows, 0:cols]),
                                 reads=["ld"], writes=[name])
                        load_cast(F1Kb, "F1K", K1, 2 * K1)
                        load_cast(F2b, "F2", 128, 384)
                        load_cast(R3b, "R3", 128, 512)
                        load_cast(G4b, "G4", K1, 2 * A)
                        for j, nm in enumerate(["twr", "twi"]):
                            for rep in range(2):
                                P.dma("sp", TW[:, j, rep * K1:(rep + 1) * K1], STB[nm], writes=["TW"])
                        for j, nm in enumerate(["twcr", "twci"]):
                            for rep in range(2):
                                P.dma("sp", TWC[0:K1, j, rep * 128:(rep + 1) * 128], STB[nm], writes=["TWC"])
                        P.dma("sp", t01[0:K1, :], STB["t01tab"], writes=["t01"])
                        P.dma("sp", msk[0:K1, :], STB["masktab"], writes=["msk"])

                        h2s = sb("h2s", (128, 128, K1), BF16, st)
                        w3s = sb("w3s", (128, 2, CH), BF16, st)
                        w1 = sb("w1", (33, 64), F32, st)
                        w2d = sb("w2d", (64, 128), F32, st)
                        fvec = sb("fvec", (128, 4), F32, st)
                        P.op("pool", lambda e: e.memset(h2s[:], 0.0), writes=["h2s"])
                        P.dma("sp", w1[:], W["filt_w1"], writes=["w1"])
                        for rep in range(2):
                            P.dma("sp", w2d[:, rep * 64:(rep + 1) * 64], W["filt_w2"], writes=["w2d"])
                            P.dma("sp", fvec[rep * 64:(rep + 1) * 64, 2:3], W["filt_b2"].rearrange("o p -> p o"), writes=["fvec"])
                            P.dma("sp", fvec[rep * 64:(rep + 1) * 64, 3:4], W["filt_freq"].rearrange("o p -> p o"), writes=["fvec"])
                        P.dma("sp", fvec[0:64, 0:1], W["filt_b1"].rearrange("o p -> p o"), writes=["fvec"])
                        P.dma("sp", fvec[0:64, 1:2], W["filt_freq"].rearrange("o p -> p o"), writes=["fvec"])
                        w3v = sW["w3l"].rearrange("j (d o c) -> j d o c", d=2, o=2)
                        for o_ in range(2):
                            for d_ in range(2):
                                P.dma("sp", ld[d_ * 64:(d_ + 1) * 64, 0:CH], w3v[:, d_, o_, :], writes=["ld"])
                            P.op("dve", lambda e, o_=o_: e.tensor_copy(out=w3s[:, o_, :], in_=ld[:, 0:CH]),
                                 reads=["ld"], writes=["w3s"])
                        with contextlib.ExitStack() as st2:
                            fb = [sb("fb%d" % i, (33, 512), F32, st2) for i in range(2)]
                            u = sb("u_mlp", (128, 512), F32, st2)
                            kk = sb("kk_mlp", (128, 512), F32, st2)
                            h1 = sb("h1_mlp", (64, 512), F32, st2)
                            mkblk = [sb("mkblk%d" % i, (128, 512), F32, st2) for i in range(2)]

                            def sin_layer(ps, np_, bcol, fcol, out_ap_fn):
                                P.op("dve", lambda e: e.tensor_scalar(out=u[0:np_, :], in0=ps[0:np_, :], scalar1=fvec[0:np_, bcol:bcol + 1],
                                                                      scalar2=fvec[0:np_, fcol:fcol + 1], op0=ALU.add, op1=ALU.mult),
                                     reads=[("ps", 7), "fvec"], writes=["u"])
                                P.op("dve", lambda e: e.tensor_scalar(out=kk[0:np_, :], in0=u[0:np_, :], scalar1=1.0 / TWO_PI, scalar2=MAGIC,
                                                                      op0=ALU.mult, op1=ALU.add), reads=["u"], writes=["kk"])
                                P.op("dve", lambda e: e.tensor_scalar(out=kk[0:np_, :], in0=kk[0:np_, :], scalar1=-MAGIC, scalar2=-TWO_PI,
                                                                      op0=ALU.add, op1=ALU.mult), reads=["kk"], writes=["kk"])
                                P.op("dve", lambda e: e.tensor_tensor(out=u[0:np_, :], in0=u[0:np_, :], in1=kk[0:np_, :], op=ALU.add),
                                     reads=["u", "kk"], writes=["u"])
                                out_ap_fn()

                            for j in range(N2 // 512):
                                bi = j % 2
                                P.dma("sp", fb[bi][:], STB["featsT"][:, j * 512:(j + 1) * 512], writes=[("fb", bi)])
                                P.op("pe", lambda e, bi=bi: e.matmul(psb[7][0:64, :], w1[:], fb[bi][:], start=True, stop=True),
                                     reads=["w1", ("fb", bi)], writes=[("ps", 7)])

                                def o1():
                                    P.op("act", lambda e: e.activation(out=h1[:], in_=u[0:64, :], func=AF.Sin), reads=["u"], writes=["h1"])
                                sin_layer(psb[7], 64, 0, 1, o1)
                                P.op("pe", lambda e: e.matmul(psb[7][:], w2d[:], h1[:], start=True, stop=True),
                                     reads=["w2d", "h1"], writes=[("ps", 7)])
                                fwd = (4 * j) < A
                                r0 = 0 if fwd else 64

                                P.dma("sp", mkblk[bi][:], bass.AP(STB["masktab"].tensor, j * 512, [[0, 128], [1, 512]]),
                                      writes=[("mkblk", bi)])

                                def o2(j=j, r0=r0, bi=bi):
                                    dst = vw(h2s, 4 * j, [[K1, 128], [1, 4]], p0=r0, np_=64)
                                    src = kk[r0:r0 + 64, :].rearrange("p (a b) -> p b a", a=4)
                                    mk_ = mkblk[bi][r0:r0 + 64, :].rearrange("p (a b) -> p b a", a=4)
                                    P.op("act", lambda e: e.activation(out=kk[r0:r0 + 64, :], in_=u[r0:r0 + 64, :], func=AF.Sin),
                                         reads=["u", "kk"], writes=["kk"])
                                    P.op("dve", lambda e: e.tensor_tensor(out=dst, in0=src, in1=mk_, op=ALU.mult),
                                         reads=["kk", ("mkblk", bi)], writes=["h2s"])
                                sin_layer(psb[7], 128, 2, 3, o2)
                            P.barrier()

                        with contextlib.ExitStack() as st3:
                            TQ = min(2048, T)
                            rawrow = [sb("rawrow%d" % i, (128, T + 2), BF16, st3) for i in range(2)]
                            cacc = [sb("cacc%d" % i, (128, TQ), F32, st3) for i in range(2)]
                            coutb = [sb("coutb%d" % i, (128, TQ), BF16, st3) for i in range(2)]
                            cwT = sb("cwT", (128, 3, nsg), F32, st3)
                            cbT = sb("cbT", (128, nsg), F32, st3)
                            cidx = sb("cidx_sb", (128, nsg * G), mybir.dt.int32, st3)
                            P.dma("sp", cidx[:], sW["cidx"], writes=["cidx"])
                            for i_ in range(2):
                                P.op("dve", lambda e, i_=i_: e.memset(rawrow[i_][:, 0:1], 0.0), writes=[("rawrow", i_)])
                                P.op("dve", lambda e, i_=i_: e.memset(rawrow[i_][:, T + 1:T + 2], 0.0), writes=[("rawrow", i_)])
                            for tp_ in range(3):
                                P.dma("sp", cwT[:, tp_, :], bass.AP(sW["cwl"].tensor, tp_ * 3 * CH, [[1, 128], [128, nsg]]), writes=["cwT"])
                            P.dma("sp", cbT[:], bass.AP(sW["cbl"].tensor, 0, [[1, 128], [128, nsg]]), writes=["cbT"])
                            it_ = 0
                            for sg in range(nsg):
                                rb = sg % 2
                                for r_ in range(G):
                                    P.dma_fn("pool", lambda e, rb=rb, r_=r_, sg=sg: e.indirect_dma_start(
                                        out=rawrow[rb][:, 1 + r_ * Tl:1 + (r_ + 1) * Tl], out_offset=None, in_=sPsg,
                                        in_offset=bass.IndirectOffsetOnAxis(ap=cidx[:, sg * G + r_:sg * G + r_ + 1], axis=0)),
                                        reads=["Ps", "cidx"], writes=[("rawrow", rb)])
                                for q in range(T // TQ):
                                    ai = it_ % 2
                                    it_ += 1
                                    q0 = q * TQ
                                    P.op("act", lambda e, rb=rb, ai=ai, q0=q0, sg=sg: e.activation(
                                        out=cacc[ai][:], in_=rawrow[rb][:, q0:q0 + TQ], func=AF.Identity,
                                        scale=cwT[:, 0, sg:sg + 1], bias=cbT[:, sg:sg + 1]),
                                        reads=[("rawrow", rb), "cwT", "cbT"], writes=[("cacc", ai)])
                                    P.op("dve", lambda e, rb=rb, ai=ai, q0=q0, sg=sg: e.scalar_tensor_tensor(
                                        out=cacc[ai][:], in0=rawrow[rb][:, q0 + 1:q0 + 1 + TQ], scalar=cwT[:, 1, sg:sg + 1], in1=cacc[ai][:],
                                        op0=ALU.mult, op1=ALU.add),
                                        reads=[("rawrow", rb), "cwT", ("cacc", ai)], writes=[("cacc", ai)])
                                    P.op("dve", lambda e, rb=rb, ai=ai, q0=q0, sg=sg: e.scalar_tensor_tensor(
                                        out=coutb[ai][:], in0=rawrow[rb][:, q0 + 2:q0 + 2 + TQ], scalar=cwT[:, 2, sg:sg + 1], in1=cacc[ai][:],
                                        op0=ALU.mult, op1=ALU.add),
                                        reads=[("rawrow", rb), "cwT", ("cacc", ai)], writes=[("coutb", ai)])
                                    P.dma("sp", sPc[sg * 128:(sg + 1) * 128, q0:q0 + TQ], coutb[ai][:],
                                          reads=[("coutb", ai)], writes=["Pc"])
                            P.barrier()
                        strm = [[sb("strm%d_%d" % (s_, i), (128, CCc, 128), BF16, st) for i in range(2)] for s_ in range(3)]
                        hbs = sb("hbs", (128, 2, CCc), F32, st)
                        adec = sb("adec", (128, 2, CCc), F32, st)
                        zf_ = sb("zf", (128, CCc, 128), F32, st)
                        dsk = sb("dsk", (128, CCc, 128), F32, st)
                        ctmp = sb("ctmp", (128, 4, 128), F32, st)
                        inb = sb("inb", (128, CCc, 128), BF16, st)
                        kf = sb("kf", (128, CCc, 128), F32, st)
                        Ee = sb("Ee", (128, CCc, 128), F32, st)
                        kfb = sb("kfb", (128, CCc, 128), BF16, st)
                        ksum = sb("ksum", (128, CCc), F32, st)
                        ksumb = sb("ksumb", (128, CCc), BF16, st)
                        rn2 = [sb("rn%d" % i, (128, CCc), F32, st) for i in range(2)]
                        Yp_d = sb("Yp", (128, CCc, 2, K1), BF16, st)
                        tm1_d = [sb("tm1_%d" % i, (128, 512), F32, st) for i in range(2)]
                        tm2_d = [sb("tm2_%d" % i, (128, 512), F32, st) for i in range(2)]
                        tm1, tm2 = tm1_d, tm2_d
                        Ksp2 = [sb("Ksp%d" % i, (128, CCc, 2, K1), BF16, st) for i in range(2)]
                        YpF = sb("YpF", (128, CCc, 2, K1), BF16, st)
                        tmF1 = [sb("tmF1_%d" % i, (128, 512), F32, st) for i in range(2)]
                        tmF2 = [sb("tmF2_%d" % i, (128, 512), F32, st) for i in range(2)]
                        Zs = [sb("Zs%d" % i, (128, 2 * 512), BF16, st) for i in range(2)]
                        ZsS = [sb("ZsS%d" % i, (128, 2 * 512), BF16, st) for i in range(2)]
                        Zf = sb("Zf", (128, CCc, 2, K1), BF16, st)
                        Up = sb("Up", (128, CCc, 2, 128), BF16, st)
                        oob = sb("oob", (128, CCc, 128), BF16, st)

                        def fwd_dft(src_b, krows, is_filter, tag, par):
                            banks1 = (0, 7) if is_filter else (1, 6)
                            tm1, tm2 = (tmF1, tmF2) if is_filter else (tm1_d, tm2_d)
                            Yp = YpF if is_filter else Yp_d
                            ypn = "YpF" if is_filter else "Yp"
                            t1n, t2n = ("tmF1", "tmF2") if is_filter else ("tm1", "tm2")
                            Ksp = Ksp2[par]
                            rn = rn2[par]
                            for q in range(CCc // nch1):
                                bk = banks1[q % 2]
                                ti = q % 2
                                for cl in range(nch1):
                                    c = q * nch1 + cl
                                    P.op("pe", lambda e, c=c, cl=cl, bk=bk: e.matmul(
                                        psb[bk][:, cl * 2 * K1:(cl + 1) * 2 * K1], src_b[0:krows, c, :], F1Kb[0:krows, :],
                                        start=True, stop=True), reads=[tag, "F1K"], writes=[("ps", bk)])
                                n_el = nch1 * 2 * K1
                                pin = psb[bk][:, 0:n_el].rearrange("p (c x) -> p c x", c=nch1)
                                twr2 = vw(TW, 0, [[0, nch1], [1, 2 * K1]])
                                twi2 = vw(TW, 2 * K1, [[0, nch1], [1, 2 * K1]])
                                o1_ = tm1[ti][:, 0:n_el].rearrange("p (c x) -> p c x", c=nch1)
                                o2_ = tm2[ti][:, 0:n_el].rearrange("p (c x) -> p c x", c=nch1)
                                P.op("dve", lambda e, pin=pin, twr2=twr2, o1_=o1_: e.tensor_tensor(out=o1_, in0=pin, in1=twr2, op=ALU.mult),
                                     reads=[("ps", bk), "TW"], writes=[(t1n, ti)])
                                P.op("dve", lambda e, pin=pin, twi2=twi2, o2_=o2_: e.tensor_tensor(out=o2_, in0=pin, in1=twi2, op=ALU.mult),
                                     reads=[("ps", bk), "TW"], writes=[(t2n, ti)])
                                a1 = tm1[ti][:, 0:n_el].rearrange("p (c r k) -> p c r k", c=nch1, r=2)
                                a2 = tm2[ti][:, 0:n_el].rearrange("p (c r k) -> p c r k", c=nch1, r=2)
                                c0_ = q * nch1
                                P.op("pool", lambda e, a1=a1, a2=a2, c0_=c0_: e.tensor_tensor(
                                    out=Yp[:, c0_:c0_ + nch1, 0, :], in0=a1[:, :, 0, :], in1=a2[:, :, 1, :], op=ALU.subtract),
                                    reads=[(t1n, ti), (t2n, ti)], writes=[(ypn, q)])
                                P.op("pool", lambda e, a1=a1, a2=a2, c0_=c0_: e.tensor_tensor(
                                    out=Yp[:, c0_:c0_ + nch1, 1, :], in0=a2[:, :, 0, :], in1=a1[:, :, 1, :], op=ALU.add),
                                    reads=[(t1n, ti), (t2n, ti)], writes=[(ypn, q)])
                            ypk = [(ypn, q) for q in range(CCc // nch1)]
                            for gi_ in range(CCc // g4):
                                cs = gi_ * g4
                                yr = Yp[:, cs:cs + g4, 0, :]
                                yi = Yp[:, cs:cs + g4, 1, :]
                                n_el = g4 * K1
                                b2r, b2i = (0, 7) if is_filter else (2, 3)
                                zr = psb[b2r][:, 0:n_el]
                                zi = psb[b2i][:, 0:n_el]
                                P.op("pe", lambda e, yr=yr, zr=zr: e.matmul(zr, F2b[:, 0:128], yr, start=True, stop=False),
                                     reads=ypk + ["F2"], writes=[("ps", b2r)])
                                P.op("pe", lambda e, yi=yi, zr=zr: e.matmul(zr, F2b[:, 256:384], yi, start=False, stop=True),
                                     reads=ypk + ["F2"], writes=[("ps", b2r)])
                                P.op("pe", lambda e, yr=yr, zi=zi: e.matmul(zi, F2b[:, 128:256], yr, start=True, stop=False),
                                     reads=ypk + ["F2"], writes=[("ps", b2i)])
                                P.op("pe", lambda e, yi=yi, zi=zi: e.matmul(zi, F2b[:, 0:128], yi, start=False, stop=True),
                                     reads=ypk + ["F2"], writes=[("ps", b2i)])
                                zr3 = zr.rearrange("p (c k) -> p c k", c=g4)
                                zi3 = zi.rearrange("p (c k) -> p c k", c=g4)
                                if is_filter:
                                    rnb = vw(rn, cs, [[1, g4], [0, K1]])
                                    for (src_, rsel) in ((zr3, 0), (zi3, 1)):
                                        P.op("dve", lambda e, src_=src_, rsel=rsel, rnb=rnb, cs=cs: e.tensor_tensor(
                                            out=Ksp[:, cs:cs + g4, rsel, :], in0=src_, in1=rnb, op=ALU.mult),
                                            reads=[("ps", (b2r, b2i)[rsel]), ("rn", par)], writes=[("Ksp", par)])
                                else:
                                    zb = gi_ % 2
                                    zs4 = Zs[zb][:, 0:2 * n_el].rearrange("p (c r k) -> p c r k", c=g4, r=2)
                                    P.op("act", lambda e, zs4=zs4, zr3=zr3: e.activation(out=zs4[:, :, 0, :], in_=zr3, func=AF.Copy),
                                         reads=[("ps", b2r)], writes=[("Zs", zb)])
                                    P.op("act", lambda e, zs4=zs4, zi3=zi3: e.activation(out=zs4[:, :, 1, :], in_=zi3, func=AF.Copy),
                                         reads=[("ps", b2i)], writes=[("Zs", zb)])
                                    zss4 = ZsS[zb][:, 0:2 * n_el].rearrange("p (c r k) -> p c r k", c=g4, r=2)
                                    P.op("act", lambda e, zss4=zss4, zi3=zi3: e.activation(out=zss4[:, :, 0, :], in_=zi3, func=AF.Copy),
                                         reads=[("ps", b2i)], writes=[("ZsS", zb)])
                                    P.op("act", lambda e, zss4=zss4, zr3=zr3: e.activation(out=zss4[:, :, 1, :], in_=zr3, func=AF.Copy),
                                         reads=[("ps", b2r)], writes=[("ZsS", zb)])
                                    zsflat = ZsS[zb][:, 0:2 * n_el].rearrange("p (c x) -> p c x", c=g4)
                                    zflat = Zs[zb][:, 0:2 * n_el].rearrange("p (c x) -> p c x", c=g4)
                                    kfl = Ksp[:, cs:cs + g4, :, :].rearrange("p c r k -> p c (r k)")
                                    p1 = tm1[zb][:, 0:2 * n_el].rearrange("p (c x) -> p c x", c=g4) if 2 * n_el <= 512 else None
                                    pa = Zs[zb]
                                    P.op("dve", lambda e, zflat=zflat, kfl=kfl, zb=zb, n_el=n_el: e.tensor_tensor(
                                        out=PR1[zb][:, 0:2 * n_el].rearrange("p (c x) -> p c x", c=g4), in0=zflat, in1=kfl, op=ALU.mult),
                                        reads=[("Zs", zb), ("Ksp", par)], writes=[("PR1", zb)])
                                    P.op("dve", lambda e, zsflat=zsflat, kfl=kfl, zb=zb, n_el=n_el: e.tensor_tensor(
                                        out=PR2[zb][:, 0:2 * n_el].rearrange("p (c x) -> p c x", c=g4), in0=zsflat, in1=kfl, op=ALU.mult),
                                        reads=[("ZsS", zb), ("Ksp", par)], writes=[("PR2", zb)])
                                    q1 = PR1[zb][:, 0:2 * n_el].rearrange("p (c r k) -> p c r k", c=g4, r=2)
                                    q2 = PR2[zb][:, 0:2 * n_el].rearrange("p (c r k) -> p c r k", c=g4, r=2)
                                    P.op("dve", lambda e, q1=q1, cs=cs: e.tensor_tensor(
                                        out=Zf[:, cs:cs + g4, 0, :], in0=q1[:, :, 0, :], in1=q1[:, :, 1, :], op=ALU.subtract),
                                        reads=[("PR1", zb)], writes=[("Zf", gi_)])
                                    P.op("pool", lambda e, q2=q2, cs=cs: e.tensor_tensor(
                                        out=Zf[:, cs:cs + g4, 1, :], in0=q2[:, :, 0, :], in1=q2[:, :, 1, :], op=ALU.add),
                                        reads=[("PR2", zb)], writes=[("Zf", gi_)])

                        PR1 = [sb("PR1_%d" % i, (128, 1024), F32, st) for i in range(2)]
                        PR2 = [sb("PR2_%d" % i, (128, 1024), F32, st) for i in range(2)]

                        def inv_dft(epilogue):
                            zfk = [("Zf", gi_) for gi_ in range(CCc // g4)]
                            for q in range(CCc // 2):
                                bk = 4 + q % 2
                                for cl in range(2):
                                    c = 2 * q + cl
                                    P.op("pe", lambda e, c=c, cl=cl, bk=bk: e.matmul(
                                        psb[bk][0:K1, cl * 256:(cl + 1) * 256], Zf[:, c, 0, :], R3b[:, 0:256], start=True, stop=False),
                                        reads=zfk + ["R3"], writes=[("ps", bk)])
                                    P.op("pe", lambda e, c=c, cl=cl, bk=bk: e.matmul(
                                        psb[bk][0:K1, cl * 256:(cl + 1) * 256], Zf[:, c, 1, :], R3b[:, 256:512], start=False, stop=True),
                                        reads=zfk + ["R3"], writes=[("ps", bk)])
                                tb_ = q % 2
                                pin = psb[bk][0:K1, :].rearrange("p (c x) -> p c x", c=2)
                                cr2 = vw(TWC, 0, [[0, 2], [1, 256]], np_=K1)
                                ci2 = vw(TWC, 256, [[0, 2], [1, 256]], np_=K1)
                                o1_ = tm1[tb_][0:K1, :].rearrange("p (c x) -> p c x", c=2)
                                o2_ = tm2[tb_][0:K1, :].rearrange("p (c x) -> p c x", c=2)
                                P.op("dve", lambda e, pin=pin, cr2=cr2, o1_=o1_: e.tensor_tensor(out=o1_, in0=pin, in1=cr2, op=ALU.mult),
                                     reads=[("ps", bk), "TWC"], writes=[("tm1", tb_)])
                                P.op("dve", lambda e, pin=pin, ci2=ci2, o2_=o2_: e.tensor_tensor(out=o2_, in0=pin, in1=ci2, op=ALU.mult),
                                     reads=[("ps", bk), "TWC"], writes=[("tm2", tb_)])
                                a1 = tm1[tb_][0:K1, :].rearrange("p (c r k) -> p c r k", c=2, r=2)
                                a2 = tm2[tb_][0:K1, :].rearrange("p (c r k) -> p c r k", c=2, r=2)
                                P.op("pool", lambda e, a1=a1, a2=a2, q=q: e.tensor_tensor(
                                    out=Up[0:K1, 2 * q:2 * q + 2, 0, :], in0=a1[:, :, 0, :], in1=a2[:, :, 1, :], op=ALU.subtract),
                                    reads=[("tm1", tb_), ("tm2", tb_)], writes=[("Up", q // 2)])
                                P.op("pool", lambda e, a1=a1, a2=a2, q=q: e.tensor_tensor(
                                    out=Up[0:K1, 2 * q:2 * q + 2, 1, :], in0=a2[:, :, 0, :], in1=a1[:, :, 1, :], op=ALU.add),
                                    reads=[("tm1", tb_), ("tm2", tb_)], writes=[("Up", q // 2)])
                            for gq in range(CCc // 4):
                                bk = 6
                                P.op("pe", lambda e, gq=gq: e.matmul(psb[6][0:A, :], G4b[0:K1, 0:A], Up[0:K1, 4 * gq:4 * gq + 4, 0, :],
                                                                     start=True, stop=False),
                                     reads=[("Up", gq), "G4"], writes=[("ps", 6)])
                                P.op("pe", lambda e, gq=gq: e.matmul(psb[6][0:A, :], G4b[0:K1, A:2 * A], Up[0:K1, 4 * gq:4 * gq + 4, 1, :],
                                                                     start=False, stop=True),
                                     reads=[("Up", gq), "G4"], writes=[("ps", 6)])
                                epilogue(gq, psb[6][0:A, :].rearrange("p (c b) -> p c b", c=4))

                        steps = [(ci, o_) for ci in range(NCH) for o_ in range(2)]

                        def emit_filter(k):
                            ci, o_ = steps[k]
                            par = k % 2
                            c0 = ci * CCc
                            rn = rn2[par]
                            if o_ == 0:
                                for d_ in range(2):
                                    P.dma("sp", adec[d_ * A:(d_ + 1) * A], bass.AP(sW["decl"].tensor, d_ * 2 * CH + c0, [[0, A], [CH, 2], [1, CCc]]),
                                          writes=["adec"])
                                P.op("act", lambda e: e.activation(out=adec[0:K1], in_=adec[0:K1], func=AF.Abs),
                                     reads=["adec"], writes=["adec"])
                            for bg in range(8):
                                bk = (0, 7)[bg % 2]
                                for bl in range(16):
                                    b_ = bg * 16 + bl
                                    P.op("pe", lambda e, b_=b_, bl=bl, bk=bk, o_=o_, c0=c0: e.matmul(
                                        psb[bk][0:K1, bl * CCc:(bl + 1) * CCc], h2s[:, b_, :], w3s[:, o_, c0:c0 + CCc], start=True, stop=True),
                                        reads=["h2s", "w3s"], writes=[("ps", bk)])
                                P.op("act", lambda e, bg=bg, bk=bk: e.activation(
                                    out=kf[0:K1, :, bg * 16:(bg + 1) * 16],
                                    in_=psb[bk][0:K1, 0:16 * CCc].rearrange("p (b c) -> p c b", b=16), func=AF.Copy),
                                    reads=[("ps", bk)], writes=["kf"])
                            t01b = vw(t01, 0, [[0, CCc], [1, 128]], np_=K1)
                            adb = vw(adec, o_ * CCc, [[1, CCc], [0, 128]], np_=K1)
                            P.op("dve", lambda e, t01b=t01b, adb=adb: e.tensor_tensor(out=Ee[0:K1], in0=t01b, in1=adb, op=ALU.mult),
                                 reads=["t01", "adec"], writes=["Ee"])
                            P.op("act", lambda e: e.activation(out=Ee[0:K1], in_=Ee[0:K1], func=AF.Exp, scale=-1.0),
                                 reads=["Ee"], writes=["Ee"])
                            P.op("dve", lambda e: e.tensor_tensor(out=kf[0:K1], in0=kf[0:K1], in1=Ee[0:K1], op=ALU.mult),
                                 reads=["kf", "Ee"], writes=["kf"])
                            P.op("dve", lambda e: e.tensor_reduce(out=ksum[0:K1], in_=kf[0:K1], axis=AX.X, op=ALU.add,
                                                                  apply_absolute_value=True), reads=["kf"], writes=["ksum"])
                            P.op("dve", lambda e: e.tensor_copy(out=ksumb[0:K1], in_=ksum[0:K1]), reads=["ksum"], writes=["ksumb"])
                            P.op("act", lambda e: e.activation(out=kfb[0:K1], in_=kf[0:K1], func=AF.Copy), reads=["kf"], writes=["kfb"])
                            P.op("pe", lambda e: e.matmul(psb[7][:, 0:CCc], ones_b[0:K1, :], ksumb[0:K1, :], start=True, stop=True),
                                 reads=["ksumb", "ones_b"], writes=[("ps", 7)])
                            P.op("dve", lambda e, rn=rn: e.reciprocal(out=rn[:], in_=psb[7][:, 0:CCc]), reads=[("ps", 7)], writes=[("rn", par)])
                            fwd_dft(kfb, K1, True, "kfb", par)

                        def emit_data(k):
                            ci, o_ = steps[k]
                            par = k % 2
                            c0 = ci * CCc
                            pb = ci % 2
                            hvb, hx1b, hx2b = strm[0][pb], strm[1][pb], strm[2][pb]
                            if o_ == 0:
                                P.dma("sp", hbs[0:A], bass.AP(sW["hbl"].tensor, c0, [[0, A], [CH, 2], [1, CCc]]), writes=["hbs"])
                                for s_ in range(3):
                                    P.dma("sp", strm[s_][pb][0:A], bass.AP(sPc.tensor, (s_ * CH + c0) * T, [[128, A], [T, CCc], [1, 128]]),
                                          reads=["Pc"], writes=[("strm", s_, pb)])
                            hb_bc = vw(hbs, o_ * CCc, [[1, CCc], [0, 128]], np_=A)
                            if o_ == 0:
                                P.op("pool", lambda e, hvb=hvb, hb_bc=hb_bc: e.tensor_tensor(out=dsk[0:A], in0=hvb[0:A], in1=hb_bc, op=ALU.mult),
                                     reads=[("strm", 0, pb), "hbs"], writes=["dsk"])
                                fwd_dft(hvb, A, False, ("strm", 0, pb), par)
                            else:
                                P.op("act", lambda e: e.activation(out=inb[0:A], in_=zf_[0:A], func=AF.Copy),
                                     reads=["zf"], writes=["inb"])
                                P.op("pool", lambda e, hb_bc=hb_bc: e.tensor_tensor(out=dsk[0:A], in0=zf_[0:A], in1=hb_bc, op=ALU.mult),
                                     reads=["zf", "hbs"], writes=["dsk"])
                                fwd_dft(inb, A, False, "inb", par)

                            def epi(gq, yps, o_=o_, hx1b=hx1b, hx2b=hx2b, pb=pb):
                                cs = 4 * gq
                                P.op("dve", lambda e: e.tensor_tensor(out=ctmp[0:A, 0:4, :], in0=yps, in1=dsk[0:A, cs:cs + 4, :], op=ALU.add),
                                     reads=[("ps", 6), "dsk"], writes=["ctmp"])
                                if o_ == 0:
                                    P.op("pool", lambda e: e.tensor_tensor(out=zf_[0:A, cs:cs + 4, :], in0=ctmp[0:A, 0:4, :],
                                                                           in1=hx1b[0:A, cs:cs + 4, :], op=ALU.mult),
                                         reads=["ctmp", ("strm", 1, pb)], writes=["zf"])
                                else:
                                    P.op("pool", lambda e: e.tensor_tensor(out=oob[0:A, cs:cs + 4, :], in0=ctmp[0:A, 0:4, :],
                                                                           in1=hx2b[0:A, cs:cs + 4, :], op=ALU.mult),
                                         reads=["ctmp", ("strm", 2, pb)], writes=["oob"])
                            inv_dft(epi)
                            if o_ == 1:
                                P.dma("pool", bass.AP(sHOsend.tensor, c0 * T, [[128, A], [T, CCc], [1, 128]]), oob[0:A],
                                      reads=["oob"], writes=["HOs"])

                        P.replay(P.record(lambda: emit_filter(0)))
                        for k in range(len(steps)):
                            recD = P.record(lambda: emit_data(k))
                            recF = P.record(lambda: emit_filter(k + 1)) if k + 1 < len(steps) else []
                            P.replay(recD, recF)
                        P.barrier()


            phase_C()
            if G > 1:
                all_gather(sHOsend, sHOg, groups)
            def phase_D():
                if "D" in phases:
                    with contextlib.ExitStack() as st:
                        xT = sb("xT", (128, KC, TB), F32, st)
                        hT = sb("hT", (128, KC, TB), BF16, st)
                        actT = sb("actT", (128, FC, TB), BF16, st)
                        sq = [sb("sq%d" % i, (128, TB), F32, st) for i in range(2)]
                        sqb = [sb("sqb%d" % i, (128, TB), BF16, st) for i in range(2)]
                        rstd = sb("rstd", (128, TB), F32, st)
                        silu_t = [sb("silu%d" % i, (128, TB), F32, st) for i in range(2)]
                        w13buf = [sb("w13buf%d" % i, (128, KC, 2, 256), BF16, st) for i in range(2)]
                        w2buf = [sb("w2buf%d" % i, (128, FC, 256), BF16, st) for i in range(2)]
                        mix = sb("mixin", (128, KC, TB), BF16, st)
                        ytok = [sb("ytok%d" % i, (128, D // 2), F32, st) for i in range(2)]
                        hoidx = sb("hoidx_sb", (128, 8 * NBl), mybir.dt.int32, st)
                        P.dma("sp", hoidx[:], sW["hoidx"], writes=["hoidx"])
                        for blk in range(NBl):
                            t0 = blk * TB
                            P.dma("sp", xT[:], sX1[:, t0:t0 + TB].rearrange("(k p) t -> p k t", p=128), reads=["X1"],
                                  writes=[("xT", kc) for kc in range(KC)])
                            P.dma("sp", mix[:, 0:8, :], sAOs[:, t0:t0 + TB].rearrange("(k p) t -> p k t", p=128), reads=["AOs"],
                                  writes=[("mix", kc) for kc in range(8)])
                            for k8 in range(8):
                                P.dma_fn("pool", lambda e, k8=k8, blk=blk: e.indirect_dma_start(
                                    out=mix[:, 8 + k8, :], out_offset=None, in_=sHOg.rearrange("c (q j) -> (c q) j", j=TB),
                                    in_offset=bass.IndirectOffsetOnAxis(ap=hoidx[:, k8 * NBl + blk:k8 * NBl + blk + 1], axis=0)),
                                    reads=["HOs", "hoidx"], writes=[("mix", 8 + k8)])
                            for grp in range(2):
                                srcv = mix[:, grp * 8:(grp + 1) * 8, :]
                                dstv = hT[:, grp * 8:(grp + 1) * 8, :]
                                pss = psb[6]
                                for c in range(8):
                                    cc_ = grp * 8 + c
                                    P.op("act", lambda e, c=c, cc_=cc_: e.activation(out=sqb[c % 2][:], in_=mix[:, cc_, :], func=AF.Square),
                                         reads=[("mix", cc_)], writes=[("sqb", c % 2)])
                                    P.op("pe", lambda e, c=c: e.matmul(pss[:], ones_b[:], sqb[c % 2][:], start=(c == 0), stop=(c == 7)),
                                         reads=[("sqb", c % 2), "ones_b"], writes=[("ps", 6)])
                                P.op("act", lambda e: e.activation(out=rstd[:], in_=pss[:], func=AF.Sqrt, scale=1.0 / 1024.0, bias=eps_t[:]),
                                     reads=[("ps", 6), "eps_t"], writes=["rstd"])
                                P.op("dve", lambda e: e.reciprocal(out=rstd[:], in_=rstd[:]), reads=["rstd"], writes=["rstd"])
                                for c in range(8):
                                    cc_ = grp * 8 + c
                                    P.op("dve", lambda e, cc_=cc_: e.scalar_tensor_tensor(
                                        out=hT[:, cc_, :], in0=mix[:, cc_, :], scalar=gains[:, 2, cc_:cc_ + 1], in1=rstd[:],
                                        op0=ALU.mult, op1=ALU.mult),
                                        reads=[("mix", cc_), "rstd", ("gains", 2)], writes=[("hT", cc_)])
                            for pn in range(4):
                                bi = pn % 2
                                wb = w13buf[bi][:].rearrange("p k g c -> p k (g c)")
                                P.dma("sp", w13buf[bi][:].rearrange("p k g c -> p (k g c)"), wouts[pn].rearrange("p k c -> p (k c)"),
                                      reads=["wscratch"], writes=[("w13buf", bi)])
                                for sub in range(4):
                                    kc_out = pn * 4 + sub
                                    pq = psb[sub % 2]
                                    for kc in range(KC):
                                        P.op("pe", lambda e, kc=kc, sub=sub, pq=pq, wb=wb: e.matmul(
                                            pq[:], wb[:, kc, sub * 128:(sub + 1) * 128], hT[:, kc, :],
                                            start=(kc == 0), stop=(kc == KC - 1)),
                                            reads=[("w13buf", bi), ("hT", kc)], writes=[("ps", sub % 2)])
                                    P.op("dve", lambda e, pq=pq, kc_out=kc_out: e.tensor_tensor(
                                        out=xT[:, kc_out, :], in0=pq[:], in1=xT[:, kc_out, :], op=ALU.add),
                                        reads=[("ps", sub % 2), ("xT", kc_out)], writes=[("xT", kc_out)])
                            ffn(xT, hT, actT, sqb, rstd, 1, 3, (w13buf, w2buf, silu_t), "f2")
                            pss = psb[6]
                            for c in range(KC):
                                P.op("act", lambda e, c=c: e.activation(out=sqb[c % 2][:], in_=xT[:, c, :], func=AF.Square),
                                     reads=[("xT", c)], writes=[("sqb", c % 2)])
                                P.op("pe", lambda e, c=c: e.matmul(pss[:], ones_b[:], sqb[c % 2][:], start=(c == 0), stop=(c == KC - 1)),
                                     reads=[("sqb", c % 2), "ones_b"], writes=[("ps", 6)])
                            P.op("act", lambda e: e.activation(out=rstd[:], in_=pss[:], func=AF.Sqrt, scale=1.0 / D, bias=eps_t[:]),
                                 reads=[("ps", 6), "eps_t"], writes=["rstd"])
                            P.op("dve", lambda e: e.reciprocal(out=rstd[:], in_=rstd[:]), reads=["rstd"], writes=["rstd"])
                            for c in range(KC):
                                P.op("dve", lambda e, c=c: e.scalar_tensor_tensor(
                                    out=xT[:, c, :], in0=xT[:, c, :], scalar=gains[:, 4, c:c + 1], in1=rstd[:],
                                    op0=ALU.mult, op1=ALU.mult),
                                    reads=[("xT", c), "rstd", ("gains", 4)], writes=[("xT", c)])
                            for tt8 in range(8):
                                tt, hx = tt8 // 2, tt8 % 2
                                bi = tt8 % 2
                                for kc in range(hx * 8, hx * 8 + 8):
                                    pst = psb[kc % 4]
                                    P.op("pe", lambda e, kc=kc, tt=tt, pst=pst: e.transpose(
                                        pst[:, 0:128], xT[:, kc, tt * 128:(tt + 1) * 128], ident[:]),
                                        reads=[("xT", kc), "ident"], writes=[("ps", kc % 4)])
                                    if kc % 2 == 0:
                                        P.op("act", lambda e, kc=kc, bi=bi, pst=pst: e.activation(
                                            out=ytok[bi][:, (kc % 8) * 128:(kc % 8 + 1) * 128], in_=pst[:, 0:128], func=AF.Copy),
                                            reads=[("ps", kc % 4)], writes=[("ytok", bi)])
                                    else:
                                        P.op("dve", lambda e, kc=kc, bi=bi, pst=pst: e.tensor_copy(
                                            out=ytok[bi][:, (kc % 8) * 128:(kc % 8 + 1) * 128], in_=pst[:, 0:128]),
                                            reads=[("ps", kc % 4)], writes=[("ytok", bi)])
                                P.dma("act", sy_out[t0 + tt * 128:t0 + (tt + 1) * 128, hx * 1024:(hx + 1) * 1024], ytok[bi][:],
                                      reads=[("ytok", bi)], writes=["y"])
                        P.barrier()


            phase_D()
        for si_, (T_, G_) in enumerate(slots):
            emit_slot(si_, T_, G_)
        P.barrier()
        counts = P.finalize(top)
    return nc, counts


CC_MAX_OUT_BYTES = 4 * 1024 * 1024


def piece_rows(R, C, G, elem=2):
    rp = R
    while G * rp * C * elem > CC_MAX_OUT_BYTES or R % rp:
        rp -= 1
    return rp


def gathered_row(row, r, R, C, G):
    rp = piece_rows(R, C, G)
    return (row // rp) * G * rp + r * rp + (row % rp)


COMMON_WEIGHTS = ("ffn1_norm", "ffn1_w13", "ffn1_w2", "mix_norm", "w_in", "q_norm", "k_norm",
                  "filt_w1", "filt_b1", "filt_w2", "filt_b2", "filt_freq",
                  "group_out_norm", "w_out", "ffn2_norm", "ffn2_w13", "ffn2_w2", "final_norm")
SLOT_TABLES = ("ropeC", "ropeS", "kbias", "featsT", "t01tab", "masktab", "F1K", "twr", "twi", "twcr", "twci",
               "F2", "R3", "G4")


def slot_table_shapes(T, G):
    A = T // 128
    K1 = 2 * A
    Tl = T // G
    return {"ropeC": (128, Tl), "ropeS": (128, Tl), "kbias": (128, T // 128), "featsT": (33, 2 * T),
            "t01tab": (K1, 128), "masktab": (K1, 128), "F1K": (K1, 2 * K1), "twr": (128, K1), "twi": (128, K1),
            "twcr": (K1, 128), "twci": (K1, 128), "F2": (128, 384), "R3": (128, 512), "G4": (K1, 2 * A)}


_TABLE_CACHE = {}


def slot_inputs(si, T, G, rank, x_local, w):
    Tl = T // G
    CH = DH // G
    NBl = Tl // TB
    sfx = "_s%d" % si
    if T not in _TABLE_CACHE:
        _TABLE_CACHE[T] = make_tables(T, T)
    tb = _TABLE_CACHE[T]
    m = {"x" + sfx: np.ascontiguousarray(x_local, dtype=np.float32)}
    for k in SLOT_TABLES:
        a = tb[k]
        if k in ("ropeC", "ropeS"):
            a = np.ascontiguousarray(a[:, rank * Tl:(rank + 1) * Tl])
        m[k + sfx] = a
    c0 = rank * CH
    cw = np.asarray(w["conv_w"], np.float32).reshape(3, 3, 1024)[:, :, c0:c0 + CH]
    m["cwl" + sfx] = np.ascontiguousarray(cw.reshape(3, 3 * CH))
    cb = np.asarray(w["conv_b"], np.float32).reshape(3, 1024)[:, c0:c0 + CH]
    m["cbl" + sfx] = np.ascontiguousarray(cb.reshape(1, 3 * CH))
    m["hbl" + sfx] = np.ascontiguousarray(np.asarray(w["hyena_bias"], np.float32).reshape(2, 1024)[:, c0:c0 + CH])
    m["decl" + sfx] = np.ascontiguousarray(
        np.asarray(w["hyena_decay"], np.float32).reshape(4, 1024)[:, c0:c0 + CH])
    w3 = np.asarray(w["filt_w3"], np.float32).reshape(64, 4, 1024)[:, :, c0:c0 + CH]
    m["w3l" + sfx] = np.ascontiguousarray(w3.reshape(64, 4 * CH))
    nsg = 3 * CH // 128
    gps = CH // 128
    p = np.arange(128)
    cidx = np.zeros((128, nsg * G), np.int32)
    for sg in range(nsg):
        s_, grp = sg // gps, sg % gps
        for r in range(G):
            rows = s_ * 1024 + c0 + grp * 128 + p
            cidx[:, sg * G + r] = gathered_row(rows, r, 3072, Tl, G)
    m["cidx" + sfx] = cidx
    hoidx = np.zeros((128, 8 * NBl), np.int32)
    for k8 in range(8):
        for blk in range(NBl):
            ch = k8 * 128 + p
            grow = gathered_row(ch % CH, ch // CH, CH, T, G)
            hoidx[:, k8 * NBl + blk] = grow * (T // TB) + rank * NBl + blk
    m["hoidx" + sfx] = hoidx
    return m


def common_inputs(w):
    m = {}
    for k in COMMON_WEIGHTS:
        shp = WEIGHT_SHAPES[k]
        m[k] = np.ascontiguousarray(np.asarray(w[k], dtype=np.float32).reshape(shp if len(shp) > 1 else (1, shp[0])))
    m["ident"] = np.eye(128, dtype=np.float32)
    m["ones"] = np.ones((128, 128), np.float32)
    m["Pm"] = make_tables(512, 512)["Pm"]
    return m


SLOTS = [(8192, 2), (4096, 4)]
_PROG_CACHE = {}


def kernel(**inputs):
    x_prompt = np.asarray(inputs["x_prompt"], dtype=np.float32)
    x_sample = np.asarray(inputs["x_sample"], dtype=np.float32)
    (TL, GL), (TS, GS) = SLOTS
    assert x_sample.shape == (4, TL, D) and x_prompt.shape == (2, TS, D)
    if "prog" not in _PROG_CACHE:
        _PROG_CACHE["prog"] = build_program(SLOTS)
    nc, _ = _PROG_CACHE["prog"]
    cm = common_inputs(inputs)
    in_maps = []
    for core in range(8):
        m = dict(cm)
        sq, rk = core // GL, core % GL
        tl = TL // GL
        m.update(slot_inputs(0, TL, GL, rk, x_sample[sq, rk * tl:(rk + 1) * tl], inputs))
        sq, rk = core // GS, core % GS
        tl = TS // GS
        m.update(slot_inputs(1, TS, GS, rk, x_prompt[sq, rk * tl:(rk + 1) * tl], inputs))
        in_maps.append(m)
    res = run_bass_kernel_spmd(nc, in_maps, core_ids=list(range(8)))
    y_sample = np.zeros((4, TL, D), np.float32)
    y_prompt = np.zeros((2, TS, D), np.float32)
    for core in range(8):
        sq, rk = core // GL, core % GL
        tl = TL // GL
        y_sample[sq, rk * tl:(rk + 1) * tl] = np.asarray(res.results[core]["y_s0"], dtype=np.float32)
        sq, rk = core // GS, core % GS
        tl = TS // GS
        y_prompt[sq, rk * tl:(rk + 1) * tl] = np.asarray(res.results[core]["y_s1"], dtype=np.float32)
    return (y_prompt, y_sample)
```
